# Optimizing a Trainium2 kernel written in Bass

```python
import jax, jax.numpy as jnp
from jax import lax
import numpy as np

D_MODEL = 2048
BATCH = 2
SEQ = 8192
DEPTH = 4

CHUNK = 64
N_BRANCH = 4
W_BRANCH = 1024
NORM_EPS = 1e-6
CONV_WIDTH = 3
RWKV_HEAD = 64
RWKV_HEADS = W_BRANCH // RWKV_HEAD
DECAY_LORA = 64
AAA_LORA = 64
RWKV_LN_EPS = 64e-5
FOX_HEAD = 64
FOX_HEADS = W_BRANCH // FOX_HEAD
Q_BLOCK = 128
POOL_WINDOWS = (2, 4, 8, 16)
POOL_GROUPS = len(POOL_WINDOWS)
POOL_GROUP_W = W_BRANCH // POOL_GROUPS

SHIFT_W = 3 * W_BRANCH + DECAY_LORA + AAA_LORA
IN_SIZES = (
    4 * W_BRANCH,
    SHIFT_W, W_BRANCH,
    3 * W_BRANCH, FOX_HEADS, W_BRANCH,
    W_BRANCH, W_BRANCH,
    N_BRANCH * D_MODEL,
)
N_IN = sum(IN_SIZES)

kernel_name = 'hybrid_gated_stream_encoder'


def _rms_norm(x, g):
    xf = x.astype(jnp.float32)
    y = xf * lax.rsqrt(jnp.mean(xf * xf, axis=-1, keepdims=True) + NORM_EPS)
    return (y * g.astype(jnp.float32)).astype(x.dtype)


def _split(u, sizes):
    idx = [int(i) for i in np.cumsum(sizes)[:-1]]
    return jnp.split(u, idx, axis=-1)


def _short_conv_branch(u, conv_w):
    b_gate, c_gate, xv, g = _split(u, (W_BRANCH,) * 4)
    z = lax.conv_general_dilated(
        c_gate * xv, conv_w[:, None, :].astype(u.dtype), window_strides=(1,),
        padding=[(CONV_WIDTH - 1, 0)], dimension_numbers=('NWC', 'WIO', 'NWC'),
        feature_group_count=W_BRANCH)
    return b_gate * z * jax.nn.silu(g)


def _rwkv7_scan(r, decay, k, v, a, b):
    bsz, _, nh, n = r.shape
    xs = tuple(jnp.moveaxis(t, 1, 0) for t in (r, decay, k, v, a, b))

    def step(state, inp):
        r_t, w_t, k_t, v_t, a_t, b_t = inp
        sa = jnp.einsum('bhvk,bhk->bhv', state, a_t)
        state = (state * w_t[:, :, None, :] + sa[..., None] * b_t[:, :, None, :]
                 + v_t[..., None] * k_t[:, :, None, :])
        return state, jnp.einsum('bhvk,bhk->bhv', state, r_t)

    s0 = jnp.zeros((bsz, nh, n, n), jnp.float32)
    _, y = lax.scan(step, s0, xs)
    return jnp.moveaxis(y, 0, 1)


def _rwkv7_branch(u, g, mu, w0, w2, a0, a2, k_k, k_a, r_k, ln_g, ln_b):
    bsz, s, _ = u.shape
    uf = u.astype(jnp.float32)
    prev = jnp.pad(uf, ((0, 0), (1, 0), (0, 0)))[:, :-1]
    xm = uf + (prev - uf) * mu.astype(jnp.float32)
    r, k, v, wl, al = _split(xm, (W_BRANCH, W_BRANCH, W_BRANCH, DECAY_LORA, AAA_LORA))
    log_w = -jax.nn.softplus(-(w0 + jnp.einsum('bsr,rc->bsc', jnp.tanh(wl), w2))) - 0.5
    decay = jnp.exp(-jnp.exp(log_w))
    a = jax.nn.sigmoid(a0 + jnp.einsum('bsr,rc->bsc', al, a2))
    heads = lambda t: t.reshape(bsz, s, RWKV_HEADS, RWKV_HEAD)
    kk = heads(k * k_k)
    kk = kk / jnp.maximum(jnp.linalg.norm(kk, axis=-1, keepdims=True), 1e-12)
    k = k * (1.0 + (a - 1.0) * k_a)
    r, decay, k, v, a = map(heads, (r, decay, k, v, a))
    y = _rwkv7_scan(r, decay, k, v, -kk, kk * a)
    mean = jnp.mean(y, axis=-1, keepdims=True)
    var = jnp.mean(jnp.square(y - mean), axis=-1, keepdims=True)
    y = ((y - mean) * lax.rsqrt(var + RWKV_LN_EPS)).reshape(bsz, s, W_BRANCH) * ln_g + ln_b
    bonus = jnp.sum(r * k * r_k, axis=-1, keepdims=True) * v
    y = y + bonus.reshape(bsz, s, W_BRANCH)
    return y.astype(u.dtype) * jax.nn.silu(g)


def _fox_branch(qkv, f_logit, g, b_f):
    bsz, s, _ = qkv.shape
    q, k, v = [t.reshape(bsz, s, FOX_HEADS, FOX_HEAD).transpose(0, 2, 1, 3)
               for t in _split(qkv, (W_BRANCH,) * 3)]
    log_f = jax.nn.log_sigmoid(f_logit.astype(jnp.float32) + b_f.astype(jnp.float32))
    c = jnp.cumsum(log_f, axis=1).transpose(0, 2, 1)
    scale = FOX_HEAD ** -0.5
    outs = []
    for i in range(s // Q_BLOCK):
        q0, q1 = i * Q_BLOCK, (i + 1) * Q_BLOCK
        logits = jnp.einsum('bhqd,bhkd->bhqk', q[:, :, q0:q1], k[:, :, :q1]).astype(jnp.float32) * scale
        logits = logits + c[:, :, q0:q1, None] - c[:, :, None, :q1]
        causal = (q0 + jnp.arange(Q_BLOCK))[:, None] >= jnp.arange(q1)[None, :]
        p = jax.nn.softmax(jnp.where(causal, logits, -jnp.inf), axis=-1)
        outs.append(jnp.einsum('bhqk,bhkd->bhqd', p.astype(v.dtype), v[:, :, :q1]))
    o = jnp.concatenate(outs, axis=2).transpose(0, 2, 1, 3).reshape(bsz, s, W_BRANCH)
    return o * jax.nn.silu(g)


def _pool_branch(u, g, pool_w, pool_scale):
    bsz, s, _ = u.shape
    xg = u.astype(jnp.float32).reshape(bsz, s, POOL_GROUPS, POOL_GROUP_W)
    pos = jnp.arange(s)
    pooled = []
    for gi, win in enumerate(POOL_WINDOWS):
        xi = xg[:, :, gi]
        cs = jnp.cumsum(xi, axis=1)
        cs_prev = jnp.pad(cs, ((0, 0), (win, 0), (0, 0)))[:, :s]
        count = jnp.minimum(pos + 1, win).astype(jnp.float32)[None, :, None]
        pooled.append((cs - cs_prev) / count - xi)
    p = jnp.stack(pooled, axis=2)
    y = jnp.einsum('bsgc,gce->bsge', p, pool_w.astype(jnp.float32)).reshape(bsz, s, W_BRANCH) * pool_scale
    return y.astype(u.dtype) * jax.nn.silu(g)


def _hybrid_layer(x, norm_g, w_in, b_merge, conv_w, rwkv_mu, rwkv_w0, rwkv_w2, rwkv_a0, rwkv_a2,
                  rwkv_kk, rwkv_ka, rwkv_rk, rwkv_ln_g, rwkv_ln_b, fox_bf, pool_w, pool_scale,
                  w_branch, w_out):
    bsz, s, d = x.shape
    h = _rms_norm(x, norm_g)
    u = jnp.einsum('bsd,dn->bsn', h, w_in)
    a_in, b_in, b_gate, c_qkv, c_f, c_gate, d_in, d_gate, m_logit = _split(u, IN_SIZES)
    y_a = _short_conv_branch(a_in, conv_w)
    y_b = _rwkv7_branch(b_in, b_gate, rwkv_mu, rwkv_w0, rwkv_w2, rwkv_a0, rwkv_a2,
                        rwkv_kk, rwkv_ka, rwkv_rk, rwkv_ln_g, rwkv_ln_b)
    y_c = _fox_branch(c_qkv, c_f, c_gate, fox_bf)
    y_d = _pool_branch(d_in, d_gate, pool_w, pool_scale)
    ys = jnp.stack([y_a, y_b, y_c, y_d], axis=2)
    proj = jnp.einsum('bskc,kcd->bskd', ys, w_branch)
    gates = jax.nn.sigmoid(m_logit.reshape(bsz, s, N_BRANCH, d) + b_merge)
    merged = jnp.sum(gates * proj, axis=2)
    return x + jnp.einsum('bsd,de->bse', merged, w_out)


def setup_inputs(seed: int = 0) -> dict:
    key = jax.random.key(seed)
    ks = jax.random.split(key, 21)
    f32 = jnp.float32
    nrm = lambda k, shape, sc: sc * jax.random.normal(k, shape, f32)
    L, D, W = DEPTH, D_MODEL, W_BRANCH
    return {
        'x': nrm(ks[0], (BATCH, SEQ, D), 1.0),
        'norm_g': 1.0 + nrm(ks[1], (L, D), 0.02),
        'w_in': nrm(ks[2], (L, D, N_IN), D ** -0.5),
        'b_merge': nrm(ks[3], (L, N_BRANCH, D), 0.02),
        'conv_w': nrm(ks[4], (L, CONV_WIDTH, W), CONV_WIDTH ** -0.5),
        'rwkv_mu': jax.random.uniform(ks[5], (L, SHIFT_W), f32),
        'rwkv_w0': jax.random.uniform(ks[6], (L, W), f32, -3.0, 0.5),
        'rwkv_w2': nrm(ks[7], (L, DECAY_LORA, W), 0.1),
        'rwkv_a0': nrm(ks[8], (L, W), 0.1),
        'rwkv_a2': nrm(ks[9], (L, AAA_LORA, W), 0.1),
        'rwkv_kk': 0.85 + nrm(ks[10], (L, W), 0.05),
        'rwkv_ka': 1.0 + nrm(ks[11], (L, W), 0.05),
        'rwkv_rk': nrm(ks[12], (L, RWKV_HEADS, RWKV_HEAD), 0.1),
        'rwkv_ln_g': 1.0 + nrm(ks[13], (L, W), 0.02),
        'rwkv_ln_b': nrm(ks[14], (L, W), 0.02),
        'fox_bf': jax.random.uniform(ks[15], (L, FOX_HEADS), f32, 1.0, 5.0),
        'pool_w': nrm(ks[16], (L, POOL_GROUPS, POOL_GROUP_W, POOL_GROUP_W), POOL_GROUP_W ** -0.5),
        'pool_scale': 1.0 + nrm(ks[17], (L, W), 0.1),
        'w_branch': nrm(ks[18], (L, N_BRANCH, W, D), W ** -0.5),
        'w_out': nrm(ks[19], (L, D, D), D ** -0.5),
        'final_g': 1.0 + nrm(ks[20], (D,), 0.02),
    }


def reference(x, norm_g, w_in, b_merge, conv_w, rwkv_mu, rwkv_w0, rwkv_w2, rwkv_a0, rwkv_a2,
              rwkv_kk, rwkv_ka, rwkv_rk, rwkv_ln_g, rwkv_ln_b, fox_bf, pool_w, pool_scale,
              w_branch, w_out, final_g):
    for l in range(DEPTH):
        x = _hybrid_layer(x, norm_g[l], w_in[l], b_merge[l], conv_w[l], rwkv_mu[l], rwkv_w0[l],
                          rwkv_w2[l], rwkv_a0[l], rwkv_a2[l], rwkv_kk[l], rwkv_ka[l], rwkv_rk[l],
                          rwkv_ln_g[l], rwkv_ln_b[l], fox_bf[l], pool_w[l], pool_scale[l],
                          w_branch[l], w_out[l])
    return _rms_norm(x, final_g)
```

```python
import numpy as np
from contextlib import ExitStack
import concourse.bass as bass
import concourse.mybir as mybir
from concourse.bass_utils import run_bass_kernel_spmd

F32 = mybir.dt.float32
BF16 = mybir.dt.bfloat16
AF = mybir.ActivationFunctionType
ALU = mybir.AluOpType
AX = mybir.AxisListType

D = 2048
S = 8192
NB = 2
DEPTH = 4
W = 1024
NIN = 22672
KC = D // 128
HG = 4
CW = W // HG
EPS = 1e-6

_names = [("A_b", CW), ("A_c", CW), ("A_x", CW), ("A_g", CW),
          ("B_r", CW), ("B_k", CW), ("B_v", CW), ("B_lora", 128), ("B_g", CW),
          ("C_q", CW), ("C_k", CW), ("C_v", CW), ("C_g", CW),
          ("D_x", CW), ("D_g", CW), ("C_f", 4)]
UROW = {}
_o = 0
for _n, _s in _names:
    UROW[_n] = (_o, _s)
    _o += _s
NU = _o


def core_cols(hg):
    c = lambda base, n=CW: list(range(base + hg * n, base + (hg + 1) * n))
    oA = 0
    oB = 4 * W
    oBg = oB + 3 * W + 128
    oC = oBg + W
    oCf = oC + 3 * W
    oCg = oCf + 16
    oD = oCg + W
    oDg = oD + W
    cols = []
    cols += c(oA) + c(oA + W) + c(oA + 2 * W) + c(oA + 3 * W)
    cols += c(oB) + c(oB + W) + c(oB + 2 * W) + list(range(oB + 3 * W, oB + 3 * W + 128)) + c(oBg)
    cols += c(oC) + c(oC + W) + c(oC + 2 * W) + c(oCg)
    cols += c(oD) + c(oDg)
    cols += list(range(oCf + hg * 4, oCf + hg * 4 + 4))
    assert len(cols) == NU
    return np.array(cols)


OM = 4 * W + (3 * W + 128) + W + 3 * W + 16 + W + W + W
assert OM + 4 * D == NIN


class Buf:
    __slots__ = ("name", "w", "r", "psum", "ep")

    def __init__(self, name="", psum=False):
        self.name = name
        self.w = None
        self.r = {}
        self.psum = psum
        self.ep = 0


class Prog:
    NDMA = 32
    NSW = 8

    def __init__(self, nc, stack):
        self.nc = nc
        self.stack = stack
        self.eng = {"pe": nc.tensor, "act": nc.scalar, "dve": nc.vector,
                    "pool": nc.gpsimd, "sp": nc.sync}
        self.ep = 0
        self.nins = 0
        self.ccsem = None
        self.cccnt = 0
        self._fresh()

    def _fresh(self):
        nc, stack = self.nc, self.stack
        self.sem = {}
        self.cnt = {}
        for e in self.eng:
            self.sem[e] = stack.enter_context(nc.semaphore("s%d_%s" % (self.ep, e)))
            self.cnt[e] = 0
        self.dsem = [stack.enter_context(nc.semaphore("d%d_%d" % (self.ep, i))) for i in range(self.NDMA)]
        self.dcnt = [0] * self.NDMA
        self.dnext = 0
        self.swnext = 0
        self.seen = {e: {} for e in self.eng}

    def new_epoch(self):
        self.barrier()
        self.ep += 1
        self._fresh()

    def _chk(self, b):
        if b.ep != self.ep:
            b.ep = self.ep
            b.w = None
            b.r = {}

    def collective(self, kind, in_t, out_t, groups):
        self.collectives(kind, [(in_t, out_t)], groups)

    def collective_async(self, kind, in_t, out_t, groups, deps=()):
        if self.ccsem is None:
            self.ccsem = self.stack.enter_context(self.nc.semaphore("ccsem"))
        self._deps("pool", list(deps), [])
        ins = self.nc.gpsimd.collective_compute(kind, ALU.bypass, replica_groups=groups,
                                                ins=[in_t.ap().opt()], outs=[out_t.ap().opt()])
        ins.then_inc(self.ccsem)
        self.cccnt += 1
        self.nins += 1

    def collective_wait(self):
        if self.ccsem is not None:
            for e in self.eng.values():
                e.wait_ge(self.ccsem, self.cccnt)

    def collectives(self, kind, pairs, groups):
        self.barrier()
        if self.ccsem is None:
            self.ccsem = self.stack.enter_context(self.nc.semaphore("ccsem"))
        for in_t, out_t in pairs:
            ins = self.nc.gpsimd.collective_compute(kind, ALU.bypass, replica_groups=groups,
                                                    ins=[in_t.ap().opt()], outs=[out_t.ap().opt()])
            ins.then_inc(self.ccsem)
            self.cccnt += 1
            self.nins += 1
        for e in self.eng.values():
            e.wait_ge(self.ccsem, self.cccnt)

    def _semobj(self, key):
        return self.sem[key] if isinstance(key, str) else self.dsem[key]

    def _wait(self, e, key, val):
        if key == e and val > self.cnt[e]:
            return
        if self.seen[e].get(key, 0) >= val:
            return
        self.seen[e][key] = val
        self.eng[e].wait_ge(self._semobj(key), val)

    def _deps(self, e, reads, writes):
        for b in reads:
            self._chk(b)
        for b in writes:
            self._chk(b)
        for b in reads:
            if b.w is not None:
                self._wait(e, *b.w)
            if b.psum:
                for k, v in b.r.items():
                    if k != e:
                        self._wait(e, k, v)
        for b in writes:
            if b.w is not None:
                self._wait(e, *b.w)
            for k, v in b.r.items():
                self._wait(e, k, v)

    def _mark(self, key, val, reads, writes):
        for b in reads:
            if b.r.get(key, 0) < val:
                b.r[key] = val
        for b in writes:
            b.w = (key, val)
            b.r = {}

    def op(self, e, fn, reads=(), writes=(), sig=True):
        self._deps(e, reads, writes)
        ins = fn(self.eng[e])
        if sig:
            self.cnt[e] += 1
            ins.then_inc(self.sem[e], 1)
            self._mark(e, self.cnt[e], reads, writes)
        else:
            self._mark(e, self.cnt[e] + 1, reads, writes)
        self.nins += 1

    def dma(self, q, out, in_, reads=(), writes=(), **kw):
        if q == "pool":
            i = self.NDMA - self.NSW + self.swnext
            self.swnext = (self.swnext + 1) % self.NSW
        else:
            i = self.dnext
            self.dnext = (self.dnext + 1) % (self.NDMA - self.NSW)
        if self.dcnt[i] > 0:
            self._wait(q, i, self.dcnt[i])
        self._deps(q, reads, writes)
        ins = self.eng[q].dma_start(out=out, in_=in_, **kw)
        self.dcnt[i] += 16
        ins.then_inc(self.dsem[i], 16)
        self._mark(i, self.dcnt[i], reads, writes)
        self.nins += 1

    def barrier(self):
        for e in self.eng:
            for e2 in self.eng:
                if e2 != e and self.cnt[e2] > 0:
                    self._wait(e, e2, self.cnt[e2])
            for i in range(self.NDMA):
                if self.dcnt[i] > 0:
                    self._wait(e, i, self.dcnt[i])


_UID = [0]


def _uid():
    _UID[0] += 1
    return _UID[0]


class Ring:
    def __init__(self, nc, st, name, shape, dtype, n, psum=False):
        alloc = nc.psum_tensor if psum else nc.sbuf_tensor
        u = _uid()
        self.t = [st.enter_context(alloc("%s_%d_%d" % (name, u, i), shape, dtype)) for i in range(n)]
        self.b = [Buf("%s%d" % (name, i), psum) for i in range(n)]
        self.i = 0

    def next(self):
        i = self.i
        self.i = (self.i + 1) % len(self.t)
        return self.t[i], self.b[i]


def sb(nc, st, name, shape, dtype):
    return st.enter_context(nc.sbuf_tensor("%s_%d" % (name, _uid()), shape, dtype)), Buf(name)


def make_identity(P, nc, st, name="ident"):
    idf, bidf = sb(nc, st, name + "f", [128, 128], F32)
    idb, bidb = sb(nc, st, name + "b", [128, 128], BF16)
    P.op("pool", lambda e: e.memset(idf[:], 1.0), writes=[bidf])
    P.op("pool", lambda e: e.affine_select(out=idf[:], in_=idf[:], pattern=[[-1, 128]],
                                           compare_op=ALU.is_equal, fill=0.0, base=0,
                                           channel_multiplier=1), reads=[bidf], writes=[bidf])
    P.op("dve", lambda e: e.tensor_copy(out=idb[:], in_=idf[:]), reads=[bidf], writes=[bidb])
    return (idf, bidf), (idb, bidb)


def stage_norm_T(P, nc, x_ap, g_ap, HT, ntok, identb, gathered=False):
    idb, bidb = identb
    with ExitStack() as st:
        gs, bgs = sb(nc, st, "n_g", [128, KC], F32)
        P.dma("sp", gs[:], g_ap, writes=[bgs])
        xr = Ring(nc, st, "n_x", [128, D], F32, 3)
        hr = Ring(nc, st, "n_h", [128, D], BF16, 3)
        jr = Ring(nc, st, "n_j", [128, D], BF16, 2)
        sr = Ring(nc, st, "n_s", [128, 4], F32, 4)
        pr = Ring(nc, st, "n_ps", [128, KC, 128], BF16, 3, psum=True)
        hTr = Ring(nc, st, "n_hT", [128, KC, 512], BF16, 2)
        gbc = gs[:, :].unsqueeze(2).to_broadcast([128, KC, 128])
        hT_of = {}

        def tile_gen(tt):
            t4, q = divmod(tt, 4)
            if q == 0:
                hT_of[t4] = hTr.next()
            hT, bhT = hT_of[t4]
            xt, bx = xr.next()
            if gathered:
                P.dma("sp" if q % 2 == 0 else "act", xt[:, :].rearrange("p (r e) -> p r e", r=4),
                      x_ap[tt * 128:(tt + 1) * 128, :, :], writes=[bx])
            else:
                P.dma("sp" if q % 2 == 0 else "act", xt[:], x_ap[tt * 128:(tt + 1) * 128, :], writes=[bx])
            s, bs = sr.next()
            j, bj = jr.next()
            P.op("act", lambda e: e.activation(out=j[:], in_=xt[:], func=AF.Square,
                                               accum_out=s[:, 0:1]), reads=[bx], writes=[bj, bs])
            yield
            P.op("dve", lambda e: e.tensor_scalar(out=s[:, 1:2], in0=s[:, 0:1], scalar1=1.0 / D, scalar2=EPS,
                                                  op0=ALU.mult, op1=ALU.add), reads=[bs], writes=[bs])
            P.op("act", lambda e: e.activation(out=s[:, 1:2], in_=s[:, 1:2], func=AF.Sqrt),
                 reads=[bs], writes=[bs])
            P.op("dve", lambda e: e.reciprocal(out=s[:, 2:3], in_=s[:, 1:2]), reads=[bs], writes=[bs])
            yield
            h, bh = hr.next()
            P.op("dve", lambda e: e.tensor_scalar(out=h[:], in0=xt[:], scalar1=s[:, 2:3], scalar2=None,
                                                  op0=ALU.mult), reads=[bx, bs], writes=[bh])
            yield
            ps, bps = pr.next()
            for kc in range(KC):
                P.op("pe", lambda e: e.transpose(out=ps[:, kc, :], in_=h[:, kc * 128:(kc + 1) * 128],
                                                 identity=idb[:]), reads=[bh, bidb], writes=[bps], sig=(kc == KC - 1))
            yield
            P.op("dve", lambda e: e.tensor_tensor(out=hT[:, :, q * 128:(q + 1) * 128], in0=ps[:], in1=gbc,
                                                  op=ALU.mult), reads=[bps, bgs], writes=[bhT])
            if q == 3:
                P.dma("sp", HT[:, :, t4 * 512:(t4 + 1) * 512], hT[:], reads=[bhT])

        ntile = ntok // 128
        active, nxt, WIN = [], 0, 3
        while nxt < ntile or active:
            while len(active) < WIN and nxt < ntile:
                active.append(tile_gen(nxt))
                nxt += 1
            for g_ in list(active):
                try:
                    next(g_)
                except StopIteration:
                    active.remove(g_)
        P.barrier()


def stage_proj(P, nc, wc_ap, ncols, HT, ntok, U, group=8):
    nchunk = (ncols + 127) // 128
    with ExitStack() as st:
        wr = Ring(nc, st, "p_w", [128, KC, group * 128], BF16, 2)
        hr = Ring(nc, st, "p_h", [128, KC, 512], BF16, 2)
        pr = Ring(nc, st, "p_ps", [128, 512], F32, 4, psum=True)
        er = Ring(nc, st, "p_e", [128, 512], F32, 4)
        wv = wc_ap.rearrange("(kc p) c -> p kc c", p=128)
        ev = 0
        for g0 in range(0, nchunk, group):
            c0 = g0 * 128
            c1 = min(ncols, (g0 + group) * 128)
            wt, bw = wr.next()
            for kc in range(0, KC, 4):
                P.dma("pool", wt[:, kc:kc + 4, 0:c1 - c0], wv[:, kc:kc + 4, c0:c1], writes=[bw])
            for tt in range(ntok // 512):
                ht, bh = hr.next()
                P.dma("sp", ht[:], HT[:, :, tt * 512:(tt + 1) * 512], writes=[bh])
                for ch in range(g0, min(nchunk, g0 + group)):
                    m = min(128, ncols - ch * 128)
                    lc = (ch - g0) * 128
                    ps, bps = pr.next()
                    for kc in range(KC):
                        P.op("pe", lambda e: e.matmul(ps[0:m, :], lhsT=wt[:, kc, lc:lc + m], rhs=ht[:, kc, :],
                                                      start=(kc == 0), stop=(kc == KC - 1)),
                             reads=[bw, bh], writes=[bps], sig=(kc == KC - 1))
                    et, be = er.next()
                    if ev % 2 == 0:
                        P.op("act", lambda e: e.copy(out=et[0:m, :], in_=ps[0:m, :]), reads=[bps], writes=[be])
                    else:
                        P.op("dve", lambda e: e.tensor_copy(out=et[0:m, :], in_=ps[0:m, :]), reads=[bps], writes=[be])
                    ev += 1
                    P.dma("sp" if ev % 2 == 0 else "act", U[ch * 128:ch * 128 + m, tt * 512:(tt + 1) * 512], et[0:m, :],
                          reads=[be])
        P.barrier()


NCONST = 128 + 512 + 128 + 512 + 128
def host_consts():
    c = np.zeros((128, NCONST), np.float32)
    blk = (np.arange(128)[:, None] // 64) == (np.arange(128)[None, :] // 64)
    c[:, 0:128] = blk
    s = np.arange(128)[:, None]
    t = np.arange(128)[None, :]
    su = (blk & (s < t)).astype(np.float32)
    u = (blk & (s <= t)).astype(np.float32)
    c[:, 128:640] = np.concatenate([su, u, su, u], axis=1)
    c[:, 640:768] = (blk & (s > t)).astype(np.float32)
    c[:, 768:1280] = (np.arange(512)[None, :] % 64 != 0)
    c[:, 1280:1408] = (s <= t)
    return c


PRM = {"mu_r": 0, "mu_k": 1, "mu_v": 2, "w0": 3, "a0": 4, "k_k": 5, "k_a": 6, "ln_g": 7, "ln_b": 8,
       "r_k": 9, "mu_lora": 10, "cw0": 11, "cw1": 12, "cw2": 13, "pscale": 14, "omka": 15}
NPRM = 16


def host_prm(inp, l, hg):
    sl = slice(hg * CW, (hg + 1) * CW)
    p = np.zeros((CW, NPRM), np.float32)
    mu = inp["rwkv_mu"][l]
    p[:, 0] = mu[0:W][sl]
    p[:, 1] = mu[W:2 * W][sl]
    p[:, 2] = mu[2 * W:3 * W][sl]
    p[:, 3] = inp["rwkv_w0"][l][sl]
    p[:, 4] = inp["rwkv_a0"][l][sl]
    p[:, 5] = inp["rwkv_kk"][l][sl]
    p[:, 6] = inp["rwkv_ka"][l][sl]
    p[:, 7] = inp["rwkv_ln_g"][l][sl]
    p[:, 8] = inp["rwkv_ln_b"][l][sl]
    p[:, 9] = inp["rwkv_rk"][l].reshape(-1)[sl]
    p[0:128, 10] = mu[3 * W:3 * W + 128]
    p[:, 11] = inp["conv_w"][l][0][sl]
    p[:, 12] = inp["conv_w"][l][1][sl]
    p[:, 13] = inp["conv_w"][l][2][sl]
    p[:, 14] = inp["pool_scale"][l][sl]
    return np.ascontiguousarray(p.reshape(2, 128, NPRM).transpose(1, 0, 2))


def host_lora(inp, l, hg):
    sl = slice(hg * CW, (hg + 1) * CW)
    return np.ascontiguousarray(np.concatenate([inp["rwkv_w2"][l][:, sl], inp["rwkv_a2"][l][:, sl]], axis=0))


DBG = 0


def stage_rwkv(P, nc, U, prm_ap, lora_ap, cst, YS, ntok, identf, on_tile=None):
    idf, bidf = identf
    cs, bcs = cst
    r0 = {k: UROW[k][0] for k in UROW}
    NT = ntok // 512
    MUL, ADD, SUB = ALU.mult, ALU.add, ALU.subtract
    with ExitStack() as st:
        prm, bprm = sb(nc, st, "r_prm", [128, 2, NPRM], F32)
        P.dma("sp", prm[:], prm_ap, writes=[bprm])
        lw2, blw2 = sb(nc, st, "r_lora", [128, CW], F32)
        P.dma("sp", lw2[:], lora_ap, writes=[blw2])
        for c in range(2):
            P.op("dve", lambda e: e.tensor_scalar(out=prm[:, c, 15:16], in0=prm[:, c, 6:7], scalar1=-1.0, scalar2=1.0,
                                                  op0=MUL, op1=ADD), reads=[bprm], writes=[bprm])
        pc = lambda c, n: prm[:, c, PRM[n]:PRM[n] + 1]
        blk = cs[:, 0:128]
        mask4 = cs[:, 128:640]
        masksl = cs[:, 640:768]
        scanm = cs[:, 768:1280]
        psr = Ring(nc, st, "r_ps", [128, 512], F32, 4, psum=True)
        pqr = [Ring(nc, st, "r_pq%d" % i, [128, 512], F32, 1, psum=True) for i in range(2)]
        pyr = [Ring(nc, st, "r_py%d" % i, [128, 512], F32, 1, psum=True) for i in range(2)]
        ur = Ring(nc, st, "r_u", [128, 513], F32, 4)
        tr = Ring(nc, st, "r_t", [128, 512], F32, 6)
        names = ["xr", "xk", "xv", "lw", "al", "kk", "L", "G", "Gi", "aT", "rT", "bT", "kT", "bh", "kh", "bon", "y"]
        opr = {n: Ring(nc, st, "r_o_" + n, [128, 512], F32, 2) for n in names}
        lor = Ring(nc, st, "r_lo", [128, 512], F32, 2)
        MTr_ = [Ring(nc, st, "r_MT%d" % i, [128, 512], F32, 2) for i in range(2)]
        XYr_ = [Ring(nc, st, "r_XY%d" % i, [128, 256], F32, 3) for i in range(2)]
        Fr_ = [Ring(nc, st, "r_F%d" % i, [128, 128], F32, 3) for i in range(2)]
        TKr_ = [Ring(nc, st, "r_TK%d" % i, [128, 320], F32, 2) for i in range(2)]
        Er_ = [Ring(nc, st, "r_E%d" % i, [128, 128], F32, 3) for i in range(2)]
        PTr_ = [Ring(nc, st, "r_PT%d" % i, [128, 2, 64], F32, 2) for i in range(2)]
        Qr_ = [Ring(nc, st, "r_Q%d" % i, [128, 2, 64], F32, 2) for i in range(2)]
        ZTr_ = [Ring(nc, st, "r_ZT%d" % i, [128, 128], F32, 2) for i in range(2)]
        Hr = [Ring(nc, st, "r_H%d" % i, [128, 64], F32, 3) for i in range(4)]
        gr = Ring(nc, st, "r_g", [128, 512], F32, 2)
        obr = Ring(nc, st, "r_ob", [128, 512], BF16, 2)
        Hcur = [None] * 4

        def load_mix(row, c, mu_name, tt, dst, bdst):
            ut, bu = ur.next()
            t0 = tt * 512
            if tt == 0:
                P.op("pool", lambda e: e.memset(ut[:, 0:1], 0.0), writes=[bu])
                P.dma("sp", ut[:, 1:513], U[row + c * 128:row + (c + 1) * 128, 0:512], writes=[bu])
            else:
                P.dma("sp", ut[:, :], U[row + c * 128:row + (c + 1) * 128, t0 - 1:t0 + 512], writes=[bu])
            d, bd = tr.next()
            P.op("pool", lambda e: e.tensor_tensor(out=d[:], in0=ut[:, 0:512], in1=ut[:, 1:513], op=SUB),
                 reads=[bu], writes=[bd])
            P.op("dve", lambda e: e.scalar_tensor_tensor(out=dst[:], in0=d[:], scalar=pc(c, mu_name), in1=ut[:, 1:513],
                                                         op0=MUL, op1=ADD), reads=[bd, bu, bprm], writes=[bdst])

        LO = {}
        YSW = {}
        NEXT = [None]

        def item(tt, pr_):
            if pr_ == 0:
                lo, blo = lor.next()
                load_mix(r0["B_lora"], 0, "mu_lora", tt, lo, blo)
                P.op("act", lambda e: e.activation(out=lo[0:64, :], in_=lo[0:64, :], func=AF.Tanh), reads=[blo], writes=[blo])
                LO[tt] = (lo, blo)
                yield
            lo, blo = LO[tt]
            T = {}
            for n in names:
                T[n] = opr[n].next()
            xr_, bxr = T["xr"]; xk, bxk = T["xk"]; xv, bxv = T["xv"]
            load_mix(r0["B_r"], pr_, "mu_r", tt, xr_, bxr)
            yield
            load_mix(r0["B_k"], pr_, "mu_k", tt, xk, bxk)
            yield
            load_mix(r0["B_v"], pr_, "mu_v", tt, xv, bxv)
            yield
            lw, blw = T["lw"]; al, bal = T["al"]
            ps, bps = psr.next()
            P.op("pe", lambda e: e.matmul(ps[:], lhsT=lw2[0:64, pr_ * 128:(pr_ + 1) * 128], rhs=lo[0:64, :],
                                          start=True, stop=True), reads=[blw2, blo], writes=[bps])
            P.op("act", lambda e: e.activation(out=lw[:], in_=ps[:], func=AF.Sigmoid, bias=pc(pr_, "w0"), scale=1.0),
                 reads=[bps, bprm], writes=[blw])
            P.op("dve", lambda e: e.tensor_scalar(out=lw[:], in0=lw[:], scalar1=-0.6065306597126334, scalar2=None,
                                                  op0=MUL), reads=[blw], writes=[blw])
            ps, bps = psr.next()
            P.op("pe", lambda e: e.matmul(ps[:], lhsT=lw2[64:128, pr_ * 128:(pr_ + 1) * 128], rhs=lo[64:128, :],
                                          start=True, stop=True), reads=[blw2, blo], writes=[bps])
            P.op("act", lambda e: e.activation(out=al[:], in_=ps[:], func=AF.Sigmoid, bias=pc(pr_, "a0"), scale=1.0),
                 reads=[bps, bprm], writes=[bal])
            yield
            kk, bkk = T["kk"]
            P.op("pool", lambda e: e.tensor_scalar(out=kk[:], in0=xk[:], scalar1=pc(pr_, "k_k"), scalar2=None, op0=MUL),
                 reads=[bxk, bprm], writes=[bkk])
            sq, bsq = tr.next()
            P.op("pool", lambda e: e.tensor_tensor(out=sq[:], in0=kk[:], in1=kk[:], op=MUL), reads=[bkk], writes=[bsq])
            ps, bps = psr.next()
            P.op("pe", lambda e: e.matmul(ps[:], lhsT=blk, rhs=sq[:], start=True, stop=True), reads=[bcs, bsq], writes=[bps])
            rn, brn = tr.next()
            P.op("act", lambda e: e.activation(out=rn[:], in_=ps[:], func=AF.Sqrt), reads=[bps], writes=[brn])
            P.op("dve", lambda e: e.tensor_scalar(out=rn[:], in0=rn[:], scalar1=1e-12, scalar2=None, op0=ALU.max),
                 reads=[brn], writes=[brn])
            P.op("dve", lambda e: e.reciprocal(out=rn[:], in_=rn[:]), reads=[brn], writes=[brn])
            P.op("pool", lambda e: e.tensor_tensor(out=kk[:], in0=kk[:], in1=rn[:], op=MUL), reads=[bkk, brn], writes=[bkk])
            yield
            t1, bt1 = tr.next()
            P.op("dve", lambda e: e.tensor_scalar(out=t1[:], in0=al[:], scalar1=pc(pr_, "k_a"), scalar2=pc(pr_, "omka"),
                                                  op0=MUL, op1=ADD), reads=[bal, bprm], writes=[bt1])
            P.op("pool", lambda e: e.tensor_tensor(out=xk[:], in0=xk[:], in1=t1[:], op=MUL), reads=[bxk, bt1], writes=[bxk])
            yield
            L, bL = T["L"]; G, bG = T["G"]; Gi, bGi = T["Gi"]
            P.op("dve", lambda e: e.tensor_tensor_scan(out=L[:], data0=scanm, data1=lw[:], initial=0.0, op0=MUL, op1=ADD),
                 reads=[bcs, blw], writes=[bL])
            P.op("act", lambda e: e.activation(out=G[:], in_=L[:], func=AF.Exp), reads=[bL], writes=[bG])
            P.op("act", lambda e: e.activation(out=Gi[:], in_=L[:], func=AF.Exp, scale=-1.0), reads=[bL], writes=[bGi])
            gp, bgp = tr.next()
            P.op("pool", lambda e: e.tensor_tensor(out=gp[:], in0=L[:], in1=lw[:], op=SUB), reads=[bL, blw], writes=[bgp])
            P.op("act", lambda e: e.activation(out=gp[:], in_=gp[:], func=AF.Exp), reads=[bgp], writes=[bgp])
            yield
            aT, baT = T["aT"]; rT, brT = T["rT"]; bT, bbT = T["bT"]; kT, bkT = T["kT"]
            bh, bbh = T["bh"]; kh, bkh = T["kh"]
            P.op("dve", lambda e: e.scalar_tensor_tensor(out=aT[:], in0=kk[:], scalar=-1.0, in1=gp[:], op0=MUL, op1=MUL),
                 reads=[bkk, bgp], writes=[baT])
            P.op("pool", lambda e: e.tensor_tensor(out=bT[:], in0=kk[:], in1=al[:], op=MUL), reads=[bkk, bal], writes=[bbT])
            P.op("pool", lambda e: e.tensor_tensor(out=bT[:], in0=bT[:], in1=Gi[:], op=MUL), reads=[bbT, bGi], writes=[bbT])
            P.op("pool", lambda e: e.tensor_tensor(out=rT[:], in0=xr_[:], in1=G[:], op=MUL), reads=[bxr, bG], writes=[brT])
            P.op("dve", lambda e: e.tensor_tensor(out=kT[:], in0=xk[:], in1=Gi[:], op=MUL), reads=[bxk, bGi], writes=[bkT])
            yield
            gcb = G[:, :].rearrange("p (c t) -> p c t", t=64)[:, :, 63:64].to_broadcast([128, 8, 64])
            P.op("dve", lambda e: e.tensor_tensor(out=bh[:, :].rearrange("p (c t) -> p c t", t=64),
                                                  in0=bT[:, :].rearrange("p (c t) -> p c t", t=64), in1=gcb, op=MUL),
                 reads=[bbT, bG], writes=[bbh])
            P.op("pool", lambda e: e.tensor_tensor(out=kh[:, :].rearrange("p (c t) -> p c t", t=64),
                                                   in0=kT[:, :].rearrange("p (c t) -> p c t", t=64), in1=gcb, op=MUL),
                 reads=[bkT, bG], writes=[bkh])
            yield
            bon, bbon = T["bon"]
            t2, bt2 = tr.next()
            P.op("dve", lambda e: e.scalar_tensor_tensor(out=t2[:], in0=xr_[:], scalar=pc(pr_, "r_k"), in1=xk[:], op0=MUL, op1=MUL),
                 reads=[bxr, bxk, bprm], writes=[bt2])
            ps, bps = psr.next()
            P.op("pe", lambda e: e.matmul(ps[:], lhsT=blk, rhs=t2[:], start=True, stop=True), reads=[bcs, bt2], writes=[bps])
            P.op("dve", lambda e: e.tensor_tensor(out=bon[:], in0=ps[:], in1=xv[:], op=MUL), reads=[bps, bxv], writes=[bbon])

            yield "PREP_DONE"
            y, by = T["y"]
            if DBG == 1:
                P.op('pool', lambda e: e.memset(y[:], 0.0), writes=[by])
            def head_stream(hh):
                MTr, XYr, Fr, TKr, Er = MTr_[hh], XYr_[hh], Fr_[hh], TKr_[hh], Er_[hh]
                PTr, Qr, ZTr = PTr_[hh], Qr_[hh], ZTr_[hh]
                hd = pr_ * 2 + hh
                R = slice(hh * 64, hh * 64 + 64)
                if Hcur[hd] is None:
                    Hcur[hd] = Hr[hd].next()
                    P.op("pool", lambda e: e.memset(Hcur[hd][0][:], 0.0), writes=[Hcur[hd][1]])
                for cp in range(4):
                    Cc = slice(cp * 128, cp * 128 + 128)
                    MT, bMT = MTr.next()
                    ps, bps = psr.next()
                    for i, (lt, blt, rt, brt) in enumerate([(bT, bbT, aT, baT), (bT, bbT, rT, brT),
                                                            (kT, bkT, aT, baT), (kT, bkT, rT, brT)]):
                        P.op("pe", lambda e: e.matmul(ps[:, i * 128:(i + 1) * 128], lhsT=lt[R, Cc], rhs=rt[R, Cc],
                                                      start=True, stop=True), reads=[blt, brt], writes=[bps], sig=(i == 3))
                    P.op("dve", lambda e: e.tensor_tensor(out=MT[:], in0=ps[:], in1=mask4, op=MUL),
                         reads=[bps, bcs], writes=[bMT])
                    XY, bXY = XYr.next()
                    ps2, bps2 = psr.next()
                    P.op("pe", lambda e: e.matmul(ps2[:, 0:128], lhsT=aT[R, Cc], rhs=bT[R, Cc], start=True, stop=True),
                         reads=[baT, bbT], writes=[bps2])
                    P.op("dve", lambda e: e.tensor_tensor(out=XY[:, 128:256], in0=ps2[:, 0:128], in1=masksl, op=MUL),
                         reads=[bps2, bcs], writes=[bXY])
                    yield
                    if DBG == 2:
                        continue
                    TK, bTK = TKr.next()
                    ps3, bps3 = psr.next()
                    for i, (src, bsrc) in enumerate([(aT, baT), (xv, bxv), (bh, bbh), (kh, bkh)]):
                        P.op("pe", lambda e: e.transpose(out=ps3[:, 64 + i * 64:128 + i * 64], in_=src[R, Cc],
                                                         identity=idf[R, R]), reads=[bsrc, bidf], writes=[bps3], sig=(i == 3))
                    P.op("act", lambda e: e.copy(out=TK[:, 64:320], in_=ps3[:, 64:320]), reads=[bps3], writes=[bTK])
                    yield
                    if DBG == 3:
                        continue
                    ps4, bps4 = psr.next()
                    P.op("pe", lambda e: e.matmul(ps4[:, 0:64], lhsT=MT[:, 256:384], rhs=TK[:, 128:192], start=True, stop=True),
                         reads=[bMT, bTK], writes=[bps4])
                    P.op("dve", lambda e: e.tensor_copy(out=TK[:, 0:64], in_=ps4[:, 0:64]), reads=[bps4], writes=[bTK])
                    if DBG == 31:
                        continue
                    F, bF = Fr.next()
                    P.op("pool", lambda e: e.tensor_tensor(out=F[:], in0=MT[:, 0:128], in1=idf[:], op=ADD),
                         reads=[bMT, bidf], writes=[bF])
                    yield
                    Ecur, bEcur = TK[:, 0:128], bTK
                    Xc, bXc = MT[:, 0:128], bMT
                    Yc, bYc = XY[:, 128:256], bXY
                    for lev in range(6):
                        pe_, bpe = psr.next()
                        P.op("pe", lambda e: e.matmul(pe_[:, 0:128], lhsT=F[:], rhs=Ecur, start=True, stop=True),
                             reads=[bF, bEcur], writes=[bpe])
                        En, bEn = Er.next()
                        P.op("act", lambda e: e.copy(out=En[:], in_=pe_[:, 0:128]), reads=[bpe], writes=[bEn])
                        Ecur, bEcur = En[:], bEn
                        yield
                        if lev == 5 or (DBG == 32):
                            break
                        px, bpx = psr.next()
                        P.op("pe", lambda e: e.matmul(px[:, 0:128], lhsT=Yc, rhs=Xc, start=True, stop=True),
                             reads=[bYc, bXc], writes=[bpx], sig=(lev >= 4))
                        if lev < 4:
                            P.op("pe", lambda e: e.matmul(px[:, 128:256], lhsT=Xc, rhs=Yc, start=True, stop=True),
                                 reads=[bYc, bXc], writes=[bpx])
                        XYn, bXYn = XYr.next()
                        F, bF = Fr.next()
                        P.op("dve", lambda e: e.tensor_tensor(out=F[:], in0=px[:, 0:128], in1=idf[:], op=ADD),
                             reads=[bpx, bidf], writes=[bF])
                        if lev < 4:
                            P.op("act", lambda e: e.copy(out=XYn[:], in_=px[:, 0:256]), reads=[bpx], writes=[bXYn])
                        Xc, bXc = XYn[:, 0:128], bXYn
                        Yc, bYc = XYn[:, 128:256], bXYn
                        yield
                        if DBG == 33 + lev:
                            break
                    if DBG == 4:
                        continue
                    if DBG >= 32 and DBG < 40:
                        continue
                    U0 = Ecur[:, 0:64]
                    Wm = Ecur[:, 64:128]
                    PT, bPT = PTr.next()
                    Q, bQ = Qr.next()
                    pq, bpq = pqr[hh].next()
                    for c2 in range(2):
                        Rc = slice(c2 * 64, c2 * 64 + 64)
                        P.op("pe", lambda e: e.matmul(pq[R, c2 * 64:c2 * 64 + 64], lhsT=Wm[Rc, :], rhs=TK[Rc, 192:256],
                                                      start=True, stop=True), reads=[bEcur, bTK], writes=[bpq])
                        gC = G[R, cp * 128 + c2 * 64 + 63:cp * 128 + c2 * 64 + 64]
                        P.op("dve", lambda e: e.scalar_tensor_tensor(out=PT[R, c2, :], in0=idf[R, R], scalar=gC,
                                                                     in1=pq[R, c2 * 64:c2 * 64 + 64], op0=MUL, op1=ADD),
                             reads=[bidf, bG, bpq], writes=[bPT])
                        P.op("pe", lambda e: e.matmul(pq[R, 128 + c2 * 64:192 + c2 * 64], lhsT=TK[Rc, 192:256], rhs=U0[Rc, :],
                                                      start=True, stop=False), reads=[bEcur, bTK], writes=[bpq], sig=False)
                        P.op("pe", lambda e: e.matmul(pq[R, 128 + c2 * 64:192 + c2 * 64], lhsT=TK[Rc, 256:320], rhs=TK[Rc, 128:192],
                                                      start=False, stop=True), reads=[bTK], writes=[bpq])
                        yield
                    P.op("act", lambda e: e.copy(out=Q[R, :, :].rearrange("p a b -> p (a b)"), in_=pq[R, 128:256]),
                         reads=[bpq], writes=[bQ])
                    ZT, bZT = ZTr.next()
                    pz, bpz = psr.next()
                    P.op("pe", lambda e: e.matmul(pz[R, 0:128], lhsT=Wm, rhs=MT[:, 128:256], start=True, stop=True),
                         reads=[bEcur, bMT], writes=[bpz])
                    P.op("dve", lambda e: e.tensor_tensor(out=ZT[R, :], in0=pz[R, 0:128], in1=rT[R, Cc], op=ADD),
                         reads=[bpz, brT], writes=[bZT])
                    yield
                    if DBG == 5:
                        continue
                    py, bpy = pyr[hh].next()
                    P.op("pe", lambda e: e.matmul(py[R, 0:128], lhsT=U0, rhs=MT[:, 128:256], start=True, stop=False),
                         reads=[bEcur, bMT], writes=[bpy], sig=False)
                    P.op("pe", lambda e: e.matmul(py[R, 0:128], lhsT=TK[:, 128:192], rhs=MT[:, 384:512], start=False, stop=False),
                         reads=[bTK, bMT], writes=[bpy], sig=False)
                    for c2 in range(2):
                        H, bH = Hcur[hd]
                        P.op("pe", lambda e: e.matmul(py[R, c2 * 64:c2 * 64 + 64], lhsT=H[R, :], rhs=ZT[R, c2 * 64:c2 * 64 + 64],
                                                      start=False, stop=(c2 == 1)), reads=[bH, bZT], writes=[bpy], sig=(c2 == 1))
                        ph, bph = psr.next()
                        P.op("pe", lambda e: e.matmul(ph[R, 0:64], lhsT=PT[R, c2, :], rhs=H[R, :], start=True, stop=True),
                             reads=[bPT, bH], writes=[bph])
                        Hn, bHn = Hr[hd].next()
                        P.op("dve", lambda e: e.tensor_tensor(out=Hn[R, :], in0=ph[R, 0:64], in1=Q[R, c2, :], op=ADD),
                             reads=[bph, bQ], writes=[bHn])
                        Hcur[hd] = (Hn, bHn)
                        yield
                    P.op("act", lambda e: e.copy(out=y[R, Cc], in_=py[R, 0:128]), reads=[bpy], writes=[by])
                    yield

            gens = [head_stream(hh_) for hh_ in range(2 if DBG != 1 else 0)]
            bg = NEXT[0]
            while gens:
                for g_ in list(gens):
                    try:
                        next(g_)
                    except StopIteration:
                        gens.remove(g_)
                if bg is not None and next(bg) == "PREP_DONE":
                    bg = None
            while bg is not None:
                if next(bg) == "PREP_DONE":
                    bg = None
            ps, bps = psr.next()
            P.op("pe", lambda e: e.matmul(ps[:], lhsT=blk, rhs=y[:], start=True, stop=True), reads=[bcs, by], writes=[bps])
            yc, byc = tr.next()
            P.op("dve", lambda e: e.scalar_tensor_tensor(out=yc[:], in0=ps[:], scalar=-1.0 / 64, in1=y[:], op0=MUL, op1=ADD),
                 reads=[bps, by], writes=[byc])
            sq, bsq = tr.next()
            P.op("pool", lambda e: e.tensor_tensor(out=sq[:], in0=yc[:], in1=yc[:], op=MUL), reads=[byc], writes=[bsq])
            ps, bps = psr.next()
            P.op("pe", lambda e: e.matmul(ps[:], lhsT=blk, rhs=sq[:], start=True, stop=True), reads=[bcs, bsq], writes=[bps])
            rs, brs = tr.next()
            P.op("dve", lambda e: e.tensor_scalar(out=rs[:], in0=ps[:], scalar1=1.0 / 64, scalar2=64e-5, op0=MUL, op1=ADD),
                 reads=[bps], writes=[brs])
            P.op("act", lambda e: e.activation(out=rs[:], in_=rs[:], func=AF.Sqrt), reads=[brs], writes=[brs])
            P.op("dve", lambda e: e.reciprocal(out=rs[:], in_=rs[:]), reads=[brs], writes=[brs])
            P.op("pool", lambda e: e.tensor_tensor(out=yc[:], in0=yc[:], in1=rs[:], op=MUL), reads=[byc, brs], writes=[byc])
            P.op("dve", lambda e: e.tensor_scalar(out=yc[:], in0=yc[:], scalar1=pc(pr_, "ln_g"), scalar2=pc(pr_, "ln_b"),
                                                  op0=MUL, op1=ADD), reads=[byc, bprm], writes=[byc])
            P.op("pool", lambda e: e.tensor_tensor(out=yc[:], in0=yc[:], in1=bon[:], op=ADD), reads=[byc, bbon], writes=[byc])
            gt, bgt = gr.next()
            P.dma("act", gt[:], U[r0["B_g"] + pr_ * 128:r0["B_g"] + (pr_ + 1) * 128, tt * 512:(tt + 1) * 512], writes=[bgt])
            P.op("act", lambda e: e.activation(out=gt[:], in_=gt[:], func=AF.Silu), reads=[bgt], writes=[bgt])
            ob, bob = obr.next()
            P.op("dve", lambda e: e.tensor_tensor(out=ob[:], in0=yc[:], in1=gt[:], op=MUL), reads=[byc, bgt], writes=[bob])
            bw_ = Buf("ysw")
            P.dma("sp", YS[CW + pr_ * 128:CW + (pr_ + 1) * 128, tt * 512:(tt + 1) * 512], ob[:], reads=[bob], writes=[bw_])
            YSW.setdefault(tt, []).append(bw_)
            if on_tile is not None and pr_ == 1:
                on_tile(tt, YSW[tt])

        items = [item(tt_, p_) for tt_ in range(NT) for p_ in range(2)]
        while next(items[0]) != "PREP_DONE":
            pass
        for i_, it_ in enumerate(items):
            NEXT[0] = items[i_ + 1] if i_ + 1 < len(items) else None
            for _ in it_:
                pass
        P.barrier()


def stage_conv(P, nc, U, prm_ap, YS, ntok):
    MUL, ADD = ALU.mult, ALU.add
    TW = 2048 if ntok >= 2048 else ntok
    with ExitStack() as st:
        prm, bprm = sb(nc, st, "c_prm", [128, 2, NPRM], F32)
        P.dma("sp", prm[:], prm_ap, writes=[bprm])
        cr = Ring(nc, st, "c_c", [128, TW + 2], F32, 2)
        xr = Ring(nc, st, "c_x", [128, TW + 2], F32, 2)
        br = Ring(nc, st, "c_b", [128, TW], F32, 2)
        gr = Ring(nc, st, "c_g", [128, TW], F32, 2)
        zr = Ring(nc, st, "c_z", [128, TW], F32, 2)
        obr = Ring(nc, st, "c_o", [128, TW], BF16, 2)
        for c in range(2):
            for tt in range(ntok // TW):
                t0 = tt * TW
                ct, bc = cr.next(); xt, bx = xr.next(); bt, bb = br.next(); gt, bg = gr.next()
                rows = lambda n: slice(UROW[n][0] + c * 128, UROW[n][0] + (c + 1) * 128)
                if tt == 0:
                    P.op("pool", lambda e: e.memset(ct[:, 0:2], 0.0), writes=[bc])
                    P.op("pool", lambda e: e.memset(xt[:, 0:2], 0.0), writes=[bx])
                    P.dma("sp", ct[:, 2:], U[rows("A_c"), 0:TW], writes=[bc])
                    P.dma("act", xt[:, 2:], U[rows("A_x"), 0:TW], writes=[bx])
                else:
                    P.dma("sp", ct[:, :], U[rows("A_c"), t0 - 2:t0 + TW], writes=[bc])
                    P.dma("act", xt[:, :], U[rows("A_x"), t0 - 2:t0 + TW], writes=[bx])
                P.dma("sp", bt[:], U[rows("A_b"), t0:t0 + TW], writes=[bb])
                P.dma("act", gt[:], U[rows("A_g"), t0:t0 + TW], writes=[bg])
                P.op("dve", lambda e: e.tensor_tensor(out=ct[:], in0=ct[:], in1=xt[:], op=MUL), reads=[bc, bx], writes=[bc])
                z, bz = zr.next()
                P.op("dve", lambda e: e.tensor_scalar(out=z[:], in0=ct[:, 0:TW], scalar1=prm[:, c, 11:12], scalar2=None, op0=MUL),
                     reads=[bc, bprm], writes=[bz])
                P.op("dve", lambda e: e.scalar_tensor_tensor(out=z[:], in0=ct[:, 1:TW + 1], scalar=prm[:, c, 12:13], in1=z[:], op0=MUL, op1=ADD),
                     reads=[bc, bprm, bz], writes=[bz])
                P.op("dve", lambda e: e.scalar_tensor_tensor(out=z[:], in0=ct[:, 2:TW + 2], scalar=prm[:, c, 13:14], in1=z[:], op0=MUL, op1=ADD),
                     reads=[bc, bprm, bz], writes=[bz])
                P.op("act", lambda e: e.activation(out=gt[:], in_=gt[:], func=AF.Silu), reads=[bg], writes=[bg])
                P.op("pool", lambda e: e.tensor_tensor(out=z[:], in0=z[:], in1=bt[:], op=MUL), reads=[bz, bb], writes=[bz])
                ob, bob = obr.next()
                P.op("pool", lambda e: e.tensor_tensor(out=ob[:], in0=z[:], in1=gt[:], op=MUL), reads=[bz, bg], writes=[bob])
                for o5 in range(0, TW, 512):
                    P.dma("sp", YS[c * 128:(c + 1) * 128, t0 + o5:t0 + o5 + 512], ob[:, o5:o5 + 512], reads=[bob])
        P.barrier()


def host_poolc(hg, ntok):
    win = (2, 4, 8, 16)[hg]
    sel = np.zeros((128, 4), np.float32)
    sel[:, hg] = 1.0
    invc = (1.0 / np.minimum(np.arange(ntok) + 1, win)).astype(np.float32)[None, :]
    return sel, invc


def stage_pool(P, nc, U, prm_ap, sel_ap, invc_ap, pw_ap, YS, ntok):
    MUL, ADD, SUB = ALU.mult, ALU.add, ALU.subtract
    TW = 2048 if ntok >= 2048 else ntok
    H = 16
    with ExitStack() as st:
        prm, bprm = sb(nc, st, "l_prm", [128, 2, NPRM], F32)
        P.dma("sp", prm[:], prm_ap, writes=[bprm])
        sel, bsel = sb(nc, st, "l_sel", [128, 4], F32)
        P.dma("sp", sel[:], sel_ap, writes=[bsel])
        pw, bpw = sb(nc, st, "l_pw", [128, 2, CW], BF16)
        P.dma("pool", pw[:], pw_ap.rearrange("(cc p) e -> p cc e", p=128), writes=[bpw])
        ivr = Ring(nc, st, "l_iv", [128, TW], F32, 1)
        xr = Ring(nc, st, "l_x", [128, TW + H], F32, 3)
        sr = Ring(nc, st, "l_s", [128, TW + H], F32, 4)
        cr = Ring(nc, st, "l_cmb", [128, TW], F32, 2)
        pbr = Ring(nc, st, "l_pb", [128, 2, TW], BF16, 1)
        gr = Ring(nc, st, "l_g", [128, 512], F32, 2)
        obr = Ring(nc, st, "l_o", [128, 512], BF16, 2)
        psr = Ring(nc, st, "l_ps", [128, 512], F32, 2, psum=True)
        for tt in range(ntok // TW):
            t0 = tt * TW
            iv, biv = ivr.next()
            P.dma("sp", iv[:], invc_ap[0:1, t0:t0 + TW].partition_broadcast(128), writes=[biv])
            pb, bpb = pbr.next()
            for c in range(2):
                rows = slice(UROW["D_x"][0] + c * 128, UROW["D_x"][0] + (c + 1) * 128)
                xt, bx = xr.next()
                if tt == 0:
                    P.op("pool", lambda e: e.memset(xt[:, 0:H], 0.0), writes=[bx])
                    P.dma("sp", xt[:, H:], U[rows, 0:TW], writes=[bx])
                else:
                    P.dma("sp", xt[:, :], U[rows, t0 - H:t0 + TW], writes=[bx])
                prev, bprev = xt, bx
                cmb, bcmb = cr.next()
                sh = 1
                for i in range(4):
                    s, bs = sr.next()
                    lo = 2 * sh - 1
                    P.op("pool" if i % 2 else "dve", lambda e: e.tensor_tensor(out=s[:, lo:], in0=prev[:, lo:], in1=prev[:, lo - sh:TW + H - sh], op=ADD),
                         reads=[bprev], writes=[bs])
                    if i == 0:
                        P.op("dve", lambda e: e.tensor_scalar(out=cmb[:], in0=s[:, H:], scalar1=sel[:, 0:1], scalar2=None, op0=MUL),
                             reads=[bs, bsel], writes=[bcmb])
                    else:
                        P.op("dve", lambda e: e.scalar_tensor_tensor(out=cmb[:], in0=s[:, H:], scalar=sel[:, i:i + 1], in1=cmb[:], op0=MUL, op1=ADD),
                             reads=[bs, bsel, bcmb], writes=[bcmb])
                    prev, bprev = s, bs
                    sh *= 2
                P.op("pool", lambda e: e.tensor_tensor(out=cmb[:], in0=cmb[:], in1=iv[:], op=MUL), reads=[bcmb, biv], writes=[bcmb])
                P.op("dve", lambda e: e.tensor_tensor(out=pb[:, c, :], in0=cmb[:], in1=xt[:, H:], op=SUB), reads=[bcmb, bx], writes=[bpb])
            for t5 in range(TW // 512):
                for ec in range(2):
                    ps, bps = psr.next()
                    for cc in range(2):
                        P.op("pe", lambda e: e.matmul(ps[:], lhsT=pw[:, cc, ec * 128:(ec + 1) * 128], rhs=pb[:, cc, t5 * 512:(t5 + 1) * 512],
                                                      start=(cc == 0), stop=(cc == 1)), reads=[bpw, bpb], writes=[bps], sig=(cc == 1))
                    gt, bg = gr.next()
                    grow = slice(UROW["D_g"][0] + ec * 128, UROW["D_g"][0] + (ec + 1) * 128)
                    P.dma("act", gt[:], U[grow, t0 + t5 * 512:t0 + (t5 + 1) * 512], writes=[bg])
                    P.op("act", lambda e: e.activation(out=gt[:], in_=gt[:], func=AF.Silu), reads=[bg], writes=[bg])
                    ob, bob = obr.next()
                    P.op("dve", lambda e: e.scalar_tensor_tensor(out=ob[:], in0=ps[:], scalar=prm[:, ec, 14:15], in1=gt[:], op0=MUL, op1=MUL),
                         reads=[bps, bprm, bg], writes=[bob])
                    P.dma("sp", YS[3 * CW + ec * 128:3 * CW + (ec + 1) * 128, t0 + t5 * 512:t0 + (t5 + 1) * 512], ob[:], reads=[bob])
        P.barrier()


def stage_fox(P, nc, U, bf_ap, cst, YS, ntok, identf, FC):
    MUL, ADD, SUB = ALU.mult, ALU.add, ALU.subtract
    idf, bidf = identf
    cs, bcs = cst
    NQ = ntok // 512
    NK = ntok // 128
    LW = 2048 if ntok >= 2048 else ntok
    with ExitStack() as st:
        SEG = 2048 if ntok >= 2048 else ntok
        with ExitStack() as st0:
            f, bfb = sb(nc, st0, "f_f", [4, SEG], F32)
            one4, bone4 = sb(nc, st0, "f_one", [4, SEG], F32)
            bft, bbft = sb(nc, st0, "f_bf", [4, 2], F32)
            carry, bcar = sb(nc, st0, "f_car", [4, 2], F32)
            parts, bparts = sb(nc, st0, "f_parts", [4, 6, SEG], BF16)
            r1, br1 = sb(nc, st0, "f_r1", [4, SEG], F32)
            P.dma("sp", bft[:, 0:1], bf_ap, writes=[bbft])
            P.op("dve", lambda e: e.tensor_scalar(out=bft[:, 1:2], in0=bft[:, 0:1], scalar1=-1.0, scalar2=None, op0=MUL), reads=[bbft], writes=[bbft])
            P.op("pool", lambda e: e.memset(one4[:], 1.0), writes=[bone4])
            P.op("pool", lambda e: e.memset(carry[:], 0.0), writes=[bcar])
            for s0 in range(0, ntok, SEG):
                P.dma("sp", f[:], U[UROW["C_f"][0]:UROW["C_f"][0] + 4, s0:s0 + SEG], writes=[bfb])
                P.op("act", lambda e: e.activation(out=f[:], in_=f[:], func=AF.Exp, bias=bft[:, 1:2], scale=-1.0), reads=[bfb, bbft], writes=[bfb])
                P.op("dve", lambda e: e.tensor_scalar(out=f[:], in0=f[:], scalar1=1.0, scalar2=None, op0=ADD), reads=[bfb], writes=[bfb])
                P.op("act", lambda e: e.activation(out=f[:], in_=f[:], func=AF.Ln), reads=[bfb], writes=[bfb])
                P.op("dve", lambda e: e.tensor_tensor_scan(out=r1[:], data0=one4[:], data1=f[:], initial=carry[:, 0:1], op0=MUL, op1=ADD),
                     reads=[bone4, bfb, bcar], writes=[br1])
                P.op("dve", lambda e: e.tensor_copy(out=carry[:, 0:1], in_=r1[:, SEG - 1:SEG]), reads=[br1], writes=[bcar])
                P.op("dve", lambda e: e.tensor_scalar(out=r1[:], in0=r1[:], scalar1=8.0, scalar2=None, op0=MUL), reads=[br1], writes=[br1])
                for i in range(3):
                    P.op("dve", lambda e: e.tensor_copy(out=parts[:, i, :], in_=r1[:]), reads=[br1], writes=[bparts])
                    P.op("dve", lambda e: e.tensor_scalar(out=parts[:, 3 + i, :], in0=parts[:, i, :], scalar1=-1.0, scalar2=None, op0=MUL),
                         reads=[bparts], writes=[bparts])
                    if i < 2:
                        P.op("dve", lambda e: e.tensor_tensor(out=r1[:], in0=r1[:], in1=parts[:, i, :], op=SUB), reads=[br1, bparts], writes=[br1])
                P.dma("sp", FC[:, :, s0:s0 + SEG], parts[:], reads=[bparts])
            P.barrier()
        maskb, bmaskb = sb(nc, st, "f_mask", [128, 128], BF16)
        P.op("dve", lambda e: e.tensor_copy(out=maskb[:], in_=cs[:, 1280:1408]), reads=[bcs], writes=[bmaskb])
        onesf, bonesf = sb(nc, st, "f_ones", [128, 64], F32)
        P.op("pool", lambda e: e.memset(onesf[:], 1.0), writes=[bonesf])
        qa, bqa = sb(nc, st, "f_qa", [70, ntok], BF16)
        ka, bka = sb(nc, st, "f_ka", [70, ntok], BF16)
        va, bva = sb(nc, st, "f_va", [128, NK, 65], BF16)
        ldr = Ring(nc, st, "f_ld", [64, LW], F32, 2)
        psr = Ring(nc, st, "f_ps", [128, 512], F32, 5, psum=True)
        por = Ring(nc, st, "f_po", [128, 512], F32, 2, psum=True)
        ptr_ = Ring(nc, st, "f_pt", [128, 512], BF16, 4)
        rdr = Ring(nc, st, "f_rd", [128, 512], F32, 2)
        osr = Ring(nc, st, "f_os", [64, 512], F32, 2)
        gr = Ring(nc, st, "f_g", [64, 512], F32, 2)
        obr = Ring(nc, st, "f_ob", [64, 512], BF16, 2)
        for h in range(4):
            qrow = UROW["C_q"][0] + h * 64
            krow = UROW["C_k"][0] + h * 64
            vrow = UROW["C_v"][0] + h * 64
            P.op("pool", lambda e: e.memset(qa[64:70, :], 1.0), writes=[bqa])
            P.op("pool", lambda e: e.memset(ka[64:70, :], 1.0), writes=[bka])
            P.op("pool", lambda e: e.memset(va[:, :, 64:65], 1.0), writes=[bva])
            for i in range(3):
                P.dma("sp", qa[64 + i:65 + i, :], FC[h:h + 1, 3 + i, :], writes=[bqa])
                P.dma("sp", ka[67 + i:68 + i, :], FC[h:h + 1, i, :], writes=[bka])
            for l0 in range(0, ntok, LW):
                for (row, dst, bdst, eng) in ((qrow, qa, bqa, "act"), (krow, ka, bka, "dve")):
                    ld, bld = ldr.next()
                    P.dma("sp", ld[:], U[row:row + 64, l0:l0 + LW], writes=[bld])
                    if eng == "act":
                        P.op("act", lambda e: e.copy(out=dst[0:64, l0:l0 + LW], in_=ld[:]), reads=[bld], writes=[bdst])
                    else:
                        P.op("dve", lambda e: e.tensor_copy(out=dst[0:64, l0:l0 + LW], in_=ld[:]), reads=[bld], writes=[bdst])
                ld, bld = ldr.next()
                P.dma("act", ld[:], U[vrow:vrow + 64, l0:l0 + LW], writes=[bld])
                for j8 in range(LW // 1024):
                    ps, bps = psr.next()
                    for j in range(8):
                        P.op("pe", lambda e: e.transpose(out=ps[:, j * 64:(j + 1) * 64], in_=ld[:, j8 * 1024 + j * 128:j8 * 1024 + (j + 1) * 128],
                                                         identity=idf[0:64, 0:64]), reads=[bld, bidf], writes=[bps], sig=(j == 7))
                    jb = l0 // 128 + j8 * 8
                    P.op("dve", lambda e: e.tensor_copy(out=va[:, jb:jb + 8, 0:64], in_=ps[:, :].rearrange("p (j d) -> p j d", d=64)),
                         reads=[bps], writes=[bva])
            for qc in range(NQ):
                nkt = 4 * (qc + 1)
                po, bpo = por.next()
                LA = 3

                def emit_st(j):
                    d = j - 4 * qc
                    c0 = max(0, d) * 128
                    ps, bps = psr.next()
                    P.op("pe", lambda e: e.matmul(ps[:, c0:512], lhsT=ka[0:70, j * 128:(j + 1) * 128], rhs=qa[0:70, qc * 512 + c0:(qc + 1) * 512],
                                                  start=True, stop=True), reads=[bka, bqa], writes=[bps])
                    return ps, bps, c0, d

                pend = {}
                for j in range(min(LA, nkt)):
                    pend[j] = emit_st(j)
                for j in range(nkt):
                    if j + LA < nkt:
                        pend[j + LA] = emit_st(j + LA)
                    ps, bps, c0, d = pend.pop(j)
                    pt, bpt = ptr_.next()
                    P.op("act", lambda e: e.activation(out=pt[:, c0:512], in_=ps[:, c0:512], func=AF.Exp, scale=0.125), reads=[bps], writes=[bpt])
                    if d >= 0:
                        P.op("pool", lambda e: e.tensor_tensor(out=pt[:, c0:c0 + 128], in0=pt[:, c0:c0 + 128], in1=maskb[:], op=MUL),
                             reads=[bpt, bmaskb], writes=[bpt])
                    P.op("pe", lambda e: e.matmul(po[0:65, c0:512], lhsT=va[:, j, 0:65], rhs=pt[:, c0:512], start=(j == 0), stop=(j == nkt - 1)),
                         reads=[bva, bpt], writes=[bpo], sig=(j == nkt - 1))
                rd, brd = rdr.next()
                P.op("dve", lambda e: e.reciprocal(out=rd[64:65, :], in_=po[64:65, :]), reads=[bpo], writes=[brd])
                pb, bpb = psr.next()
                P.op("pe", lambda e: e.matmul(pb[0:64, :], lhsT=onesf[64:65, 0:64], rhs=rd[64:65, :], start=True, stop=True),
                     reads=[bonesf, brd], writes=[bpb])
                os_, bos = osr.next()
                P.op("act", lambda e: e.copy(out=os_[:], in_=po[0:64, :]), reads=[bpo], writes=[bos])
                gt, bg = gr.next()
                grow = UROW["C_g"][0] + h * 64
                P.dma("sp", gt[:], U[grow:grow + 64, qc * 512:(qc + 1) * 512], writes=[bg])
                P.op("act", lambda e: e.activation(out=gt[:], in_=gt[:], func=AF.Silu), reads=[bg], writes=[bg])
                P.op("dve", lambda e: e.tensor_tensor(out=os_[:], in0=os_[:], in1=pb[0:64, :], op=MUL), reads=[bos, bpb], writes=[bos])
                ob, bob = obr.next()
                P.op("pool", lambda e: e.tensor_tensor(out=ob[:], in0=os_[:], in1=gt[:], op=MUL), reads=[bos, bg], writes=[bob])
                P.dma("sp", YS[2 * CW + h * 64:2 * CW + (h + 1) * 64, qc * 512:(qc + 1) * 512], ob[:], reads=[bob])
        P.barrier()


def stage_back(P, nc, x_ap, HT, YSg, wm_ap, wb_ap, wo_ap, bm_ap, xo_ap, ntok, fg_ap=None):
    MUL, ADD = ALU.mult, ALU.add
    with ExitStack() as st:
        bm, bbm = sb(nc, st, "b_bm", [128, 4, KC], F32)
        P.dma("sp", bm[:], bm_ap, writes=[bbm])
        wo, bwo = sb(nc, st, "b_wo", [128, KC, D], BF16)
        wov = wo_ap.rearrange("(kc p) e -> p kc e", p=128)
        for kc in range(0, KC, 2):
            P.dma("pool", wo[:, kc:kc + 2, :], wov[:, kc:kc + 2, :], writes=[bwo])
        if fg_ap is not None:
            fg, bfg = sb(nc, st, "b_fg", [128, D], F32)
            P.dma("sp", fg[:], fg_ap[0:1, :].partition_broadcast(128), writes=[bfg])
        hr = Ring(nc, st, "b_h", [128, KC, 512], BF16, 1)
        yr = Ring(nc, st, "b_y", [128, 32, 512], BF16, 1)
        mr = Ring(nc, st, "b_m", [128, KC, 512], BF16, 1)
        wmr = Ring(nc, st, "b_wm", [128, KC, 128], BF16, 3)
        wbr = Ring(nc, st, "b_wb", [128, 8, 128], BF16, 3)
        gr = Ring(nc, st, "b_g", [128, 512], F32, 2)
        ar = Ring(nc, st, "b_a", [128, 512], F32, 2)
        tr = Ring(nc, st, "b_t", [128, 512], F32, 2)
        xr = Ring(nc, st, "b_x", [128, D], F32, 2)
        sr = Ring(nc, st, "b_s", [128, 4], F32, 2)
        jr = Ring(nc, st, "b_j", [128, D], BF16, 1)
        psm = Ring(nc, st, "b_psm", [128, 512], F32, 2, psum=True)
        psp = Ring(nc, st, "b_psp", [128, 512], F32, 2, psum=True)
        pso = Ring(nc, st, "b_pso", [128, 512], F32, 2, psum=True)
        wmv = wm_ap.rearrange("(kc p) c -> p kc c", p=128)
        ysv = YSg.rearrange("(j p) t -> p j t", p=128)
        for tt in range(ntok // 512):
            ht, bh = hr.next()
            P.dma("sp", ht[:], HT[:, :, tt * 512:(tt + 1) * 512], writes=[bh])
            ys, bys = yr.next()
            for j0 in range(0, 32, 8):
                P.dma("act", ys[:, j0:j0 + 8, :], ysv[:, j0:j0 + 8, tt * 512:(tt + 1) * 512], writes=[bys])
            mg, bmg = mr.next()
            for dc in range(KC):
                acc, bacc = ar.next()
                for k in range(4):
                    wmt, bwm = wmr.next()
                    c0 = k * D + dc * 128
                    P.dma("pool", wmt[:], wmv[:, :, c0:c0 + 128], writes=[bwm])
                    wbt, bwb = wbr.next()
                    P.dma("pool", wbt[:], wb_ap[k, :, dc * 128:(dc + 1) * 128].rearrange("(cc p) d -> p cc d", p=128), writes=[bwb])
                    pm, bpm = psm.next()
                    for kc in range(KC):
                        P.op("pe", lambda e: e.matmul(pm[:], lhsT=wmt[:, kc, :], rhs=ht[:, kc, :], start=(kc == 0), stop=(kc == KC - 1)),
                             reads=[bwm, bh], writes=[bpm])
                    gt, bg = gr.next()
                    P.op("act", lambda e: e.activation(out=gt[:], in_=pm[:], func=AF.Sigmoid, bias=bm[:, k, dc:dc + 1], scale=1.0),
                         reads=[bpm, bbm], writes=[bg])
                    pp, bpp = psp.next()
                    for cc in range(8):
                        P.op("pe", lambda e: e.matmul(pp[:], lhsT=wbt[:, cc, :], rhs=ys[:, k * 8 + cc, :], start=(cc == 0), stop=(cc == 7)),
                             reads=[bwb, bys], writes=[bpp])
                    if k == 0:
                        P.op("dve", lambda e: e.tensor_tensor(out=acc[:], in0=pp[:], in1=gt[:], op=MUL), reads=[bpp, bg], writes=[bacc])
                    else:
                        t_, bt_ = tr.next()
                        P.op("dve", lambda e: e.tensor_tensor(out=t_[:], in0=pp[:], in1=gt[:], op=MUL), reads=[bpp, bg], writes=[bt_])
                        if k < 3:
                            P.op("pool", lambda e: e.tensor_tensor(out=acc[:], in0=acc[:], in1=t_[:], op=ADD), reads=[bacc, bt_], writes=[bacc])
                        else:
                            P.op("pool", lambda e: e.tensor_tensor(out=mg[:, dc, :], in0=acc[:], in1=t_[:], op=ADD), reads=[bacc, bt_], writes=[bmg])
            for ts in range(4):
                xt, bx = xr.next()
                r0_ = tt * 512 + ts * 128
                P.dma("sp", xt[:], x_ap[r0_:r0_ + 128, :], writes=[bx])
                for ec in range(4):
                    po, bpo = pso.next()
                    for dc in range(KC):
                        P.op("pe", lambda e: e.matmul(po[:], lhsT=mg[:, dc, ts * 128:(ts + 1) * 128], rhs=wo[:, dc, ec * 512:(ec + 1) * 512],
                                                      start=(dc == 0), stop=(dc == KC - 1)), reads=[bmg, bwo], writes=[bpo], sig=(dc == KC - 1))
                    P.op("dve", lambda e: e.tensor_tensor(out=xt[:, ec * 512:(ec + 1) * 512], in0=po[:], in1=xt[:, ec * 512:(ec + 1) * 512], op=ADD),
                         reads=[bpo, bx], writes=[bx])
                if fg_ap is not None:
                    s, bs = sr.next()
                    j, bj = jr.next()
                    P.op("act", lambda e: e.activation(out=j[:], in_=xt[:], func=AF.Square, accum_out=s[:, 0:1]), reads=[bx], writes=[bj, bs])
                    P.op("dve", lambda e: e.tensor_scalar(out=s[:, 1:2], in0=s[:, 0:1], scalar1=1.0 / D, scalar2=EPS, op0=MUL, op1=ADD),
                         reads=[bs], writes=[bs])
                    P.op("act", lambda e: e.activation(out=s[:, 1:2], in_=s[:, 1:2], func=AF.Sqrt), reads=[bs], writes=[bs])
                    P.op("dve", lambda e: e.reciprocal(out=s[:, 2:3], in_=s[:, 1:2]), reads=[bs], writes=[bs])
                    P.op("dve", lambda e: e.scalar_tensor_tensor(out=xt[:], in0=xt[:], scalar=s[:, 2:3], in1=fg[:], op0=MUL, op1=MUL),
                         reads=[bx, bs, bfg], writes=[bx])
                P.dma("sp", xo_ap[r0_:r0_ + 128, :], xt[:], reads=[bx])
        P.barrier()


DQ = D // HG


def stage_back_a(P, nc, HT, YSall, wm_ap, wb_ap, bm_ap, MT, ntok, on_chunk=None):
    MUL, ADD = ALU.mult, ALU.add
    with ExitStack() as st:
        bm, bbm = sb(nc, st, "ba_bm", [128, 4, 4], F32)
        P.dma("sp", bm[:], bm_ap, writes=[bbm])
        wm, bwm = sb(nc, st, "ba_wm", [128, KC, 4 * DQ], BF16)
        wmv = wm_ap.rearrange("(kc p) c -> p kc c", p=128)
        for kc in range(0, KC, 2):
            P.dma("pool", wm[:, kc:kc + 2, :], wmv[:, kc:kc + 2, :], writes=[bwm])
        wb, bwb = sb(nc, st, "ba_wb", [128, 4, 8, DQ], BF16)
        for k in range(4):
            P.dma("pool", wb[:, k, :, :], wb_ap[k, :, :].rearrange("(cc p) d -> p cc d", p=128), writes=[bwb])
        hr = Ring(nc, st, "ba_h", [128, KC, 512], BF16, 2)
        yr = Ring(nc, st, "ba_y", [128, 32, 512], BF16, 1)
        gr = Ring(nc, st, "ba_g", [128, 512], F32, 2)
        ar = Ring(nc, st, "ba_a", [128, 512], F32, 2)
        tr = Ring(nc, st, "ba_t", [128, 512], F32, 2)
        mr = Ring(nc, st, "ba_m", [128, 512], BF16, 2)
        psm = Ring(nc, st, "ba_psm", [128, 512], F32, 3, psum=True)
        psp = Ring(nc, st, "ba_psp", [128, 512], F32, 3, psum=True)
        mtw = []
        bysk = [Buf("ysk%d" % k_) for k_ in range(4)]
        for tt in range(ntok // 512):
            ht, bh = hr.next()
            P.dma("sp", ht[:], HT[:, :, tt * 512:(tt + 1) * 512], writes=[bh])
            ys, _ = yr.next()
            for k in range(4):
                for hg in range(HG):
                    row = hg * 4 * CW + k * CW
                    P.dma("act" if (k + hg) % 2 else "sp", ys[:, k * 8 + hg * 2:k * 8 + hg * 2 + 2, :],
                          YSall[row:row + CW, tt * 512:(tt + 1) * 512].rearrange("(j p) t -> p j t", p=128), writes=[bysk[k]])
            for dcl in range(4):
                acc, bacc = ar.next()
                for k in range(4):
                    pm, bpm = psm.next()
                    for kc in range(KC):
                        P.op("pe", lambda e: e.matmul(pm[:], lhsT=wm[:, kc, k * DQ + dcl * 128:k * DQ + (dcl + 1) * 128], rhs=ht[:, kc, :],
                                                      start=(kc == 0), stop=(kc == KC - 1)), reads=[bwm, bh], writes=[bpm], sig=(kc == KC - 1))
                    gt, bg = gr.next()
                    P.op("act", lambda e: e.activation(out=gt[:], in_=pm[:], func=AF.Sigmoid, bias=bm[:, k, dcl:dcl + 1], scale=1.0),
                         reads=[bpm, bbm], writes=[bg])
                    pp, bpp = psp.next()
                    for cc in range(8):
                        P.op("pe", lambda e: e.matmul(pp[:], lhsT=wb[:, k, cc, dcl * 128:(dcl + 1) * 128], rhs=ys[:, k * 8 + cc, :],
                                                      start=(cc == 0), stop=(cc == 7)), reads=[bwb, bysk[k]], writes=[bpp], sig=(cc == 7))
                    if k == 0:
                        P.op("dve", lambda e: e.tensor_tensor(out=acc[:], in0=pp[:], in1=gt[:], op=MUL), reads=[bpp, bg], writes=[bacc])
                    else:
                        t_, bt_ = tr.next()
                        P.op("dve", lambda e: e.tensor_tensor(out=t_[:], in0=pp[:], in1=gt[:], op=MUL), reads=[bpp, bg], writes=[bt_])
                        if k < 3:
                            P.op("pool", lambda e: e.tensor_tensor(out=acc[:], in0=acc[:], in1=t_[:], op=ADD), reads=[bacc, bt_], writes=[bacc])
                        else:
                            mg, bmg = mr.next()
                            P.op("pool", lambda e: e.tensor_tensor(out=mg[:], in0=acc[:], in1=t_[:], op=ADD), reads=[bacc, bt_], writes=[bmg])
                            bw_ = Buf("mtw")
                            P.dma("sp", MT[dcl * 128:(dcl + 1) * 128, tt * 512:(tt + 1) * 512], mg[:], reads=[bmg], writes=[bw_])
                            mtw.append(bw_)
            if on_chunk is not None and tt % 2 == 1:
                on_chunk(tt // 2, mtw)
                mtw = []
        P.barrier()


def stage_back_b(P, nc, MTall, wo_ap, Xcol, ntok, on_tile=None):
    with ExitStack() as st:
        wo, bwo = sb(nc, st, "bb_wo", [128, KC, DQ], BF16)
        P.dma("pool", wo[:], wo_ap.rearrange("(kc p) e -> p kc e", p=128), writes=[bwo])
        mr = Ring(nc, st, "bb_m", [128, KC, 512], BF16, 2)
        xr = Ring(nc, st, "bb_x", [128, DQ], F32, 3)
        pso = Ring(nc, st, "bb_ps", [128, 512], F32, 3, psum=True)
        for tt in range(ntok // 512):
            mg, bmg = mr.next()
            P.dma("sp", mg[:], MTall[0:D, tt * 512:(tt + 1) * 512].rearrange("(dc p) t -> p dc t", p=128), writes=[bmg])
            xw = []
            for ts in range(4):
                r0_ = tt * 512 + ts * 128
                xt, bx = xr.next()
                P.dma("act", xt[:], Xcol[r0_:r0_ + 128, :], writes=[bx])
                po, bpo = pso.next()
                for dc in range(KC):
                    P.op("pe", lambda e: e.matmul(po[:], lhsT=mg[:, dc, ts * 128:(ts + 1) * 128], rhs=wo[:, dc, :],
                                                  start=(dc == 0), stop=(dc == KC - 1)), reads=[bmg, bwo], writes=[bpo], sig=(dc == KC - 1))
                P.op("dve", lambda e: e.tensor_tensor(out=xt[:], in0=po[:], in1=xt[:], op=ALU.add), reads=[bpo, bx], writes=[bx])
                bw_ = Buf("xw")
                P.dma("sp", Xcol[r0_:r0_ + 128, :], xt[:], reads=[bx], writes=[bw_])
                xw.append(bw_)
            if on_tile is not None:
                on_tile(tt, xw)
        P.barrier()


def stage_final(P, nc, xg4, Xcol, fg_ap, out_ap, ntok):
    MUL, ADD = ALU.mult, ALU.add
    with ExitStack() as st:
        fg, bfg = sb(nc, st, "fn_fg", [128, DQ], F32)
        P.dma("sp", fg[:], fg_ap[0:1, :].partition_broadcast(128), writes=[bfg])
        xr = Ring(nc, st, "fn_x", [128, D], F32, 2)
        cr = Ring(nc, st, "fn_c", [128, DQ], F32, 2)
        jr = Ring(nc, st, "fn_j", [128, D], BF16, 1)
        sr = Ring(nc, st, "fn_s", [128, 4], F32, 2)
        for tt in range(ntok // 128):
            xt, bx = xr.next()
            P.dma("sp", xt[:, :].rearrange("p (r e) -> p r e", r=4), xg4[tt * 128:(tt + 1) * 128, :, :], writes=[bx])
            ct, bc = cr.next()
            P.dma("act", ct[:], Xcol[tt * 128:(tt + 1) * 128, :], writes=[bc])
            s, bs = sr.next()
            j, bj = jr.next()
            P.op("act", lambda e: e.activation(out=j[:], in_=xt[:], func=AF.Square, accum_out=s[:, 0:1]), reads=[bx], writes=[bj, bs])
            P.op("dve", lambda e: e.tensor_scalar(out=s[:, 1:2], in0=s[:, 0:1], scalar1=1.0 / D, scalar2=EPS, op0=MUL, op1=ADD),
                 reads=[bs], writes=[bs])
            P.op("act", lambda e: e.activation(out=s[:, 1:2], in_=s[:, 1:2], func=AF.Sqrt), reads=[bs], writes=[bs])
            P.op("dve", lambda e: e.reciprocal(out=s[:, 2:3], in_=s[:, 1:2]), reads=[bs], writes=[bs])
            P.op("dve", lambda e: e.scalar_tensor_tensor(out=ct[:], in0=ct[:], scalar=s[:, 2:3], in1=fg[:], op0=MUL, op1=MUL),
                 reads=[bc, bs, bfg], writes=[bc])
            P.dma("sp", out_ap[tt * 128:(tt + 1) * 128, :], ct[:], reads=[bc])
        P.barrier()


GROUPS = [[0, 1, 2, 3], [4, 5, 6, 7]]


class ColChunks:
    def __init__(self, aps, ch):
        self.aps, self.ch = aps, ch

    def __getitem__(self, idx):
        rsl, csl = idx
        j = csl.start // self.ch
        assert (csl.stop - 1) // self.ch == j
        return self.aps[j][rsl, csl.start - j * self.ch:csl.stop - j * self.ch]


class RowChunks:
    def __init__(self, aps, ch):
        self.aps, self.ch = aps, ch

    def __getitem__(self, idx):
        rsl = idx[0]
        j = rsl.start // self.ch
        assert (rsl.stop - 1) // self.ch == j
        return self.aps[j][(slice(rsl.start - j * self.ch, rsl.stop - j * self.ch),) + tuple(idx[1:])]


def build_fused(depth=DEPTH, S=S):
    nc = bass.Bass("TRN2", target_bir_lowering=False)
    di = lambda n, s, d: nc.dram_tensor(n, s, d, kind="ExternalInput").ap()
    xcol_in = di("xcol", [S, DQ], F32)
    xfull = di("xfull", [S, HG, DQ], F32)
    wc = di("wc", [depth, D, NU], F32)
    g = di("g", [depth, 128, KC], F32)
    prm = di("prm", [depth, 128, 2, NPRM], F32)
    lora = di("lora", [depth, 128, CW], F32)
    pw = di("pw", [depth, CW, CW], F32)
    bf = di("bf", [depth, 4, 1], F32)
    wm = di("wm", [depth, D, 4 * DQ], F32)
    wb = di("wb", [depth, 4, W, DQ], F32)
    wo = di("wo", [depth, D, DQ], F32)
    bm = di("bm", [depth, 128, 4, 4], F32)
    fg = di("fg", [1, DQ], F32)
    cst = di("cst", [128, NCONST], F32)
    sel = di("sel", [128, 4], F32)
    invc = di("invc", [1, S], F32)
    out = nc.dram_tensor("out", [S, DQ], F32, kind="ExternalOutput").ap()
    NXC = S // 512
    Xcol_t = [nc.dram_tensor("Xcol%d" % j, [512, DQ], F32) for j in range(NXC)]
    XG_t = [nc.dram_tensor("XG%d" % j, [HG * 512, DQ], F32) for j in range(NXC)]
    YS_t = [nc.dram_tensor("YS%d" % j, [4 * CW, 512], BF16) for j in range(NXC)]
    YSall_t = [nc.dram_tensor("YSall%d" % j, [HG * 4 * CW, 512], BF16) for j in range(NXC)]
    NMC = S // 1024
    MT_t = [nc.dram_tensor("MT%d" % j, [DQ, 1024], BF16) for j in range(NMC)]
    MTall_t = [nc.dram_tensor("MTall%d" % j, [D, 1024], BF16) for j in range(NMC)]
    HT = nc.dram_tensor("HT", [128, KC, S], BF16).ap()
    U = nc.dram_tensor("U", [NU, S], F32).ap()
    FC = nc.dram_tensor("FC", [4, 6, S], BF16).ap()
    Xcol = RowChunks([t.ap() for t in Xcol_t], 512)
    xg4 = RowChunks([t.ap().rearrange("(r t) e -> t r e", r=HG) for t in XG_t], 512)
    YS = ColChunks([t.ap() for t in YS_t], 512)
    YSall = ColChunks([t.ap() for t in YSall_t], 512)
    MT = ColChunks([t.ap() for t in MT_t], 1024)
    MTall = ColChunks([t.ap() for t in MTall_t], 1024)
    xpairs = list(zip(Xcol_t, XG_t))
    ypairs = list(zip(YS_t, YSall_t))
    mpairs = list(zip(MT_t, MTall_t))
    with ExitStack() as st:
        import os
        skip = os.environ.get("FUSE_SKIP", "")
        P = Prog(nc, st)
        if "i" not in skip:
            idf, idb = make_identity(P, nc, st)
            cs, bcs = sb(nc, st, "cst_sb", [128, NCONST], F32)
            P.dma("sp", cs[:], cst, writes=[bcs])
        for i in range(NXC):
            P.dma("sp" if i % 2 else "act", Xcol[i * 512:(i + 1) * 512, :], xcol_in[i * 512:(i + 1) * 512, :])
        import os
        dbg = os.environ.get("FUSE_DBG", "npcqfrab")
        cb_y = lambda j, deps: P.collective_async("AllGather", YS_t[j], YSall_t[j], GROUPS, deps)
        cb_m = lambda j, deps: P.collective_async("AllGather", MT_t[j], MTall_t[j], GROUPS, deps)
        cb_x = lambda j, deps: P.collective_async("AllGather", Xcol_t[j], XG_t[j], GROUPS, deps)
        for l in range(depth if "L" not in skip else 0):
            if l > 0:
                P.collective_wait()
            if "n" in dbg: stage_norm_T(P, nc, xfull if l == 0 else xg4, g[l], HT, S, idb, gathered=True)
            if "p" in dbg: stage_proj(P, nc, wc[l], NU, HT, S, U)
            if "c" in dbg: stage_conv(P, nc, U, prm[l], YS, S)
            if "q" in dbg: stage_pool(P, nc, U, prm[l], sel, invc, pw[l], YS, S)
            if "f" in dbg: stage_fox(P, nc, U, bf[l], (cs, bcs), YS, S, idf, FC)
            stage_rwkv(P, nc, U, prm[l], lora[l], (cs, bcs), YS, S, idf, on_tile=cb_y)
            P.collective_wait()
            stage_back_a(P, nc, HT, YSall, wm[l], wb[l], bm[l], MT, S, on_chunk=cb_m)
            P.collective_wait()
            stage_back_b(P, nc, MTall, wo[l], Xcol, S, on_tile=cb_x)
            if "e" in dbg: P.new_epoch()
        if "g" not in skip:
            if depth > 0:
                P.collective_wait()
            else:
                P.collectives("AllGather", xpairs, GROUPS)
        if "f" not in skip:
            stage_final(P, nc, xg4, Xcol, fg, out, S)
        P.barrier()
        print("fused nins", P.nins)
    return nc


_CACHE = {}


def kernel(**inp):
    inp = {k: np.asarray(v) for k, v in inp.items()}
    x = np.ascontiguousarray(inp["x"], dtype=np.float32)
    S = x.shape[1]
    if "fused" not in _CACHE:
        _CACHE["fused"] = build_fused(DEPTH, S)
    cst = host_consts()
    cores = list(range(8))
    L = DEPTH
    maps = []
    for c in cores:
        b, q = c // HG, c % HG
        es = slice(q * DQ, (q + 1) * DQ)
        sel, invc = host_poolc(q, S)
        cols = core_cols(q)
        m = {
            "xcol": np.ascontiguousarray(x[b][:, es]),
            "xfull": x[b].reshape(S, HG, DQ),
            "wc": np.ascontiguousarray(inp["w_in"][:, :, cols]),
            "g": np.ascontiguousarray(inp["norm_g"].reshape(L, KC, 128).transpose(0, 2, 1)),
            "prm": np.stack([host_prm(inp, l, q) for l in range(L)]),
            "lora": np.stack([host_lora(inp, l, q) for l in range(L)]),
            "pw": np.ascontiguousarray(inp["pool_w"][:, q]),
            "bf": np.ascontiguousarray(inp["fox_bf"][:, q * 4:(q + 1) * 4].reshape(L, 4, 1)),
            "wm": np.ascontiguousarray(np.concatenate(
                [inp["w_in"][:, :, OM + k * D + q * DQ:OM + k * D + (q + 1) * DQ] for k in range(4)], axis=2)),
            "wb": np.ascontiguousarray(inp["w_branch"][:, :, :, es]),
            "wo": np.ascontiguousarray(inp["w_out"][:, :, es]),
            "bm": np.ascontiguousarray(inp["b_merge"][:, :, es].reshape(L, 4, 4, 128).transpose(0, 3, 1, 2)),
            "fg": np.ascontiguousarray(inp["final_g"][es].reshape(1, DQ)),
            "cst": cst, "sel": sel, "invc": invc,
        }
        maps.append(m)
    res = run_bass_kernel_spmd(_CACHE["fused"], maps, core_ids=cores)
    out = np.empty_like(x)
    for c in cores:
        b, q = c // HG, c % HG
        out[b][:, q * DQ:(q + 1) * DQ] = np.asarray(res.results[c]["out"])
    return out
```

```python
import numpy as np
from contextlib import ExitStack
import concourse.bass as bass
import concourse.mybir as mybir
from concourse.bass_utils import run_bass_kernel_spmd

F32 = mybir.dt.float32
BF16 = mybir.dt.bfloat16
AF = mybir.ActivationFunctionType
ALU = mybir.AluOpType
AX = mybir.AxisListType

D = 2048
S = 8192
NB = 2
DEPTH = 4
W = 1024
NIN = 22672
KC = D // 128
HG = 4
CW = W // HG
EPS = 1e-6

_names = [("A_b", CW), ("A_c", CW), ("A_x", CW), ("A_g", CW),
          ("B_r", CW), ("B_k", CW), ("B_v", CW), ("B_lora", 128), ("B_g", CW),
          ("C_q", CW), ("C_k", CW), ("C_v", CW), ("C_g", CW),
          ("D_x", CW), ("D_g", CW), ("C_f", 4)]
UROW = {}
_o = 0
for _n, _s in _names:
    UROW[_n] = (_o, _s)
    _o += _s
NU = _o


def core_cols(hg):
    c = lambda base, n=CW: list(range(base + hg * n, base + (hg + 1) * n))
    oA = 0
    oB = 4 * W
    oBg = oB + 3 * W + 128
    oC = oBg + W
    oCf = oC + 3 * W
    oCg = oCf + 16
    oD = oCg + W
    oDg = oD + W
    cols = []
    cols += c(oA) + c(oA + W) + c(oA + 2 * W) + c(oA + 3 * W)
    cols += c(oB) + c(oB + W) + c(oB + 2 * W) + list(range(oB + 3 * W, oB + 3 * W + 128)) + c(oBg)
    cols += c(oC) + c(oC + W) + c(oC + 2 * W) + c(oCg)
    cols += c(oD) + c(oDg)
    cols += list(range(oCf + hg * 4, oCf + hg * 4 + 4))
    assert len(cols) == NU
    return np.array(cols)


OM = 4 * W + (3 * W + 128) + W + 3 * W + 16 + W + W + W
assert OM + 4 * D == NIN


class Buf:
    __slots__ = ("name", "w", "r", "psum", "ep")

    def __init__(self, name="", psum=False):
        self.name = name
        self.w = None
        self.r = {}
        self.psum = psum
        self.ep = 0


class Prog:
    NDMA = 32
    NSW = 8

    def __init__(self, nc, stack):
        self.nc = nc
        self.stack = stack
        self.eng = {"pe": nc.tensor, "act": nc.scalar, "dve": nc.vector,
                    "pool": nc.gpsimd, "sp": nc.sync}
        self.ep = 0
        self.nins = 0
        self.ccsem = None
        self.cccnt = 0
        self._fresh()

    def _fresh(self):
        nc, stack = self.nc, self.stack
        self.sem = {}
        self.cnt = {}
        for e in self.eng:
            self.sem[e] = stack.enter_context(nc.semaphore("s%d_%s" % (self.ep, e)))
            self.cnt[e] = 0
        self.dsem = [stack.enter_context(nc.semaphore("d%d_%d" % (self.ep, i))) for i in range(self.NDMA)]
        self.dcnt = [0] * self.NDMA
        self.dnext = 0
        self.swnext = 0
        self.seen = {e: {} for e in self.eng}

    def new_epoch(self):
        self.barrier()
        self.ep += 1
        self._fresh()

    def _chk(self, b):
        if b.ep != self.ep:
            b.ep = self.ep
            b.w = None
            b.r = {}

    def collective(self, kind, in_t, out_t, groups):
        self.collectives(kind, [(in_t, out_t)], groups)

    def collective_async(self, kind, in_t, out_t, groups, deps=()):
        if self.ccsem is None:
            self.ccsem = self.stack.enter_context(self.nc.semaphore("ccsem"))
        self._deps("pool", list(deps), [])
        ins = self.nc.gpsimd.collective_compute(kind, ALU.bypass, replica_groups=groups,
                                                ins=[in_t.ap().opt()], outs=[out_t.ap().opt()])
        ins.then_inc(self.ccsem)
        self.cccnt += 1
        self.nins += 1

    def collective_wait(self):
        if self.ccsem is not None:
            for e in self.eng.values():
                e.wait_ge(self.ccsem, self.cccnt)

    def collectives(self, kind, pairs, groups):
        self.barrier()
        if self.ccsem is None:
            self.ccsem = self.stack.enter_context(self.nc.semaphore("ccsem"))
        for in_t, out_t in pairs:
            ins = self.nc.gpsimd.collective_compute(kind, ALU.bypass, replica_groups=groups,
                                                    ins=[in_t.ap().opt()], outs=[out_t.ap().opt()])
            ins.then_inc(self.ccsem)
            self.cccnt += 1
            self.nins += 1
        for e in self.eng.values():
            e.wait_ge(self.ccsem, self.cccnt)

    def _semobj(self, key):
        return self.sem[key] if isinstance(key, str) else self.dsem[key]

    def _wait(self, e, key, val):
        if key == e and val > self.cnt[e]:
            return
        if self.seen[e].get(key, 0) >= val:
            return
        self.seen[e][key] = val
        self.eng[e].wait_ge(self._semobj(key), val)

    def _deps(self, e, reads, writes):
        for b in reads:
            self._chk(b)
        for b in writes:
            self._chk(b)
        for b in reads:
            if b.w is not None:
                self._wait(e, *b.w)
            if b.psum:
                for k, v in b.r.items():
                    if k != e:
                        self._wait(e, k, v)
        for b in writes:
            if b.w is not None:
                self._wait(e, *b.w)
            for k, v in b.r.items():
                self._wait(e, k, v)

    def _mark(self, key, val, reads, writes):
        for b in reads:
            if b.r.get(key, 0) < val:
                b.r[key] = val
        for b in writes:
            b.w = (key, val)
            b.r = {}

    def op(self, e, fn, reads=(), writes=(), sig=True):
        self._deps(e, reads, writes)
        ins = fn(self.eng[e])
        if sig:
            self.cnt[e] += 1
            ins.then_inc(self.sem[e], 1)
            self._mark(e, self.cnt[e], reads, writes)
        else:
            self._mark(e, self.cnt[e] + 1, reads, writes)
        self.nins += 1

    def dma(self, q, out, in_, reads=(), writes=(), **kw):
        if q == "pool":
            i = self.NDMA - self.NSW + self.swnext
            self.swnext = (self.swnext + 1) % self.NSW
        else:
            i = self.dnext
            self.dnext = (self.dnext + 1) % (self.NDMA - self.NSW)
        if self.dcnt[i] > 0:
            self._wait(q, i, self.dcnt[i])
        self._deps(q, reads, writes)
        ins = self.eng[q].dma_start(out=out, in_=in_, **kw)
        self.dcnt[i] += 16
        ins.then_inc(self.dsem[i], 16)
        self._mark(i, self.dcnt[i], reads, writes)
        self.nins += 1

    def barrier(self):
        for e in self.eng:
            for e2 in self.eng:
                if e2 != e and self.cnt[e2] > 0:
                    self._wait(e, e2, self.cnt[e2])
            for i in range(self.NDMA):
                if self.dcnt[i] > 0:
                    self._wait(e, i, self.dcnt[i])


_UID = [0]


def _uid():
    _UID[0] += 1
    return _UID[0]


class Ring:
    def __init__(self, nc, st, name, shape, dtype, n, psum=False):
        alloc = nc.psum_tensor if psum else nc.sbuf_tensor
        u = _uid()
        self.t = [st.enter_context(alloc("%s_%d_%d" % (name, u, i), shape, dtype)) for i in range(n)]
        self.b = [Buf("%s%d" % (name, i), psum) for i in range(n)]
        self.i = 0

    def next(self):
        i = self.i
        self.i = (self.i + 1) % len(self.t)
        return self.t[i], self.b[i]


def sb(nc, st, name, shape, dtype):
    return st.enter_context(nc.sbuf_tensor("%s_%d" % (name, _uid()), shape, dtype)), Buf(name)


def make_identity(P, nc, st, name="ident"):
    idf, bidf = sb(nc, st, name + "f", [128, 128], F32)
    idb, bidb = sb(nc, st, name + "b", [128, 128], BF16)
    P.op("pool", lambda e: e.memset(idf[:], 1.0), writes=[bidf])
    P.op("pool", lambda e: e.affine_select(out=idf[:], in_=idf[:], pattern=[[-1, 128]],
                                           compare_op=ALU.is_equal, fill=0.0, base=0,
                                           channel_multiplier=1), reads=[bidf], writes=[bidf])
    P.op("dve", lambda e: e.tensor_copy(out=idb[:], in_=idf[:]), reads=[bidf], writes=[bidb])
    return (idf, bidf), (idb, bidb)


def stage_norm_T(P, nc, x_ap, g_ap, HT, ntok, identb, gathered=False):
    idb, bidb = identb
    with ExitStack() as st:
        gs, bgs = sb(nc, st, "n_g", [128, KC], F32)
        P.dma("sp", gs[:], g_ap, writes=[bgs])
        xr = Ring(nc, st, "n_x", [128, D], F32, 3)
        hr = Ring(nc, st, "n_h", [128, D], BF16, 3)
        jr = Ring(nc, st, "n_j", [128, D], BF16, 2)
        sr = Ring(nc, st, "n_s", [128, 4], F32, 4)
        pr = Ring(nc, st, "n_ps", [128, KC, 128], BF16, 3, psum=True)
        hTr = Ring(nc, st, "n_hT", [128, KC, 512], BF16, 2)
        gbc = gs[:, :].unsqueeze(2).to_broadcast([128, KC, 128])
        hT_of = {}

        def tile_gen(tt):
            t4, q = divmod(tt, 4)
            if q == 0:
                hT_of[t4] = hTr.next()
            hT, bhT = hT_of[t4]
            xt, bx = xr.next()
            if gathered:
                P.dma("sp" if q % 2 == 0 else "act", xt[:, :].rearrange("p (r e) -> p r e", r=4),
                      x_ap[tt * 128:(tt + 1) * 128, :, :], writes=[bx])
            else:
                P.dma("sp" if q % 2 == 0 else "act", xt[:], x_ap[tt * 128:(tt + 1) * 128, :], writes=[bx])
            s, bs = sr.next()
            j, bj = jr.next()
            P.op("act", lambda e: e.activation(out=j[:], in_=xt[:], func=AF.Square,
                                               accum_out=s[:, 0:1]), reads=[bx], writes=[bj, bs])
            yield
            P.op("dve", lambda e: e.tensor_scalar(out=s[:, 1:2], in0=s[:, 0:1], scalar1=1.0 / D, scalar2=EPS,
                                                  op0=ALU.mult, op1=ALU.add), reads=[bs], writes=[bs])
            P.op("act", lambda e: e.activation(out=s[:, 1:2], in_=s[:, 1:2], func=AF.Sqrt),
                 reads=[bs], writes=[bs])
            P.op("dve", lambda e: e.reciprocal(out=s[:, 2:3], in_=s[:, 1:2]), reads=[bs], writes=[bs])
            yield
            h, bh = hr.next()
            P.op("dve", lambda e: e.tensor_scalar(out=h[:], in0=xt[:], scalar1=s[:, 2:3], scalar2=None,
                                                  op0=ALU.mult), reads=[bx, bs], writes=[bh])
            yield
            ps, bps = pr.next()
            for kc in range(KC):
                P.op("pe", lambda e: e.transpose(out=ps[:, kc, :], in_=h[:, kc * 128:(kc + 1) * 128],
                                                 identity=idb[:]), reads=[bh, bidb], writes=[bps], sig=(kc == KC - 1))
            yield
            P.op("dve", lambda e: e.tensor_tensor(out=hT[:, :, q * 128:(q + 1) * 128], in0=ps[:], in1=gbc,
                                                  op=ALU.mult), reads=[bps, bgs], writes=[bhT])
            if q == 3:
                P.dma("sp", HT[:, :, t4 * 512:(t4 + 1) * 512], hT[:], reads=[bhT])

        ntile = ntok // 128
        active, nxt, WIN = [], 0, 3
        while nxt < ntile or active:
            while len(active) < WIN and nxt < ntile:
                active.append(tile_gen(nxt))
                nxt += 1
            for g_ in list(active):
                try:
                    next(g_)
                except StopIteration:
                    active.remove(g_)
        P.barrier()


def stage_proj(P, nc, wc_ap, ncols, HT, ntok, U, group=8):
    nchunk = (ncols + 127) // 128
    with ExitStack() as st:
        wr = Ring(nc, st, "p_w", [128, KC, group * 128], BF16, 2)
        hr = Ring(nc, st, "p_h", [128, KC, 512], BF16, 2)
        pr = Ring(nc, st, "p_ps", [128, 512], F32, 6, psum=True)
        er = Ring(nc, st, "p_e", [128, 512], F32, 6)
        wv = wc_ap.rearrange("(kc p) c -> p kc c", p=128)
        ev = 0
        for g0 in range(0, nchunk, group):
            c0 = g0 * 128
            c1 = min(ncols, (g0 + group) * 128)
            wt, bw = wr.next()
            for kc in range(0, KC, 4):
                P.dma("pool", wt[:, kc:kc + 4, 0:c1 - c0], wv[:, kc:kc + 4, c0:c1], writes=[bw])
            for tt in range(ntok // 512):
                ht, bh = hr.next()
                P.dma("sp", ht[:], HT[:, :, tt * 512:(tt + 1) * 512], writes=[bh])
                for ch in range(g0, min(nchunk, g0 + group)):
                    m = min(128, ncols - ch * 128)
                    lc = (ch - g0) * 128
                    ps, bps = pr.next()
                    for kc in range(KC):
                        P.op("pe", lambda e: e.matmul(ps[0:m, :], lhsT=wt[:, kc, lc:lc + m], rhs=ht[:, kc, :],
                                                      start=(kc == 0), stop=(kc == KC - 1)),
                             reads=[bw, bh], writes=[bps], sig=(kc == KC - 1))
                    et, be = er.next()
                    if ev % 2 == 0:
                        P.op("act", lambda e: e.copy(out=et[0:m, :], in_=ps[0:m, :]), reads=[bps], writes=[be])
                    else:
                        P.op("dve", lambda e: e.tensor_copy(out=et[0:m, :], in_=ps[0:m, :]), reads=[bps], writes=[be])
                    ev += 1
                    P.dma("sp" if ev % 2 == 0 else "act", U[ch * 128:ch * 128 + m, tt * 512:(tt + 1) * 512], et[0:m, :],
                          reads=[be])
        P.barrier()


NCONST = 128 + 512 + 128 + 512 + 128
def host_consts():
    c = np.zeros((128, NCONST), np.float32)
    blk = (np.arange(128)[:, None] // 64) == (np.arange(128)[None, :] // 64)
    c[:, 0:128] = blk
    s = np.arange(128)[:, None]
    t = np.arange(128)[None, :]
    su = (blk & (s < t)).astype(np.float32)
    u = (blk & (s <= t)).astype(np.float32)
    c[:, 128:640] = np.concatenate([su, u, su, u], axis=1)
    c[:, 640:768] = (blk & (s > t)).astype(np.float32)
    c[:, 768:1280] = (np.arange(512)[None, :] % 64 != 0)
    c[:, 1280:1408] = (s <= t)
    return c


PRM = {"mu_r": 0, "mu_k": 1, "mu_v": 2, "w0": 3, "a0": 4, "k_k": 5, "k_a": 6, "ln_g": 7, "ln_b": 8,
       "r_k": 9, "mu_lora": 10, "cw0": 11, "cw1": 12, "cw2": 13, "pscale": 14, "omka": 15}
NPRM = 16


def host_prm(inp, l, hg):
    sl = slice(hg * CW, (hg + 1) * CW)
    p = np.zeros((CW, NPRM), np.float32)
    mu = inp["rwkv_mu"][l]
    p[:, 0] = mu[0:W][sl]
    p[:, 1] = mu[W:2 * W][sl]
    p[:, 2] = mu[2 * W:3 * W][sl]
    p[:, 3] = inp["rwkv_w0"][l][sl]
    p[:, 4] = inp["rwkv_a0"][l][sl]
    p[:, 5] = inp["rwkv_kk"][l][sl]
    p[:, 6] = inp["rwkv_ka"][l][sl]
    p[:, 7] = inp["rwkv_ln_g"][l][sl]
    p[:, 8] = inp["rwkv_ln_b"][l][sl]
    p[:, 9] = inp["rwkv_rk"][l].reshape(-1)[sl]
    p[0:128, 10] = mu[3 * W:3 * W + 128]
    p[:, 11] = inp["conv_w"][l][0][sl]
    p[:, 12] = inp["conv_w"][l][1][sl]
    p[:, 13] = inp["conv_w"][l][2][sl]
    p[:, 14] = inp["pool_scale"][l][sl]
    return np.ascontiguousarray(p.reshape(2, 128, NPRM).transpose(1, 0, 2))


def host_lora(inp, l, hg):
    sl = slice(hg * CW, (hg + 1) * CW)
    return np.ascontiguousarray(np.concatenate([inp["rwkv_w2"][l][:, sl], inp["rwkv_a2"][l][:, sl]], axis=0))


DBG = 0


def stage_rwkv(P, nc, U, prm_ap, lora_ap, cst, YS, ntok, identf, on_tile=None):
    idf, bidf = identf
    cs, bcs = cst
    r0 = {k: UROW[k][0] for k in UROW}
    NT = ntok // 512
    MUL, ADD, SUB = ALU.mult, ALU.add, ALU.subtract
    with ExitStack() as st:
        prm, bprm = sb(nc, st, "r_prm", [128, 2, NPRM], F32)
        P.dma("sp", prm[:], prm_ap, writes=[bprm])
        lw2, blw2 = sb(nc, st, "r_lora", [128, CW], F32)
        P.dma("sp", lw2[:], lora_ap, writes=[blw2])
        for c in range(2):
            P.op("dve", lambda e: e.tensor_scalar(out=prm[:, c, 15:16], in0=prm[:, c, 6:7], scalar1=-1.0, scalar2=1.0,
                                                  op0=MUL, op1=ADD), reads=[bprm], writes=[bprm])
        pc = lambda c, n: prm[:, c, PRM[n]:PRM[n] + 1]
        blk = cs[:, 0:128]
        mask4 = cs[:, 128:640]
        masksl = cs[:, 640:768]
        scanm = cs[:, 768:1280]
        psr = Ring(nc, st, "r_ps", [128, 512], F32, 4, psum=True)
        pqr = [Ring(nc, st, "r_pq%d" % i, [128, 512], F32, 1, psum=True) for i in range(2)]
        pyr = [Ring(nc, st, "r_py%d" % i, [128, 512], F32, 1, psum=True) for i in range(2)]
        ur = Ring(nc, st, "r_u", [128, 513], F32, 4)
        tr = Ring(nc, st, "r_t", [128, 512], F32, 6)
        names = ["xr", "xk", "xv", "lw", "al", "kk", "L", "G", "Gi", "aT", "rT", "bT", "kT", "bh", "kh", "bon", "y"]
        opr = {n: Ring(nc, st, "r_o_" + n, [128, 512], F32, 2) for n in names}
        lor = Ring(nc, st, "r_lo", [128, 512], F32, 2)
        MTr_ = [Ring(nc, st, "r_MT%d" % i, [128, 512], F32, 2) for i in range(2)]
        XYr_ = [Ring(nc, st, "r_XY%d" % i, [128, 256], F32, 3) for i in range(2)]
        Fr_ = [Ring(nc, st, "r_F%d" % i, [128, 128], F32, 3) for i in range(2)]
        TKr_ = [Ring(nc, st, "r_TK%d" % i, [128, 320], F32, 2) for i in range(2)]
        Er_ = [Ring(nc, st, "r_E%d" % i, [128, 128], F32, 3) for i in range(2)]
        PTr_ = [Ring(nc, st, "r_PT%d" % i, [128, 2, 64], F32, 2) for i in range(2)]
        Qr_ = [Ring(nc, st, "r_Q%d" % i, [128, 2, 64], F32, 2) for i in range(2)]
        ZTr_ = [Ring(nc, st, "r_ZT%d" % i, [128, 128], F32, 2) for i in range(2)]
        Hr = [Ring(nc, st, "r_H%d" % i, [128, 64], F32, 3) for i in range(4)]
        gr = Ring(nc, st, "r_g", [128, 512], F32, 2)
        obr = Ring(nc, st, "r_ob", [128, 512], BF16, 2)
        Hcur = [None] * 4

        def load_mix(row, c, mu_name, tt, dst, bdst):
            ut, bu = ur.next()
            t0 = tt * 512
            if tt == 0:
                P.op("pool", lambda e: e.memset(ut[:, 0:1], 0.0), writes=[bu])
                P.dma("sp", ut[:, 1:513], U[row + c * 128:row + (c + 1) * 128, 0:512], writes=[bu])
            else:
                P.dma("sp", ut[:, :], U[row + c * 128:row + (c + 1) * 128, t0 - 1:t0 + 512], writes=[bu])
            d, bd = tr.next()
            P.op("pool", lambda e: e.tensor_tensor(out=d[:], in0=ut[:, 0:512], in1=ut[:, 1:513], op=SUB),
                 reads=[bu], writes=[bd])
            P.op("dve", lambda e: e.scalar_tensor_tensor(out=dst[:], in0=d[:], scalar=pc(c, mu_name), in1=ut[:, 1:513],
                                                         op0=MUL, op1=ADD), reads=[bd, bu, bprm], writes=[bdst])

        LO = {}
        YSW = {}
        NEXT = [None]

        def item(tt, pr_):
            if pr_ == 0:
                lo, blo = lor.next()
                load_mix(r0["B_lora"], 0, "mu_lora", tt, lo, blo)
                P.op("act", lambda e: e.activation(out=lo[0:64, :], in_=lo[0:64, :], func=AF.Tanh), reads=[blo], writes=[blo])
                LO[tt] = (lo, blo)
                yield
            lo, blo = LO[tt]
            T = {}
            for n in names:
                T[n] = opr[n].next()
            xr_, bxr = T["xr"]; xk, bxk = T["xk"]; xv, bxv = T["xv"]
            load_mix(r0["B_r"], pr_, "mu_r", tt, xr_, bxr)
            yield
            load_mix(r0["B_k"], pr_, "mu_k", tt, xk, bxk)
            yield
            load_mix(r0["B_v"], pr_, "mu_v", tt, xv, bxv)
            yield
            lw, blw = T["lw"]; al, bal = T["al"]
            ps, bps = psr.next()
            P.op("pe", lambda e: e.matmul(ps[:], lhsT=lw2[0:64, pr_ * 128:(pr_ + 1) * 128], rhs=lo[0:64, :],
                                          start=True, stop=True), reads=[blw2, blo], writes=[bps])
            P.op("act", lambda e: e.activation(out=lw[:], in_=ps[:], func=AF.Sigmoid, bias=pc(pr_, "w0"), scale=1.0),
                 reads=[bps, bprm], writes=[blw])
            P.op("dve", lambda e: e.tensor_scalar(out=lw[:], in0=lw[:], scalar1=-0.6065306597126334, scalar2=None,
                                                  op0=MUL), reads=[blw], writes=[blw])
            ps, bps = psr.next()
            P.op("pe", lambda e: e.matmul(ps[:], lhsT=lw2[64:128, pr_ * 128:(pr_ + 1) * 128], rhs=lo[64:128, :],
                                          start=True, stop=True), reads=[blw2, blo], writes=[bps])
            P.op("act", lambda e: e.activation(out=al[:], in_=ps[:], func=AF.Sigmoid, bias=pc(pr_, "a0"), scale=1.0),
                 reads=[bps, bprm], writes=[bal])
            yield
            kk, bkk = T["kk"]
            P.op("pool", lambda e: e.tensor_scalar(out=kk[:], in0=xk[:], scalar1=pc(pr_, "k_k"), scalar2=None, op0=MUL),
                 reads=[bxk, bprm], writes=[bkk])
            sq, bsq = tr.next()
            P.op("pool", lambda e: e.tensor_tensor(out=sq[:], in0=kk[:], in1=kk[:], op=MUL), reads=[bkk], writes=[bsq])
            ps, bps = psr.next()
            P.op("pe", lambda e: e.matmul(ps[:], lhsT=blk, rhs=sq[:], start=True, stop=True), reads=[bcs, bsq], writes=[bps])
            rn, brn = tr.next()
            P.op("act", lambda e: e.activation(out=rn[:], in_=ps[:], func=AF.Sqrt), reads=[bps], writes=[brn])
            P.op("dve", lambda e: e.tensor_scalar(out=rn[:], in0=rn[:], scalar1=1e-12, scalar2=None, op0=ALU.max),
                 reads=[brn], writes=[brn])
            P.op("dve", lambda e: e.reciprocal(out=rn[:], in_=rn[:]), reads=[brn], writes=[brn])
            P.op("pool", lambda e: e.tensor_tensor(out=kk[:], in0=kk[:], in1=rn[:], op=MUL), reads=[bkk, brn], writes=[bkk])
            yield
            t1, bt1 = tr.next()
            P.op("dve", lambda e: e.tensor_scalar(out=t1[:], in0=al[:], scalar1=pc(pr_, "k_a"), scalar2=pc(pr_, "omka"),
                                                  op0=MUL, op1=ADD), reads=[bal, bprm], writes=[bt1])
            P.op("pool", lambda e: e.tensor_tensor(out=xk[:], in0=xk[:], in1=t1[:], op=MUL), reads=[bxk, bt1], writes=[bxk])
            yield
            L, bL = T["L"]; G, bG = T["G"]; Gi, bGi = T["Gi"]
            P.op("dve", lambda e: e.tensor_tensor_scan(out=L[:], data0=scanm, data1=lw[:], initial=0.0, op0=MUL, op1=ADD),
                 reads=[bcs, blw], writes=[bL])
            P.op("act", lambda e: e.activation(out=G[:], in_=L[:], func=AF.Exp), reads=[bL], writes=[bG])
            P.op("act", lambda e: e.activation(out=Gi[:], in_=L[:], func=AF.Exp, scale=-1.0), reads=[bL], writes=[bGi])
            gp, bgp = tr.next()
            P.op("pool", lambda e: e.tensor_tensor(out=gp[:], in0=L[:], in1=lw[:], op=SUB), reads=[bL, blw], writes=[bgp])
            P.op("act", lambda e: e.activation(out=gp[:], in_=gp[:], func=AF.Exp), reads=[bgp], writes=[bgp])
            yield
            aT, baT = T["aT"]; rT, brT = T["rT"]; bT, bbT = T["bT"]; kT, bkT = T["kT"]
            bh, bbh = T["bh"]; kh, bkh = T["kh"]
            P.op("dve", lambda e: e.scalar_tensor_tensor(out=aT[:], in0=kk[:], scalar=-1.0, in1=gp[:], op0=MUL, op1=MUL),
                 reads=[bkk, bgp], writes=[baT])
            P.op("pool", lambda e: e.tensor_tensor(out=bT[:], in0=kk[:], in1=al[:], op=MUL), reads=[bkk, bal], writes=[bbT])
            P.op("pool", lambda e: e.tensor_tensor(out=bT[:], in0=bT[:], in1=Gi[:], op=MUL), reads=[bbT, bGi], writes=[bbT])
            P.op("pool", lambda e: e.tensor_tensor(out=rT[:], in0=xr_[:], in1=G[:], op=MUL), reads=[bxr, bG], writes=[brT])
            P.op("dve", lambda e: e.tensor_tensor(out=kT[:], in0=xk[:], in1=Gi[:], op=MUL), reads=[bxk, bGi], writes=[bkT])
            yield
            gcb = G[:, :].rearrange("p (c t) -> p c t", t=64)[:, :, 63:64].to_broadcast([128, 8, 64])
            P.op("dve", lambda e: e.tensor_tensor(out=bh[:, :].rearrange("p (c t) -> p c t", t=64),
                                                  in0=bT[:, :].rearrange("p (c t) -> p c t", t=64), in1=gcb, op=MUL),
                 reads=[bbT, bG], writes=[bbh])
            P.op("pool", lambda e: e.tensor_tensor(out=kh[:, :].rearrange("p (c t) -> p c t", t=64),
                                                   in0=kT[:, :].rearrange("p (c t) -> p c t", t=64), in1=gcb, op=MUL),
                 reads=[bkT, bG], writes=[bkh])
            yield
            bon, bbon = T["bon"]
            t2, bt2 = tr.next()
            P.op("dve", lambda e: e.scalar_tensor_tensor(out=t2[:], in0=xr_[:], scalar=pc(pr_, "r_k"), in1=xk[:], op0=MUL, op1=MUL),
                 reads=[bxr, bxk, bprm], writes=[bt2])
            ps, bps = psr.next()
            P.op("pe", lambda e: e.matmul(ps[:], lhsT=blk, rhs=t2[:], start=True, stop=True), reads=[bcs, bt2], writes=[bps])
            P.op("dve", lambda e: e.tensor_tensor(out=bon[:], in0=ps[:], in1=xv[:], op=MUL), reads=[bps, bxv], writes=[bbon])

            yield "PREP_DONE"
            y, by = T["y"]
            if DBG == 1:
                P.op('pool', lambda e: e.memset(y[:], 0.0), writes=[by])
            def head_stream(hh):
                MTr, XYr, Fr, TKr, Er = MTr_[hh], XYr_[hh], Fr_[hh], TKr_[hh], Er_[hh]
                PTr, Qr, ZTr = PTr_[hh], Qr_[hh], ZTr_[hh]
                hd = pr_ * 2 + hh
                R = slice(hh * 64, hh * 64 + 64)
                if Hcur[hd] is None:
                    Hcur[hd] = Hr[hd].next()
                    P.op("pool", lambda e: e.memset(Hcur[hd][0][:], 0.0), writes=[Hcur[hd][1]])
                for cp in range(4):
                    Cc = slice(cp * 128, cp * 128 + 128)
                    MT, bMT = MTr.next()
                    ps, bps = psr.next()
                    for i, (lt, blt, rt, brt) in enumerate([(bT, bbT, aT, baT), (bT, bbT, rT, brT),
                                                            (kT, bkT, aT, baT), (kT, bkT, rT, brT)]):
                        P.op("pe", lambda e: e.matmul(ps[:, i * 128:(i + 1) * 128], lhsT=lt[R, Cc], rhs=rt[R, Cc],
                                                      start=True, stop=True), reads=[blt, brt], writes=[bps], sig=(i == 3))
                    P.op("dve", lambda e: e.tensor_tensor(out=MT[:], in0=ps[:], in1=mask4, op=MUL),
                         reads=[bps, bcs], writes=[bMT])
                    XY, bXY = XYr.next()
                    ps2, bps2 = psr.next()
                    P.op("pe", lambda e: e.matmul(ps2[:, 0:128], lhsT=aT[R, Cc], rhs=bT[R, Cc], start=True, stop=True),
                         reads=[baT, bbT], writes=[bps2])
                    P.op("dve", lambda e: e.tensor_tensor(out=XY[:, 128:256], in0=ps2[:, 0:128], in1=masksl, op=MUL),
                         reads=[bps2, bcs], writes=[bXY])
                    yield
                    if DBG == 2:
                        continue
                    TK, bTK = TKr.next()
                    ps3, bps3 = psr.next()
                    for i, (src, bsrc) in enumerate([(aT, baT), (xv, bxv), (bh, bbh), (kh, bkh)]):
                        P.op("pe", lambda e: e.transpose(out=ps3[:, 64 + i * 64:128 + i * 64], in_=src[R, Cc],
                                                         identity=idf[R, R]), reads=[bsrc, bidf], writes=[bps3], sig=(i == 3))
                    P.op("act", lambda e: e.copy(out=TK[:, 64:320], in_=ps3[:, 64:320]), reads=[bps3], writes=[bTK])
                    yield
                    if DBG == 3:
                        continue
                    ps4, bps4 = psr.next()
                    P.op("pe", lambda e: e.matmul(ps4[:, 0:64], lhsT=MT[:, 256:384], rhs=TK[:, 128:192], start=True, stop=True),
                         reads=[bMT, bTK], writes=[bps4])
                    P.op("dve", lambda e: e.tensor_copy(out=TK[:, 0:64], in_=ps4[:, 0:64]), reads=[bps4], writes=[bTK])
                    if DBG == 31:
                        continue
                    F, bF = Fr.next()
                    P.op("pool", lambda e: e.tensor_tensor(out=F[:], in0=MT[:, 0:128], in1=idf[:], op=ADD),
                         reads=[bMT, bidf], writes=[bF])
                    yield
                    Ecur, bEcur = TK[:, 0:128], bTK
                    Xc, bXc = MT[:, 0:128], bMT
                    Yc, bYc = XY[:, 128:256], bXY
                    for lev in range(6):
                        pe_, bpe = psr.next()
                        P.op("pe", lambda e: e.matmul(pe_[:, 0:128], lhsT=F[:], rhs=Ecur, start=True, stop=True),
                             reads=[bF, bEcur], writes=[bpe])
                        En, bEn = Er.next()
                        P.op("act", lambda e: e.copy(out=En[:], in_=pe_[:, 0:128]), reads=[bpe], writes=[bEn])
                        Ecur, bEcur = En[:], bEn
                        yield
                        if lev == 5 or (DBG == 32):
                            break
                        px, bpx = psr.next()
                        P.op("pe", lambda e: e.matmul(px[:, 0:128], lhsT=Yc, rhs=Xc, start=True, stop=True),
                             reads=[bYc, bXc], writes=[bpx], sig=(lev >= 4))
                        if lev < 4:
                            P.op("pe", lambda e: e.matmul(px[:, 128:256], lhsT=Xc, rhs=Yc, start=True, stop=True),
                                 reads=[bYc, bXc], writes=[bpx])
                        XYn, bXYn = XYr.next()
                        F, bF = Fr.next()
                        P.op("dve", lambda e: e.tensor_tensor(out=F[:], in0=px[:, 0:128], in1=idf[:], op=ADD),
                             reads=[bpx, bidf], writes=[bF])
                        if lev < 4:
                            P.op("act", lambda e: e.copy(out=XYn[:], in_=px[:, 0:256]), reads=[bpx], writes=[bXYn])
                        Xc, bXc = XYn[:, 0:128], bXYn
                        Yc, bYc = XYn[:, 128:256], bXYn
                        yield
                        if DBG == 33 + lev:
                            break
                    if DBG == 4:
                        continue
                    if DBG >= 32 and DBG < 40:
                        continue
                    U0 = Ecur[:, 0:64]
                    Wm = Ecur[:, 64:128]
                    PT, bPT = PTr.next()
                    Q, bQ = Qr.next()
                    pq, bpq = pqr[hh].next()
                    for c2 in range(2):
                        Rc = slice(c2 * 64, c2 * 64 + 64)
                        P.op("pe", lambda e: e.matmul(pq[R, c2 * 64:c2 * 64 + 64], lhsT=Wm[Rc, :], rhs=TK[Rc, 192:256],
                                                      start=True, stop=True), reads=[bEcur, bTK], writes=[bpq])
                        gC = G[R, cp * 128 + c2 * 64 + 63:cp * 128 + c2 * 64 + 64]
                        P.op("dve", lambda e: e.scalar_tensor_tensor(out=PT[R, c2, :], in0=idf[R, R], scalar=gC,
                                                                     in1=pq[R, c2 * 64:c2 * 64 + 64], op0=MUL, op1=ADD),
                             reads=[bidf, bG, bpq], writes=[bPT])
                        P.op("pe", lambda e: e.matmul(pq[R, 128 + c2 * 64:192 + c2 * 64], lhsT=TK[Rc, 192:256], rhs=U0[Rc, :],
                                                      start=True, stop=False), reads=[bEcur, bTK], writes=[bpq], sig=False)
                        P.op("pe", lambda e: e.matmul(pq[R, 128 + c2 * 64:192 + c2 * 64], lhsT=TK[Rc, 256:320], rhs=TK[Rc, 128:192],
                                                      start=False, stop=True), reads=[bTK], writes=[bpq])
                        yield
                    P.op("act", lambda e: e.copy(out=Q[R, :, :].rearrange("p a b -> p (a b)"), in_=pq[R, 128:256]),
                         reads=[bpq], writes=[bQ])
                    ZT, bZT = ZTr.next()
                    pz, bpz = psr.next()
                    P.op("pe", lambda e: e.matmul(pz[R, 0:128], lhsT=Wm, rhs=MT[:, 128:256], start=True, stop=True),
                         reads=[bEcur, bMT], writes=[bpz])
                    P.op("dve", lambda e: e.tensor_tensor(out=ZT[R, :], in0=pz[R, 0:128], in1=rT[R, Cc], op=ADD),
                         reads=[bpz, brT], writes=[bZT])
                    yield
                    if DBG == 5:
                        continue
                    py, bpy = pyr[hh].next()
                    P.op("pe", lambda e: e.matmul(py[R, 0:128], lhsT=U0, rhs=MT[:, 128:256], start=True, stop=False),
                         reads=[bEcur, bMT], writes=[bpy], sig=False)
                    P.op("pe", lambda e: e.matmul(py[R, 0:128], lhsT=TK[:, 128:192], rhs=MT[:, 384:512], start=False, stop=False),
                         reads=[bTK, bMT], writes=[bpy], sig=False)
                    for c2 in range(2):
                        H, bH = Hcur[hd]
                        P.op("pe", lambda e: e.matmul(py[R, c2 * 64:c2 * 64 + 64], lhsT=H[R, :], rhs=ZT[R, c2 * 64:c2 * 64 + 64],
                                                      start=False, stop=(c2 == 1)), reads=[bH, bZT], writes=[bpy], sig=(c2 == 1))
                        ph, bph = psr.next()
                        P.op("pe", lambda e: e.matmul(ph[R, 0:64], lhsT=PT[R, c2, :], rhs=H[R, :], start=True, stop=True),
                             reads=[bPT, bH], writes=[bph])
                        Hn, bHn = Hr[hd].next()
                        P.op("dve", lambda e: e.tensor_tensor(out=Hn[R, :], in0=ph[R, 0:64], in1=Q[R, c2, :], op=ADD),
                             reads=[bph, bQ], writes=[bHn])
                        Hcur[hd] = (Hn, bHn)
                        yield
                    P.op("act", lambda e: e.copy(out=y[R, Cc], in_=py[R, 0:128]), reads=[bpy], writes=[by])
                    yield

            gens = [head_stream(hh_) for hh_ in range(2 if DBG != 1 else 0)]
            bg = NEXT[0]
            while gens:
                for g_ in list(gens):
                    try:
                        next(g_)
                    except StopIteration:
                        gens.remove(g_)
                if bg is not None and next(bg) == "PREP_DONE":
                    bg = None
            while bg is not None:
                if next(bg) == "PREP_DONE":
                    bg = None
            ps, bps = psr.next()
            P.op("pe", lambda e: e.matmul(ps[:], lhsT=blk, rhs=y[:], start=True, stop=True), reads=[bcs, by], writes=[bps])
            yc, byc = tr.next()
            P.op("dve", lambda e: e.scalar_tensor_tensor(out=yc[:], in0=ps[:], scalar=-1.0 / 64, in1=y[:], op0=MUL, op1=ADD),
                 reads=[bps, by], writes=[byc])
            sq, bsq = tr.next()
            P.op("pool", lambda e: e.tensor_tensor(out=sq[:], in0=yc[:], in1=yc[:], op=MUL), reads=[byc], writes=[bsq])
            ps, bps = psr.next()
            P.op("pe", lambda e: e.matmul(ps[:], lhsT=blk, rhs=sq[:], start=True, stop=True), reads=[bcs, bsq], writes=[bps])
            rs, brs = tr.next()
            P.op("dve", lambda e: e.tensor_scalar(out=rs[:], in0=ps[:], scalar1=1.0 / 64, scalar2=64e-5, op0=MUL, op1=ADD),
                 reads=[bps], writes=[brs])
            P.op("act", lambda e: e.activation(out=rs[:], in_=rs[:], func=AF.Sqrt), reads=[brs], writes=[brs])
            P.op("dve", lambda e: e.reciprocal(out=rs[:], in_=rs[:]), reads=[brs], writes=[brs])
            P.op("pool", lambda e: e.tensor_tensor(out=yc[:], in0=yc[:], in1=rs[:], op=MUL), reads=[byc, brs], writes=[byc])
            P.op("dve", lambda e: e.tensor_scalar(out=yc[:], in0=yc[:], scalar1=pc(pr_, "ln_g"), scalar2=pc(pr_, "ln_b"),
                                                  op0=MUL, op1=ADD), reads=[byc, bprm], writes=[byc])
            P.op("pool", lambda e: e.tensor_tensor(out=yc[:], in0=yc[:], in1=bon[:], op=ADD), reads=[byc, bbon], writes=[byc])
            gt, bgt = gr.next()
            P.dma("act", gt[:], U[r0["B_g"] + pr_ * 128:r0["B_g"] + (pr_ + 1) * 128, tt * 512:(tt + 1) * 512], writes=[bgt])
            P.op("act", lambda e: e.activation(out=gt[:], in_=gt[:], func=AF.Silu), reads=[bgt], writes=[bgt])
            ob, bob = obr.next()
            P.op("dve", lambda e: e.tensor_tensor(out=ob[:], in0=yc[:], in1=gt[:], op=MUL), reads=[byc, bgt], writes=[bob])
            bw_ = Buf("ysw")
            P.dma("sp", YS[CW + pr_ * 128:CW + (pr_ + 1) * 128, tt * 512:(tt + 1) * 512], ob[:], reads=[bob], writes=[bw_])
            YSW.setdefault(tt, []).append(bw_)
            if on_tile is not None and pr_ == 1:
                on_tile(tt, YSW[tt])

        items = [item(tt_, p_) for tt_ in range(NT) for p_ in range(2)]
        while next(items[0]) != "PREP_DONE":
            pass
        for i_, it_ in enumerate(items):
            NEXT[0] = items[i_ + 1] if i_ + 1 < len(items) else None
            for _ in it_:
                pass
        P.barrier()


def stage_conv(P, nc, U, prm_ap, YS, ntok):
    MUL, ADD = ALU.mult, ALU.add
    TW = 2048 if ntok >= 2048 else ntok
    with ExitStack() as st:
        prm, bprm = sb(nc, st, "c_prm", [128, 2, NPRM], F32)
        P.dma("sp", prm[:], prm_ap, writes=[bprm])
        cr = Ring(nc, st, "c_c", [128, TW + 2], F32, 3)
        xr = Ring(nc, st, "c_x", [128, TW + 2], F32, 3)
        br = Ring(nc, st, "c_b", [128, TW], F32, 3)
        gr = Ring(nc, st, "c_g", [128, TW], F32, 3)
        zr = Ring(nc, st, "c_z", [128, TW], F32, 3)
        obr = Ring(nc, st, "c_o", [128, TW], BF16, 3)
        for c in range(2):
            for tt in range(ntok // TW):
                t0 = tt * TW
                ct, bc = cr.next(); xt, bx = xr.next(); bt, bb = br.next(); gt, bg = gr.next()
                rows = lambda n: slice(UROW[n][0] + c * 128, UROW[n][0] + (c + 1) * 128)
                if tt == 0:
                    P.op("pool", lambda e: e.memset(ct[:, 0:2], 0.0), writes=[bc])
                    P.op("pool", lambda e: e.memset(xt[:, 0:2], 0.0), writes=[bx])
                    P.dma("sp", ct[:, 2:], U[rows("A_c"), 0:TW], writes=[bc])
                    P.dma("act", xt[:, 2:], U[rows("A_x"), 0:TW], writes=[bx])
                else:
                    P.dma("sp", ct[:, :], U[rows("A_c"), t0 - 2:t0 + TW], writes=[bc])
                    P.dma("act", xt[:, :], U[rows("A_x"), t0 - 2:t0 + TW], writes=[bx])
                P.dma("sp", bt[:], U[rows("A_b"), t0:t0 + TW], writes=[bb])
                P.dma("act", gt[:], U[rows("A_g"), t0:t0 + TW], writes=[bg])
                P.op("dve", lambda e: e.tensor_tensor(out=ct[:], in0=ct[:], in1=xt[:], op=MUL), reads=[bc, bx], writes=[bc])
                z, bz = zr.next()
                P.op("dve", lambda e: e.tensor_scalar(out=z[:], in0=ct[:, 0:TW], scalar1=prm[:, c, 11:12], scalar2=None, op0=MUL),
                     reads=[bc, bprm], writes=[bz])
                P.op("dve", lambda e: e.scalar_tensor_tensor(out=z[:], in0=ct[:, 1:TW + 1], scalar=prm[:, c, 12:13], in1=z[:], op0=MUL, op1=ADD),
                     reads=[bc, bprm, bz], writes=[bz])
                P.op("dve", lambda e: e.scalar_tensor_tensor(out=z[:], in0=ct[:, 2:TW + 2], scalar=prm[:, c, 13:14], in1=z[:], op0=MUL, op1=ADD),
                     reads=[bc, bprm, bz], writes=[bz])
                P.op("act", lambda e: e.activation(out=gt[:], in_=gt[:], func=AF.Silu), reads=[bg], writes=[bg])
                P.op("pool", lambda e: e.tensor_tensor(out=z[:], in0=z[:], in1=bt[:], op=MUL), reads=[bz, bb], writes=[bz])
                ob, bob = obr.next()
                P.op("pool", lambda e: e.tensor_tensor(out=ob[:], in0=z[:], in1=gt[:], op=MUL), reads=[bz, bg], writes=[bob])
                for o5 in range(0, TW, 512):
                    P.dma("sp", YS[c * 128:(c + 1) * 128, t0 + o5:t0 + o5 + 512], ob[:, o5:o5 + 512], reads=[bob])
        P.barrier()


def host_poolc(hg, ntok):
    win = (2, 4, 8, 16)[hg]
    sel = np.zeros((128, 4), np.float32)
    sel[:, hg] = 1.0
    invc = (1.0 / np.minimum(np.arange(ntok) + 1, win)).astype(np.float32)[None, :]
    return sel, invc


def stage_pool(P, nc, U, prm_ap, sel_ap, invc_ap, pw_ap, YS, ntok):
    MUL, ADD, SUB = ALU.mult, ALU.add, ALU.subtract
    TW = 2048 if ntok >= 2048 else ntok
    H = 16
    with ExitStack() as st:
        prm, bprm = sb(nc, st, "l_prm", [128, 2, NPRM], F32)
        P.dma("sp", prm[:], prm_ap, writes=[bprm])
        sel, bsel = sb(nc, st, "l_sel", [128, 4], F32)
        P.dma("sp", sel[:], sel_ap, writes=[bsel])
        pw, bpw = sb(nc, st, "l_pw", [128, 2, CW], BF16)
        P.dma("pool", pw[:], pw_ap.rearrange("(cc p) e -> p cc e", p=128), writes=[bpw])
        ivr = Ring(nc, st, "l_iv", [128, TW], F32, 1)
        xr = Ring(nc, st, "l_x", [128, TW + H], F32, 3)
        sr = Ring(nc, st, "l_s", [128, TW + H], F32, 4)
        cr = Ring(nc, st, "l_cmb", [128, TW], F32, 2)
        pbr = Ring(nc, st, "l_pb", [128, 2, TW], BF16, 1)
        gr = Ring(nc, st, "l_g", [128, 512], F32, 2)
        obr = Ring(nc, st, "l_o", [128, 512], BF16, 2)
        psr = Ring(nc, st, "l_ps", [128, 512], F32, 2, psum=True)
        for tt in range(ntok // TW):
            t0 = tt * TW
            iv, biv = ivr.next()
            P.dma("sp", iv[:], invc_ap[0:1, t0:t0 + TW].partition_broadcast(128), writes=[biv])
            pb, bpb = pbr.next()
            for c in range(2):
                rows = slice(UROW["D_x"][0] + c * 128, UROW["D_x"][0] + (c + 1) * 128)
                xt, bx = xr.next()
                if tt == 0:
                    P.op("pool", lambda e: e.memset(xt[:, 0:H], 0.0), writes=[bx])
                    P.dma("sp", xt[:, H:], U[rows, 0:TW], writes=[bx])
                else:
                    P.dma("sp", xt[:, :], U[rows, t0 - H:t0 + TW], writes=[bx])
                prev, bprev = xt, bx
                cmb, bcmb = cr.next()
                sh = 1
                for i in range(4):
                    s, bs = sr.next()
                    lo = 2 * sh - 1
                    P.op("pool" if i % 2 else "dve", lambda e: e.tensor_tensor(out=s[:, lo:], in0=prev[:, lo:], in1=prev[:, lo - sh:TW + H - sh], op=ADD),
                         reads=[bprev], writes=[bs])
                    if i == 0:
                        P.op("dve", lambda e: e.tensor_scalar(out=cmb[:], in0=s[:, H:], scalar1=sel[:, 0:1], scalar2=None, op0=MUL),
                             reads=[bs, bsel], writes=[bcmb])
                    else:
                        P.op("dve", lambda e: e.scalar_tensor_tensor(out=cmb[:], in0=s[:, H:], scalar=sel[:, i:i + 1], in1=cmb[:], op0=MUL, op1=ADD),
                             reads=[bs, bsel, bcmb], writes=[bcmb])
                    prev, bprev = s, bs
                    sh *= 2
                P.op("pool", lambda e: e.tensor_tensor(out=cmb[:], in0=cmb[:], in1=iv[:], op=MUL), reads=[bcmb, biv], writes=[bcmb])
                P.op("dve", lambda e: e.tensor_tensor(out=pb[:, c, :], in0=cmb[:], in1=xt[:, H:], op=SUB), reads=[bcmb, bx], writes=[bpb])
            for t5 in range(TW // 512):
                for ec in range(2):
                    ps, bps = psr.next()
                    for cc in range(2):
                        P.op("pe", lambda e: e.matmul(ps[:], lhsT=pw[:, cc, ec * 128:(ec + 1) * 128], rhs=pb[:, cc, t5 * 512:(t5 + 1) * 512],
                                                      start=(cc == 0), stop=(cc == 1)), reads=[bpw, bpb], writes=[bps], sig=(cc == 1))
                    gt, bg = gr.next()
                    grow = slice(UROW["D_g"][0] + ec * 128, UROW["D_g"][0] + (ec + 1) * 128)
                    P.dma("act", gt[:], U[grow, t0 + t5 * 512:t0 + (t5 + 1) * 512], writes=[bg])
                    P.op("act", lambda e: e.activation(out=gt[:], in_=gt[:], func=AF.Silu), reads=[bg], writes=[bg])
                    ob, bob = obr.next()
                    P.op("dve", lambda e: e.scalar_tensor_tensor(out=ob[:], in0=ps[:], scalar=prm[:, ec, 14:15], in1=gt[:], op0=MUL, op1=MUL),
                         reads=[bps, bprm, bg], writes=[bob])
                    P.dma("sp", YS[3 * CW + ec * 128:3 * CW + (ec + 1) * 128, t0 + t5 * 512:t0 + (t5 + 1) * 512], ob[:], reads=[bob])
        P.barrier()


def stage_fox(P, nc, U, bf_ap, cst, YS, ntok, identf, FC):
    MUL, ADD, SUB = ALU.mult, ALU.add, ALU.subtract
    idf, bidf = identf
    cs, bcs = cst
    NQ = ntok // 512
    NK = ntok // 128
    LW = 2048 if ntok >= 2048 else ntok
    with ExitStack() as st:
        SEG = 2048 if ntok >= 2048 else ntok
        with ExitStack() as st0:
            f, bfb = sb(nc, st0, "f_f", [4, SEG], F32)
            one4, bone4 = sb(nc, st0, "f_one", [4, SEG], F32)
            bft, bbft = sb(nc, st0, "f_bf", [4, 2], F32)
            carry, bcar = sb(nc, st0, "f_car", [4, 2], F32)
            parts, bparts = sb(nc, st0, "f_parts", [4, 6, SEG], BF16)
            r1, br1 = sb(nc, st0, "f_r1", [4, SEG], F32)
            P.dma("sp", bft[:, 0:1], bf_ap, writes=[bbft])
            P.op("dve", lambda e: e.tensor_scalar(out=bft[:, 1:2], in0=bft[:, 0:1], scalar1=-1.0, scalar2=None, op0=MUL), reads=[bbft], writes=[bbft])
            P.op("pool", lambda e: e.memset(one4[:], 1.0), writes=[bone4])
            P.op("pool", lambda e: e.memset(carry[:], 0.0), writes=[bcar])
            for s0 in range(0, ntok, SEG):
                P.dma("sp", f[:], U[UROW["C_f"][0]:UROW["C_f"][0] + 4, s0:s0 + SEG], writes=[bfb])
                P.op("act", lambda e: e.activation(out=f[:], in_=f[:], func=AF.Exp, bias=bft[:, 1:2], scale=-1.0), reads=[bfb, bbft], writes=[bfb])
                P.op("dve", lambda e: e.tensor_scalar(out=f[:], in0=f[:], scalar1=1.0, scalar2=None, op0=ADD), reads=[bfb], writes=[bfb])
                P.op("act", lambda e: e.activation(out=f[:], in_=f[:], func=AF.Ln), reads=[bfb], writes=[bfb])
                P.op("dve", lambda e: e.tensor_tensor_scan(out=r1[:], data0=one4[:], data1=f[:], initial=carry[:, 0:1], op0=MUL, op1=ADD),
                     reads=[bone4, bfb, bcar], writes=[br1])
                P.op("dve", lambda e: e.tensor_copy(out=carry[:, 0:1], in_=r1[:, SEG - 1:SEG]), reads=[br1], writes=[bcar])
                P.op("dve", lambda e: e.tensor_scalar(out=r1[:], in0=r1[:], scalar1=8.0, scalar2=None, op0=MUL), reads=[br1], writes=[br1])
                for i in range(3):
                    P.op("dve", lambda e: e.tensor_copy(out=parts[:, i, :], in_=r1[:]), reads=[br1], writes=[bparts])
                    P.op("dve", lambda e: e.tensor_scalar(out=parts[:, 3 + i, :], in0=parts[:, i, :], scalar1=-1.0, scalar2=None, op0=MUL),
                         reads=[bparts], writes=[bparts])
                    if i < 2:
                        P.op("dve", lambda e: e.tensor_tensor(out=r1[:], in0=r1[:], in1=parts[:, i, :], op=SUB), reads=[br1, bparts], writes=[br1])
                P.dma("sp", FC[:, :, s0:s0 + SEG], parts[:], reads=[bparts])
            P.barrier()
        maskb, bmaskb = sb(nc, st, "f_mask", [128, 128], BF16)
        P.op("dve", lambda e: e.tensor_copy(out=maskb[:], in_=cs[:, 1280:1408]), reads=[bcs], writes=[bmaskb])
        onesf, bonesf = sb(nc, st, "f_ones", [128, 64], F32)
        P.op("pool", lambda e: e.memset(onesf[:], 1.0), writes=[bonesf])
        qa, bqa = sb(nc, st, "f_qa", [70, ntok], BF16)
        ka, bka = sb(nc, st, "f_ka", [70, ntok], BF16)
        va, bva = sb(nc, st, "f_va", [128, NK, 65], BF16)
        ldr = Ring(nc, st, "f_ld", [64, LW], F32, 2)
        psr = Ring(nc, st, "f_ps", [128, 512], F32, 6, psum=True)
        por = Ring(nc, st, "f_po", [128, 512], F32, 2, psum=True)
        ptr_ = Ring(nc, st, "f_pt", [128, 512], BF16, 5)
        rdr = Ring(nc, st, "f_rd", [128, 512], F32, 2)
        osr = Ring(nc, st, "f_os", [64, 512], F32, 2)
        gr = Ring(nc, st, "f_g", [64, 512], F32, 2)
        obr = Ring(nc, st, "f_ob", [64, 512], BF16, 2)
        for h in range(4):
            qrow = UROW["C_q"][0] + h * 64
            krow = UROW["C_k"][0] + h * 64
            vrow = UROW["C_v"][0] + h * 64
            P.op("pool", lambda e: e.memset(qa[64:70, :], 1.0), writes=[bqa])
            P.op("pool", lambda e: e.memset(ka[64:70, :], 1.0), writes=[bka])
            P.op("pool", lambda e: e.memset(va[:, :, 64:65], 1.0), writes=[bva])
            for i in range(3):
                P.dma("sp", qa[64 + i:65 + i, :], FC[h:h + 1, 3 + i, :], writes=[bqa])
                P.dma("sp", ka[67 + i:68 + i, :], FC[h:h + 1, i, :], writes=[bka])
            for l0 in range(0, ntok, LW):
                for (row, dst, bdst, eng) in ((qrow, qa, bqa, "act"), (krow, ka, bka, "dve")):
                    ld, bld = ldr.next()
                    P.dma("sp", ld[:], U[row:row + 64, l0:l0 + LW], writes=[bld])
                    if eng == "act":
                        P.op("act", lambda e: e.copy(out=dst[0:64, l0:l0 + LW], in_=ld[:]), reads=[bld], writes=[bdst])
                    else:
                        P.op("dve", lambda e: e.tensor_copy(out=dst[0:64, l0:l0 + LW], in_=ld[:]), reads=[bld], writes=[bdst])
                ld, bld = ldr.next()
                P.dma("act", ld[:], U[vrow:vrow + 64, l0:l0 + LW], writes=[bld])
                for j8 in range(LW // 1024):
                    ps, bps = psr.next()
                    for j in range(8):
                        P.op("pe", lambda e: e.transpose(out=ps[:, j * 64:(j + 1) * 64], in_=ld[:, j8 * 1024 + j * 128:j8 * 1024 + (j + 1) * 128],
                                                         identity=idf[0:64, 0:64]), reads=[bld, bidf], writes=[bps], sig=(j == 7))
                    jb = l0 // 128 + j8 * 8
                    P.op("dve", lambda e: e.tensor_copy(out=va[:, jb:jb + 8, 0:64], in_=ps[:, :].rearrange("p (j d) -> p j d", d=64)),
                         reads=[bps], writes=[bva])
            for qc in range(NQ):
                nkt = 4 * (qc + 1)
                po, bpo = por.next()
                LA = 4

                def emit_st(j):
                    d = j - 4 * qc
                    c0 = max(0, d) * 128
                    ps, bps = psr.next()
                    P.op("pe", lambda e: e.matmul(ps[:, c0:512], lhsT=ka[0:70, j * 128:(j + 1) * 128], rhs=qa[0:70, qc * 512 + c0:(qc + 1) * 512],
                                                  start=True, stop=True), reads=[bka, bqa], writes=[bps])
                    return ps, bps, c0, d

                pend = {}
                for j in range(min(LA, nkt)):
                    pend[j] = emit_st(j)
                for j in range(nkt):
                    if j + LA < nkt:
                        pend[j + LA] = emit_st(j + LA)
                    ps, bps, c0, d = pend.pop(j)
                    pt, bpt = ptr_.next()
                    P.op("act", lambda e: e.activation(out=pt[:, c0:512], in_=ps[:, c0:512], func=AF.Exp, scale=0.125), reads=[bps], writes=[bpt])
                    if d >= 0:
                        P.op("pool", lambda e: e.tensor_tensor(out=pt[:, c0:c0 + 128], in0=pt[:, c0:c0 + 128], in1=maskb[:], op=MUL),
                             reads=[bpt, bmaskb], writes=[bpt])
                    P.op("pe", lambda e: e.matmul(po[0:65, c0:512], lhsT=va[:, j, 0:65], rhs=pt[:, c0:512], start=(j == 0), stop=(j == nkt - 1)),
                         reads=[bva, bpt], writes=[bpo], sig=(j == nkt - 1))
                rd, brd = rdr.next()
                P.op("dve", lambda e: e.reciprocal(out=rd[64:65, :], in_=po[64:65, :]), reads=[bpo], writes=[brd])
                pb, bpb = psr.next()
                P.op("pe", lambda e: e.matmul(pb[0:64, :], lhsT=onesf[64:65, 0:64], rhs=rd[64:65, :], start=True, stop=True),
                     reads=[bonesf, brd], writes=[bpb])
                os_, bos = osr.next()
                P.op("act", lambda e: e.copy(out=os_[:], in_=po[0:64, :]), reads=[bpo], writes=[bos])
                gt, bg = gr.next()
                grow = UROW["C_g"][0] + h * 64
                P.dma("sp", gt[:], U[grow:grow + 64, qc * 512:(qc + 1) * 512], writes=[bg])
                P.op("act", lambda e: e.activation(out=gt[:], in_=gt[:], func=AF.Silu), reads=[bg], writes=[bg])
                P.op("dve", lambda e: e.tensor_tensor(out=os_[:], in0=os_[:], in1=pb[0:64, :], op=MUL), reads=[bos, bpb], writes=[bos])
                ob, bob = obr.next()
                P.op("pool", lambda e: e.tensor_tensor(out=ob[:], in0=os_[:], in1=gt[:], op=MUL), reads=[bos, bg], writes=[bob])
                P.dma("sp", YS[2 * CW + h * 64:2 * CW + (h + 1) * 64, qc * 512:(qc + 1) * 512], ob[:], reads=[bob])
        P.barrier()


def stage_back(P, nc, x_ap, HT, YSg, wm_ap, wb_ap, wo_ap, bm_ap, xo_ap, ntok, fg_ap=None):
    MUL, ADD = ALU.mult, ALU.add
    with ExitStack() as st:
        bm, bbm = sb(nc, st, "b_bm", [128, 4, KC], F32)
        P.dma("sp", bm[:], bm_ap, writes=[bbm])
        wo, bwo = sb(nc, st, "b_wo", [128, KC, D], BF16)
        wov = wo_ap.rearrange("(kc p) e -> p kc e", p=128)
        for kc in range(0, KC, 2):
            P.dma("pool", wo[:, kc:kc + 2, :], wov[:, kc:kc + 2, :], writes=[bwo])
        if fg_ap is not None:
            fg, bfg = sb(nc, st, "b_fg", [128, D], F32)
            P.dma("sp", fg[:], fg_ap[0:1, :].partition_broadcast(128), writes=[bfg])
        hr = Ring(nc, st, "b_h", [128, KC, 512], BF16, 1)
        yr = Ring(nc, st, "b_y", [128, 32, 512], BF16, 1)
        mr = Ring(nc, st, "b_m", [128, KC, 512], BF16, 1)
        wmr = Ring(nc, st, "b_wm", [128, KC, 128], BF16, 3)
        wbr = Ring(nc, st, "b_wb", [128, 8, 128], BF16, 3)
        gr = Ring(nc, st, "b_g", [128, 512], F32, 2)
        ar = Ring(nc, st, "b_a", [128, 512], F32, 2)
        tr = Ring(nc, st, "b_t", [128, 512], F32, 2)
        xr = Ring(nc, st, "b_x", [128, D], F32, 2)
        sr = Ring(nc, st, "b_s", [128, 4], F32, 2)
        jr = Ring(nc, st, "b_j", [128, D], BF16, 1)
        psm = Ring(nc, st, "b_psm", [128, 512], F32, 2, psum=True)
        psp = Ring(nc, st, "b_psp", [128, 512], F32, 2, psum=True)
        pso = Ring(nc, st, "b_pso", [128, 512], F32, 2, psum=True)
        wmv = wm_ap.rearrange("(kc p) c -> p kc c", p=128)
        ysv = YSg.rearrange("(j p) t -> p j t", p=128)
        for tt in range(ntok // 512):
            ht, bh = hr.next()
            P.dma("sp", ht[:], HT[:, :, tt * 512:(tt + 1) * 512], writes=[bh])
            ys, bys = yr.next()
            for j0 in range(0, 32, 8):
                P.dma("act", ys[:, j0:j0 + 8, :], ysv[:, j0:j0 + 8, tt * 512:(tt + 1) * 512], writes=[bys])
            mg, bmg = mr.next()
            for dc in range(KC):
                acc, bacc = ar.next()
                for k in range(4):
                    wmt, bwm = wmr.next()
                    c0 = k * D + dc * 128
                    P.dma("pool", wmt[:], wmv[:, :, c0:c0 + 128], writes=[bwm])
                    wbt, bwb = wbr.next()
                    P.dma("pool", wbt[:], wb_ap[k, :, dc * 128:(dc + 1) * 128].rearrange("(cc p) d -> p cc d", p=128), writes=[bwb])
                    pm, bpm = psm.next()
                    for kc in range(KC):
                        P.op("pe", lambda e: e.matmul(pm[:], lhsT=wmt[:, kc, :], rhs=ht[:, kc, :], start=(kc == 0), stop=(kc == KC - 1)),
                             reads=[bwm, bh], writes=[bpm])
                    gt, bg = gr.next()
                    P.op("act", lambda e: e.activation(out=gt[:], in_=pm[:], func=AF.Sigmoid, bias=bm[:, k, dc:dc + 1], scale=1.0),
                         reads=[bpm, bbm], writes=[bg])
                    pp, bpp = psp.next()
                    for cc in range(8):
                        P.op("pe", lambda e: e.matmul(pp[:], lhsT=wbt[:, cc, :], rhs=ys[:, k * 8 + cc, :], start=(cc == 0), stop=(cc == 7)),
                             reads=[bwb, bys], writes=[bpp])
                    if k == 0:
                        P.op("dve", lambda e: e.tensor_tensor(out=acc[:], in0=pp[:], in1=gt[:], op=MUL), reads=[bpp, bg], writes=[bacc])
                    else:
                        t_, bt_ = tr.next()
                        P.op("dve", lambda e: e.tensor_tensor(out=t_[:], in0=pp[:], in1=gt[:], op=MUL), reads=[bpp, bg], writes=[bt_])
                        if k < 3:
                            P.op("pool", lambda e: e.tensor_tensor(out=acc[:], in0=acc[:], in1=t_[:], op=ADD), reads=[bacc, bt_], writes=[bacc])
                        else:
                            P.op("pool", lambda e: e.tensor_tensor(out=mg[:, dc, :], in0=acc[:], in1=t_[:], op=ADD), reads=[bacc, bt_], writes=[bmg])
            for ts in range(4):
                xt, bx = xr.next()
                r0_ = tt * 512 + ts * 128
                P.dma("sp", xt[:], x_ap[r0_:r0_ + 128, :], writes=[bx])
                for ec in range(4):
                    po, bpo = pso.next()
                    for dc in range(KC):
                        P.op("pe", lambda e: e.matmul(po[:], lhsT=mg[:, dc, ts * 128:(ts + 1) * 128], rhs=wo[:, dc, ec * 512:(ec + 1) * 512],
                                                      start=(dc == 0), stop=(dc == KC - 1)), reads=[bmg, bwo], writes=[bpo], sig=(dc == KC - 1))
                    P.op("dve", lambda e: e.tensor_tensor(out=xt[:, ec * 512:(ec + 1) * 512], in0=po[:], in1=xt[:, ec * 512:(ec + 1) * 512], op=ADD),
                         reads=[bpo, bx], writes=[bx])
                if fg_ap is not None:
                    s, bs = sr.next()
                    j, bj = jr.next()
                    P.op("act", lambda e: e.activation(out=j[:], in_=xt[:], func=AF.Square, accum_out=s[:, 0:1]), reads=[bx], writes=[bj, bs])
                    P.op("dve", lambda e: e.tensor_scalar(out=s[:, 1:2], in0=s[:, 0:1], scalar1=1.0 / D, scalar2=EPS, op0=MUL, op1=ADD),
                         reads=[bs], writes=[bs])
                    P.op("act", lambda e: e.activation(out=s[:, 1:2], in_=s[:, 1:2], func=AF.Sqrt), reads=[bs], writes=[bs])
                    P.op("dve", lambda e: e.reciprocal(out=s[:, 2:3], in_=s[:, 1:2]), reads=[bs], writes=[bs])
                    P.op("dve", lambda e: e.scalar_tensor_tensor(out=xt[:], in0=xt[:], scalar=s[:, 2:3], in1=fg[:], op0=MUL, op1=MUL),
                         reads=[bx, bs, bfg], writes=[bx])
                P.dma("sp", xo_ap[r0_:r0_ + 128, :], xt[:], reads=[bx])
        P.barrier()


DQ = D // HG


def stage_back_a(P, nc, HT, YSall, wm_ap, wb_ap, bm_ap, MT, ntok, on_chunk=None):
    MUL, ADD = ALU.mult, ALU.add
    with ExitStack() as st:
        bm, bbm = sb(nc, st, "ba_bm", [128, 4, 4], F32)
        P.dma("sp", bm[:], bm_ap, writes=[bbm])
        wm, bwm = sb(nc, st, "ba_wm", [128, KC, 4 * DQ], BF16)
        wmv = wm_ap.rearrange("(kc p) c -> p kc c", p=128)
        for kc in range(0, KC, 2):
            P.dma("pool", wm[:, kc:kc + 2, :], wmv[:, kc:kc + 2, :], writes=[bwm])
        wb, bwb = sb(nc, st, "ba_wb", [128, 4, 8, DQ], BF16)
        for k in range(4):
            P.dma("pool", wb[:, k, :, :], wb_ap[k, :, :].rearrange("(cc p) d -> p cc d", p=128), writes=[bwb])
        hr = Ring(nc, st, "ba_h", [128, KC, 512], BF16, 2)
        yr = Ring(nc, st, "ba_y", [128, 32, 512], BF16, 1)
        gr = Ring(nc, st, "ba_g", [128, 512], F32, 2)
        ar = Ring(nc, st, "ba_a", [128, 512], F32, 2)
        tr = Ring(nc, st, "ba_t", [128, 512], F32, 2)
        mr = Ring(nc, st, "ba_m", [128, 512], BF16, 2)
        psm = Ring(nc, st, "ba_psm", [128, 512], F32, 4, psum=True)
        psp = Ring(nc, st, "ba_psp", [128, 512], F32, 4, psum=True)
        mtw = []
        bysk = [Buf("ysk%d" % k_) for k_ in range(4)]
        for tt in range(ntok // 512):
            ht, bh = hr.next()
            P.dma("sp", ht[:], HT[:, :, tt * 512:(tt + 1) * 512], writes=[bh])
            ys, _ = yr.next()
            for k in range(4):
                for hg in range(HG):
                    row = hg * 4 * CW + k * CW
                    P.dma("act" if (k + hg) % 2 else "sp", ys[:, k * 8 + hg * 2:k * 8 + hg * 2 + 2, :],
                          YSall[row:row + CW, tt * 512:(tt + 1) * 512].rearrange("(j p) t -> p j t", p=128), writes=[bysk[k]])
            for dcl in range(4):
                acc, bacc = ar.next()
                for k in range(4):
                    pm, bpm = psm.next()
                    for kc in range(KC):
                        P.op("pe", lambda e: e.matmul(pm[:], lhsT=wm[:, kc, k * DQ + dcl * 128:k * DQ + (dcl + 1) * 128], rhs=ht[:, kc, :],
                                                      start=(kc == 0), stop=(kc == KC - 1)), reads=[bwm, bh], writes=[bpm], sig=(kc == KC - 1))
                    gt, bg = gr.next()
                    P.op("act", lambda e: e.activation(out=gt[:], in_=pm[:], func=AF.Sigmoid, bias=bm[:, k, dcl:dcl + 1], scale=1.0),
                         reads=[bpm, bbm], writes=[bg])
                    pp, bpp = psp.next()
                    for cc in range(8):
                        P.op("pe", lambda e: e.matmul(pp[:], lhsT=wb[:, k, cc, dcl * 128:(dcl + 1) * 128], rhs=ys[:, k * 8 + cc, :],
                                                      start=(cc == 0), stop=(cc == 7)), reads=[bwb, bysk[k]], writes=[bpp], sig=(cc == 7))
                    if k == 0:
                        P.op("dve", lambda e: e.tensor_tensor(out=acc[:], in0=pp[:], in1=gt[:], op=MUL), reads=[bpp, bg], writes=[bacc])
                    else:
                        t_, bt_ = tr.next()
                        P.op("dve", lambda e: e.tensor_tensor(out=t_[:], in0=pp[:], in1=gt[:], op=MUL), reads=[bpp, bg], writes=[bt_])
                        if k < 3:
                            P.op("pool", lambda e: e.tensor_tensor(out=acc[:], in0=acc[:], in1=t_[:], op=ADD), reads=[bacc, bt_], writes=[bacc])
                        else:
                            mg, bmg = mr.next()
                            P.op("pool", lambda e: e.tensor_tensor(out=mg[:], in0=acc[:], in1=t_[:], op=ADD), reads=[bacc, bt_], writes=[bmg])
                            bw_ = Buf("mtw")
                            P.dma("sp", MT[dcl * 128:(dcl + 1) * 128, tt * 512:(tt + 1) * 512], mg[:], reads=[bmg], writes=[bw_])
                            mtw.append(bw_)
            if on_chunk is not None and tt % 2 == 1:
                on_chunk(tt // 2, mtw)
                mtw = []
        P.barrier()


def stage_back_b(P, nc, MTall, wo_ap, Xcol, ntok, on_tile=None):
    with ExitStack() as st:
        wo, bwo = sb(nc, st, "bb_wo", [128, KC, DQ], BF16)
        P.dma("pool", wo[:], wo_ap.rearrange("(kc p) e -> p kc e", p=128), writes=[bwo])
        mr = Ring(nc, st, "bb_m", [128, KC, 512], BF16, 2)
        xr = Ring(nc, st, "bb_x", [128, DQ], F32, 3)
        pso = Ring(nc, st, "bb_ps", [128, 512], F32, 3, psum=True)
        for tt in range(ntok // 512):
            mg, bmg = mr.next()
            P.dma("sp", mg[:], MTall[0:D, tt * 512:(tt + 1) * 512].rearrange("(dc p) t -> p dc t", p=128), writes=[bmg])
            xw = []
            for ts in range(4):
                r0_ = tt * 512 + ts * 128
                xt, bx = xr.next()
                P.dma("act", xt[:], Xcol[r0_:r0_ + 128, :], writes=[bx])
                po, bpo = pso.next()
                for dc in range(KC):
                    P.op("pe", lambda e: e.matmul(po[:], lhsT=mg[:, dc, ts * 128:(ts + 1) * 128], rhs=wo[:, dc, :],
                                                  start=(dc == 0), stop=(dc == KC - 1)), reads=[bmg, bwo], writes=[bpo], sig=(dc == KC - 1))
                P.op("dve", lambda e: e.tensor_tensor(out=xt[:], in0=po[:], in1=xt[:], op=ALU.add), reads=[bpo, bx], writes=[bx])
                bw_ = Buf("xw")
                P.dma("sp", Xcol[r0_:r0_ + 128, :], xt[:], reads=[bx], writes=[bw_])
                xw.append(bw_)
            if on_tile is not None:
                on_tile(tt, xw)
        P.barrier()


def stage_final(P, nc, xg4, Xcol, fg_ap, out_ap, ntok):
    MUL, ADD = ALU.mult, ALU.add
    with ExitStack() as st:
        fg, bfg = sb(nc, st, "fn_fg", [128, DQ], F32)
        P.dma("sp", fg[:], fg_ap[0:1, :].partition_broadcast(128), writes=[bfg])
        xr = Ring(nc, st, "fn_x", [128, D], F32, 2)
        cr = Ring(nc, st, "fn_c", [128, DQ], F32, 2)
        jr = Ring(nc, st, "fn_j", [128, D], BF16, 1)
        sr = Ring(nc, st, "fn_s", [128, 4], F32, 2)
        for tt in range(ntok // 128):
            xt, bx = xr.next()
            P.dma("sp", xt[:, :].rearrange("p (r e) -> p r e", r=4), xg4[tt * 128:(tt + 1) * 128, :, :], writes=[bx])
            ct, bc = cr.next()
            P.dma("act", ct[:], Xcol[tt * 128:(tt + 1) * 128, :], writes=[bc])
            s, bs = sr.next()
            j, bj = jr.next()
            P.op("act", lambda e: e.activation(out=j[:], in_=xt[:], func=AF.Square, accum_out=s[:, 0:1]), reads=[bx], writes=[bj, bs])
            P.op("dve", lambda e: e.tensor_scalar(out=s[:, 1:2], in0=s[:, 0:1], scalar1=1.0 / D, scalar2=EPS, op0=MUL, op1=ADD),
                 reads=[bs], writes=[bs])
            P.op("act", lambda e: e.activation(out=s[:, 1:2], in_=s[:, 1:2], func=AF.Sqrt), reads=[bs], writes=[bs])
            P.op("dve", lambda e: e.reciprocal(out=s[:, 2:3], in_=s[:, 1:2]), reads=[bs], writes=[bs])
            P.op("dve", lambda e: e.scalar_tensor_tensor(out=ct[:], in0=ct[:], scalar=s[:, 2:3], in1=fg[:], op0=MUL, op1=MUL),
                 reads=[bc, bs, bfg], writes=[bc])
            P.dma("sp", out_ap[tt * 128:(tt + 1) * 128, :], ct[:], reads=[bc])
        P.barrier()


GROUPS = [[0, 1, 2, 3], [4, 5, 6, 7]]


class ColChunks:
    def __init__(self, aps, ch):
        self.aps, self.ch = aps, ch

    def __getitem__(self, idx):
        rsl, csl = idx
        j = csl.start // self.ch
        assert (csl.stop - 1) // self.ch == j
        return self.aps[j][rsl, csl.start - j * self.ch:csl.stop - j * self.ch]


class RowChunks:
    def __init__(self, aps, ch):
        self.aps, self.ch = aps, ch

    def __getitem__(self, idx):
        rsl = idx[0]
        j = rsl.start // self.ch
        assert (rsl.stop - 1) // self.ch == j
        return self.aps[j][(slice(rsl.start - j * self.ch, rsl.stop - j * self.ch),) + tuple(idx[1:])]


def build_fused(depth=DEPTH, S=S):
    nc = bass.Bass("TRN2", target_bir_lowering=False)
    di = lambda n, s, d: nc.dram_tensor(n, s, d, kind="ExternalInput").ap()
    xcol_in = di("xcol", [S, DQ], F32)
    xfull = di("xfull", [S, HG, DQ], F32)
    wc = di("wc", [depth, D, NU], F32)
    g = di("g", [depth, 128, KC], F32)
    prm = di("prm", [depth, 128, 2, NPRM], F32)
    lora = di("lora", [depth, 128, CW], F32)
    pw = di("pw", [depth, CW, CW], F32)
    bf = di("bf", [depth, 4, 1], F32)
    wm = di("wm", [depth, D, 4 * DQ], F32)
    wb = di("wb", [depth, 4, W, DQ], F32)
    wo = di("wo", [depth, D, DQ], F32)
    bm = di("bm", [depth, 128, 4, 4], F32)
    fg = di("fg", [1, DQ], F32)
    cst = di("cst", [128, NCONST], F32)
    sel = di("sel", [128, 4], F32)
    invc = di("invc", [1, S], F32)
    out = nc.dram_tensor("out", [S, DQ], F32, kind="ExternalOutput").ap()
    NXC = S // 512
    Xcol_t = [nc.dram_tensor("Xcol%d" % j, [512, DQ], F32) for j in range(NXC)]
    XG_t = [nc.dram_tensor("XG%d" % j, [HG * 512, DQ], F32) for j in range(NXC)]
    YS_t = [nc.dram_tensor("YS%d" % j, [4 * CW, 512], BF16) for j in range(NXC)]
    YSall_t = [nc.dram_tensor("YSall%d" % j, [HG * 4 * CW, 512], BF16) for j in range(NXC)]
    NMC = S // 1024
    MT_t = [nc.dram_tensor("MT%d" % j, [DQ, 1024], BF16) for j in range(NMC)]
    MTall_t = [nc.dram_tensor("MTall%d" % j, [D, 1024], BF16) for j in range(NMC)]
    HT = nc.dram_tensor("HT", [128, KC, S], BF16).ap()
    U = nc.dram_tensor("U", [NU, S], F32).ap()
    FC = nc.dram_tensor("FC", [4, 6, S], BF16).ap()
    Xcol = RowChunks([t.ap() for t in Xcol_t], 512)
    xg4 = RowChunks([t.ap().rearrange("(r t) e -> t r e", r=HG) for t in XG_t], 512)
    YS = ColChunks([t.ap() for t in YS_t], 512)
    YSall = ColChunks([t.ap() for t in YSall_t], 512)
    MT = ColChunks([t.ap() for t in MT_t], 1024)
    MTall = ColChunks([t.ap() for t in MTall_t], 1024)
    xpairs = list(zip(Xcol_t, XG_t))
    ypairs = list(zip(YS_t, YSall_t))
    mpairs = list(zip(MT_t, MTall_t))
    with ExitStack() as st:
        import os
        skip = os.environ.get("FUSE_SKIP", "")
        P = Prog(nc, st)
        if "i" not in skip:
            idf, idb = make_identity(P, nc, st)
            cs, bcs = sb(nc, st, "cst_sb", [128, NCONST], F32)
            P.dma("sp", cs[:], cst, writes=[bcs])
        for i in range(NXC):
            P.dma("sp" if i % 2 else "act", Xcol[i * 512:(i + 1) * 512, :], xcol_in[i * 512:(i + 1) * 512, :])
        import os
        dbg = os.environ.get("FUSE_DBG", "npcqfrab")
        cb_y = lambda j, deps: P.collective_async("AllGather", YS_t[j], YSall_t[j], GROUPS, deps)
        cb_m = lambda j, deps: P.collective_async("AllGather", MT_t[j], MTall_t[j], GROUPS, deps)
        cb_x = lambda j, deps: P.collective_async("AllGather", Xcol_t[j], XG_t[j], GROUPS, deps)
        for l in range(depth if "L" not in skip else 0):
            if l > 0:
                P.collective_wait()
            if "n" in dbg: stage_norm_T(P, nc, xfull if l == 0 else xg4, g[l], HT, S, idb, gathered=True)
            if "p" in dbg: stage_proj(P, nc, wc[l], NU, HT, S, U)
            if "c" in dbg: stage_conv(P, nc, U, prm[l], YS, S)
            if "q" in dbg: stage_pool(P, nc, U, prm[l], sel, invc, pw[l], YS, S)
            if "f" in dbg: stage_fox(P, nc, U, bf[l], (cs, bcs), YS, S, idf, FC)
            stage_rwkv(P, nc, U, prm[l], lora[l], (cs, bcs), YS, S, idf, on_tile=cb_y)
            P.collective_wait()
            stage_back_a(P, nc, HT, YSall, wm[l], wb[l], bm[l], MT, S, on_chunk=cb_m)
            P.collective_wait()
            stage_back_b(P, nc, MTall, wo[l], Xcol, S, on_tile=cb_x)
            if "e" in dbg: P.new_epoch()
        if "g" not in skip:
            if depth > 0:
                P.collective_wait()
            else:
                P.collectives("AllGather", xpairs, GROUPS)
        if "f" not in skip:
            stage_final(P, nc, xg4, Xcol, fg, out, S)
        P.barrier()
        print("fused nins", P.nins)
    return nc


_CACHE = {}


def kernel(**inp):
    inp = {k: np.asarray(v) for k, v in inp.items()}
    x = np.ascontiguousarray(inp["x"], dtype=np.float32)
    S = x.shape[1]
    if "fused" not in _CACHE:
        _CACHE["fused"] = build_fused(DEPTH, S)
    cst = host_consts()
    cores = list(range(8))
    L = DEPTH
    maps = []
    for c in cores:
        b, q = c // HG, c % HG
        es = slice(q * DQ, (q + 1) * DQ)
        sel, invc = host_poolc(q, S)
        cols = core_cols(q)
        m = {
            "xcol": np.ascontiguousarray(x[b][:, es]),
            "xfull": x[b].reshape(S, HG, DQ),
            "wc": np.ascontiguousarray(inp["w_in"][:, :, cols]),
            "g": np.ascontiguousarray(inp["norm_g"].reshape(L, KC, 128).transpose(0, 2, 1)),
            "prm": np.stack([host_prm(inp, l, q) for l in range(L)]),
            "lora": np.stack([host_lora(inp, l, q) for l in range(L)]),
            "pw": np.ascontiguousarray(inp["pool_w"][:, q]),
            "bf": np.ascontiguousarray(inp["fox_bf"][:, q * 4:(q + 1) * 4].reshape(L, 4, 1)),
            "wm": np.ascontiguousarray(np.concatenate(
                [inp["w_in"][:, :, OM + k * D + q * DQ:OM + k * D + (q + 1) * DQ] for k in range(4)], axis=2)),
            "wb": np.ascontiguousarray(inp["w_branch"][:, :, :, es]),
            "wo": np.ascontiguousarray(inp["w_out"][:, :, es]),
            "bm": np.ascontiguousarray(inp["b_merge"][:, :, es].reshape(L, 4, 4, 128).transpose(0, 3, 1, 2)),
            "fg": np.ascontiguousarray(inp["final_g"][es].reshape(1, DQ)),
            "cst": cst, "sel": sel, "invc": invc,
        }
        maps.append(m)
    res = run_bass_kernel_spmd(_CACHE["fused"], maps, core_ids=cores)
    out = np.empty_like(x)
    for c in cores:
        b, q = c // HG, c % HG
        out[b][:, q * DQ:(q + 1) * DQ] = np.asarray(res.results[c]["out"])
    return out
```

```python
import numpy as np
from contextlib import ExitStack
import concourse.bass as bass
import concourse.mybir as mybir
from concourse.bass_utils import run_bass_kernel_spmd

F32 = mybir.dt.float32
BF16 = mybir.dt.bfloat16
AF = mybir.ActivationFunctionType
ALU = mybir.AluOpType
AX = mybir.AxisListType

D = 2048
S = 8192
NB = 2
DEPTH = 4
W = 1024
NIN = 22672
KC = D // 128
HG = 4
CW = W // HG
EPS = 1e-6

_names = [("A_b", CW), ("A_c", CW), ("A_x", CW), ("A_g", CW),
          ("B_r", CW), ("B_k", CW), ("B_v", CW), ("B_lora", 128), ("B_g", CW),
          ("C_q", CW), ("C_k", CW), ("C_v", CW), ("C_g", CW),
          ("D_x", CW), ("D_g", CW), ("C_f", 4)]
UROW = {}
_o = 0
for _n, _s in _names:
    UROW[_n] = (_o, _s)
    _o += _s
NU = _o


def core_cols(hg):
    c = lambda base, n=CW: list(range(base + hg * n, base + (hg + 1) * n))
    oA = 0
    oB = 4 * W
    oBg = oB + 3 * W + 128
    oC = oBg + W
    oCf = oC + 3 * W
    oCg = oCf + 16
    oD = oCg + W
    oDg = oD + W
    cols = []
    cols += c(oA) + c(oA + W) + c(oA + 2 * W) + c(oA + 3 * W)
    cols += c(oB) + c(oB + W) + c(oB + 2 * W) + list(range(oB + 3 * W, oB + 3 * W + 128)) + c(oBg)
    cols += c(oC) + c(oC + W) + c(oC + 2 * W) + c(oCg)
    cols += c(oD) + c(oDg)
    cols += list(range(oCf + hg * 4, oCf + hg * 4 + 4))
    assert len(cols) == NU
    return np.array(cols)


OM = 4 * W + (3 * W + 128) + W + 3 * W + 16 + W + W + W
assert OM + 4 * D == NIN


class Buf:
    __slots__ = ("name", "w", "r", "psum", "ep")

    def __init__(self, name="", psum=False):
        self.name = name
        self.w = None
        self.r = {}
        self.psum = psum
        self.ep = 0


class Prog:
    NDMA = 32
    NSW = 8

    def __init__(self, nc, stack):
        self.nc = nc
        self.stack = stack
        self.eng = {"pe": nc.tensor, "act": nc.scalar, "dve": nc.vector,
                    "pool": nc.gpsimd, "sp": nc.sync}
        self.ep = 0
        self.nins = 0
        self.ccsem = None
        self.cccnt = 0
        self._fresh()

    def _fresh(self):
        nc, stack = self.nc, self.stack
        self.sem = {}
        self.cnt = {}
        for e in self.eng:
            self.sem[e] = stack.enter_context(nc.semaphore("s%d_%s" % (self.ep, e)))
            self.cnt[e] = 0
        self.dsem = [stack.enter_context(nc.semaphore("d%d_%d" % (self.ep, i))) for i in range(self.NDMA)]
        self.dcnt = [0] * self.NDMA
        self.dnext = 0
        self.swnext = 0
        self.seen = {e: {} for e in self.eng}

    def new_epoch(self):
        self.barrier()
        self.ep += 1
        self._fresh()

    def _chk(self, b):
        if b.ep != self.ep:
            b.ep = self.ep
            b.w = None
            b.r = {}

    def collective(self, kind, in_t, out_t, groups):
        self.collectives(kind, [(in_t, out_t)], groups)

    def collective_async(self, kind, in_t, out_t, groups, deps=()):
        if self.ccsem is None:
            self.ccsem = self.stack.enter_context(self.nc.semaphore("ccsem"))
        self._deps("pool", list(deps), [])
        ins = self.nc.gpsimd.collective_compute(kind, ALU.bypass, replica_groups=groups,
                                                ins=[in_t.ap().opt()], outs=[out_t.ap().opt()])
        ins.then_inc(self.ccsem)
        self.cccnt += 1
        self.nins += 1

    def collective_wait(self):
        if self.ccsem is not None:
            for e in self.eng.values():
                e.wait_ge(self.ccsem, self.cccnt)

    def collectives(self, kind, pairs, groups):
        self.barrier()
        if self.ccsem is None:
            self.ccsem = self.stack.enter_context(self.nc.semaphore("ccsem"))
        for in_t, out_t in pairs:
            ins = self.nc.gpsimd.collective_compute(kind, ALU.bypass, replica_groups=groups,
                                                    ins=[in_t.ap().opt()], outs=[out_t.ap().opt()])
            ins.then_inc(self.ccsem)
            self.cccnt += 1
            self.nins += 1
        for e in self.eng.values():
            e.wait_ge(self.ccsem, self.cccnt)

    def _semobj(self, key):
        return self.sem[key] if isinstance(key, str) else self.dsem[key]

    def _wait(self, e, key, val):
        if key == e and val > self.cnt[e]:
            return
        if self.seen[e].get(key, 0) >= val:
            return
        self.seen[e][key] = val
        self.eng[e].wait_ge(self._semobj(key), val)

    def _deps(self, e, reads, writes):
        for b in reads:
            self._chk(b)
        for b in writes:
            self._chk(b)
        for b in reads:
            if b.w is not None:
                self._wait(e, *b.w)
            if b.psum:
                for k, v in b.r.items():
                    if k != e:
                        self._wait(e, k, v)
        for b in writes:
            if b.w is not None:
                self._wait(e, *b.w)
            for k, v in b.r.items():
                self._wait(e, k, v)

    def _mark(self, key, val, reads, writes):
        for b in reads:
            if b.r.get(key, 0) < val:
                b.r[key] = val
        for b in writes:
            b.w = (key, val)
            b.r = {}

    def op(self, e, fn, reads=(), writes=(), sig=True):
        self._deps(e, reads, writes)
        ins = fn(self.eng[e])
        if sig:
            self.cnt[e] += 1
            ins.then_inc(self.sem[e], 1)
            self._mark(e, self.cnt[e], reads, writes)
        else:
            self._mark(e, self.cnt[e] + 1, reads, writes)
        self.nins += 1

    def dma(self, q, out, in_, reads=(), writes=(), **kw):
        if q == "pool":
            i = self.NDMA - self.NSW + self.swnext
            self.swnext = (self.swnext + 1) % self.NSW
        else:
            i = self.dnext
            self.dnext = (self.dnext + 1) % (self.NDMA - self.NSW)
        if self.dcnt[i] > 0:
            self._wait(q, i, self.dcnt[i])
        self._deps(q, reads, writes)
        ins = self.eng[q].dma_start(out=out, in_=in_, **kw)
        self.dcnt[i] += 16
        ins.then_inc(self.dsem[i], 16)
        self._mark(i, self.dcnt[i], reads, writes)
        self.nins += 1

    def barrier(self):
        for e in self.eng:
            for e2 in self.eng:
                if e2 != e and self.cnt[e2] > 0:
                    self._wait(e, e2, self.cnt[e2])
            for i in range(self.NDMA):
                if self.dcnt[i] > 0:
                    self._wait(e, i, self.dcnt[i])


_UID = [0]


def _uid():
    _UID[0] += 1
    return _UID[0]


class Ring:
    def __init__(self, nc, st, name, shape, dtype, n, psum=False):
        alloc = nc.psum_tensor if psum else nc.sbuf_tensor
        u = _uid()
        self.t = [st.enter_context(alloc("%s_%d_%d" % (name, u, i), shape, dtype)) for i in range(n)]
        self.b = [Buf("%s%d" % (name, i), psum) for i in range(n)]
        self.i = 0

    def next(self):
        i = self.i
        self.i = (self.i + 1) % len(self.t)
        return self.t[i], self.b[i]


def sb(nc, st, name, shape, dtype):
    return st.enter_context(nc.sbuf_tensor("%s_%d" % (name, _uid()), shape, dtype)), Buf(name)


def make_identity(P, nc, st, name="ident"):
    idf, bidf = sb(nc, st, name + "f", [128, 128], F32)
    idb, bidb = sb(nc, st, name + "b", [128, 128], BF16)
    P.op("pool", lambda e: e.memset(idf[:], 1.0), writes=[bidf])
    P.op("pool", lambda e: e.affine_select(out=idf[:], in_=idf[:], pattern=[[-1, 128]],
                                           compare_op=ALU.is_equal, fill=0.0, base=0,
                                           channel_multiplier=1), reads=[bidf], writes=[bidf])
    P.op("dve", lambda e: e.tensor_copy(out=idb[:], in_=idf[:]), reads=[bidf], writes=[bidb])
    return (idf, bidf), (idb, bidb)


def stage_norm_T(P, nc, x_ap, g_ap, HT, ntok, identb, gathered=False):
    idb, bidb = identb
    with ExitStack() as st:
        gs, bgs = sb(nc, st, "n_g", [128, KC], F32)
        P.dma("sp", gs[:], g_ap, writes=[bgs])
        xr = Ring(nc, st, "n_x", [128, D], F32, 3)
        hr = Ring(nc, st, "n_h", [128, D], BF16, 3)
        jr = Ring(nc, st, "n_j", [128, D], BF16, 2)
        sr = Ring(nc, st, "n_s", [128, 4], F32, 4)
        pr = Ring(nc, st, "n_ps", [128, KC, 128], BF16, 3, psum=True)
        hTr = Ring(nc, st, "n_hT", [128, KC, 512], BF16, 2)
        gbc = gs[:, :].unsqueeze(2).to_broadcast([128, KC, 128])
        hT_of = {}

        def tile_gen(tt):
            t4, q = divmod(tt, 4)
            if q == 0:
                hT_of[t4] = hTr.next()
            hT, bhT = hT_of[t4]
            xt, bx = xr.next()
            if gathered:
                P.dma("sp" if q % 2 == 0 else "act", xt[:, :].rearrange("p (r e) -> p r e", r=4),
                      x_ap[tt * 128:(tt + 1) * 128, :, :], writes=[bx])
            else:
                P.dma("sp" if q % 2 == 0 else "act", xt[:], x_ap[tt * 128:(tt + 1) * 128, :], writes=[bx])
            s, bs = sr.next()
            j, bj = jr.next()
            P.op("act", lambda e: e.activation(out=j[:], in_=xt[:], func=AF.Square,
                                               accum_out=s[:, 0:1]), reads=[bx], writes=[bj, bs])
            yield
            P.op("dve", lambda e: e.tensor_scalar(out=s[:, 1:2], in0=s[:, 0:1], scalar1=1.0 / D, scalar2=EPS,
                                                  op0=ALU.mult, op1=ALU.add), reads=[bs], writes=[bs])
            P.op("act", lambda e: e.activation(out=s[:, 1:2], in_=s[:, 1:2], func=AF.Sqrt),
                 reads=[bs], writes=[bs])
            P.op("dve", lambda e: e.reciprocal(out=s[:, 2:3], in_=s[:, 1:2]), reads=[bs], writes=[bs])
            yield
            h, bh = hr.next()
            P.op("dve", lambda e: e.tensor_scalar(out=h[:], in0=xt[:], scalar1=s[:, 2:3], scalar2=None,
                                                  op0=ALU.mult), reads=[bx, bs], writes=[bh])
            yield
            ps, bps = pr.next()
            for kc in range(KC):
                P.op("pe", lambda e: e.transpose(out=ps[:, kc, :], in_=h[:, kc * 128:(kc + 1) * 128],
                                                 identity=idb[:]), reads=[bh, bidb], writes=[bps], sig=(kc == KC - 1))
            yield
            P.op("dve", lambda e: e.tensor_tensor(out=hT[:, :, q * 128:(q + 1) * 128], in0=ps[:], in1=gbc,
                                                  op=ALU.mult), reads=[bps, bgs], writes=[bhT])
            if q == 3:
                P.dma("sp", HT[:, :, t4 * 512:(t4 + 1) * 512], hT[:], reads=[bhT])

        ntile = ntok // 128
        active, nxt, WIN = [], 0, 3
        while nxt < ntile or active:
            while len(active) < WIN and nxt < ntile:
                active.append(tile_gen(nxt))
                nxt += 1
            for g_ in list(active):
                try:
                    next(g_)
                except StopIteration:
                    active.remove(g_)
        P.barrier()


def stage_proj(P, nc, wc_ap, ncols, HT, ntok, U, group=8):
    nchunk = (ncols + 127) // 128
    with ExitStack() as st:
        wr = Ring(nc, st, "p_w", [128, KC, group * 128], BF16, 2)
        hr = Ring(nc, st, "p_h", [128, KC, 512], BF16, 2)
        pr = Ring(nc, st, "p_ps", [128, 512], F32, 6, psum=True)
        er = Ring(nc, st, "p_e", [128, 512], F32, 6)
        wv = wc_ap.rearrange("(kc p) c -> p kc c", p=128)
        ev = 0
        for g0 in range(0, nchunk, group):
            c0 = g0 * 128
            c1 = min(ncols, (g0 + group) * 128)
            wt, bw = wr.next()
            for kc in range(0, KC, 4):
                P.dma("pool", wt[:, kc:kc + 4, 0:c1 - c0], wv[:, kc:kc + 4, c0:c1], writes=[bw])
            for tt in range(ntok // 512):
                ht, bh = hr.next()
                P.dma("sp", ht[:], HT[:, :, tt * 512:(tt + 1) * 512], writes=[bh])
                for ch in range(g0, min(nchunk, g0 + group)):
                    m = min(128, ncols - ch * 128)
                    lc = (ch - g0) * 128
                    ps, bps = pr.next()
                    for kc in range(KC):
                        P.op("pe", lambda e: e.matmul(ps[0:m, :], lhsT=wt[:, kc, lc:lc + m], rhs=ht[:, kc, :],
                                                      start=(kc == 0), stop=(kc == KC - 1)),
                             reads=[bw, bh], writes=[bps], sig=(kc == KC - 1))
                    et, be = er.next()
                    if ev % 2 == 0:
                        P.op("act", lambda e: e.copy(out=et[0:m, :], in_=ps[0:m, :]), reads=[bps], writes=[be])
                    else:
                        P.op("dve", lambda e: e.tensor_copy(out=et[0:m, :], in_=ps[0:m, :]), reads=[bps], writes=[be])
                    ev += 1
                    P.dma("sp" if ev % 2 == 0 else "act", U[ch * 128:ch * 128 + m, tt * 512:(tt + 1) * 512], et[0:m, :],
                          reads=[be])
        P.barrier()


NCONST = 128 + 512 + 128 + 512 + 128
def host_consts():
    c = np.zeros((128, NCONST), np.float32)
    blk = (np.arange(128)[:, None] // 64) == (np.arange(128)[None, :] // 64)
    c[:, 0:128] = blk
    s = np.arange(128)[:, None]
    t = np.arange(128)[None, :]
    su = (blk & (s < t)).astype(np.float32)
    u = (blk & (s <= t)).astype(np.float32)
    c[:, 128:640] = np.concatenate([su, u, su, u], axis=1)
    c[:, 640:768] = (blk & (s > t)).astype(np.float32)
    c[:, 768:1280] = (np.arange(512)[None, :] % 64 != 0)
    c[:, 1280:1408] = (s <= t)
    return c


PRM = {"mu_r": 0, "mu_k": 1, "mu_v": 2, "w0": 3, "a0": 4, "k_k": 5, "k_a": 6, "ln_g": 7, "ln_b": 8,
       "r_k": 9, "mu_lora": 10, "cw0": 11, "cw1": 12, "cw2": 13, "pscale": 14, "omka": 15}
NPRM = 16


def host_prm(inp, l, hg):
    sl = slice(hg * CW, (hg + 1) * CW)
    p = np.zeros((CW, NPRM), np.float32)
    mu = inp["rwkv_mu"][l]
    p[:, 0] = mu[0:W][sl]
    p[:, 1] = mu[W:2 * W][sl]
    p[:, 2] = mu[2 * W:3 * W][sl]
    p[:, 3] = inp["rwkv_w0"][l][sl]
    p[:, 4] = inp["rwkv_a0"][l][sl]
    p[:, 5] = inp["rwkv_kk"][l][sl]
    p[:, 6] = inp["rwkv_ka"][l][sl]
    p[:, 7] = inp["rwkv_ln_g"][l][sl]
    p[:, 8] = inp["rwkv_ln_b"][l][sl]
    p[:, 9] = inp["rwkv_rk"][l].reshape(-1)[sl]
    p[0:128, 10] = mu[3 * W:3 * W + 128]
    p[:, 11] = inp["conv_w"][l][0][sl]
    p[:, 12] = inp["conv_w"][l][1][sl]
    p[:, 13] = inp["conv_w"][l][2][sl]
    p[:, 14] = inp["pool_scale"][l][sl]
    return np.ascontiguousarray(p.reshape(2, 128, NPRM).transpose(1, 0, 2))


def host_lora(inp, l, hg):
    sl = slice(hg * CW, (hg + 1) * CW)
    return np.ascontiguousarray(np.concatenate([inp["rwkv_w2"][l][:, sl], inp["rwkv_a2"][l][:, sl]], axis=0))


DBG = 0


def stage_rwkv(P, nc, U, prm_ap, lora_ap, cst, YS, ntok, identf, on_tile=None):
    idf, bidf = identf
    cs, bcs = cst
    r0 = {k: UROW[k][0] for k in UROW}
    NT = ntok // 512
    MUL, ADD, SUB = ALU.mult, ALU.add, ALU.subtract
    with ExitStack() as st:
        prm, bprm = sb(nc, st, "r_prm", [128, 2, NPRM], F32)
        P.dma("sp", prm[:], prm_ap, writes=[bprm])
        lw2, blw2 = sb(nc, st, "r_lora", [128, CW], F32)
        P.dma("sp", lw2[:], lora_ap, writes=[blw2])
        for c in range(2):
            P.op("dve", lambda e: e.tensor_scalar(out=prm[:, c, 15:16], in0=prm[:, c, 6:7], scalar1=-1.0, scalar2=1.0,
                                                  op0=MUL, op1=ADD), reads=[bprm], writes=[bprm])
        pc = lambda c, n: prm[:, c, PRM[n]:PRM[n] + 1]
        blk = cs[:, 0:128]
        mask4 = cs[:, 128:640]
        masksl = cs[:, 640:768]
        scanm = cs[:, 768:1280]
        psr = Ring(nc, st, "r_ps", [128, 512], F32, 4, psum=True)
        pqr = [Ring(nc, st, "r_pq%d" % i, [128, 512], F32, 1, psum=True) for i in range(2)]
        pyr = [Ring(nc, st, "r_py%d" % i, [128, 512], F32, 1, psum=True) for i in range(2)]
        ur = Ring(nc, st, "r_u", [128, 513], F32, 4)
        tr = Ring(nc, st, "r_t", [128, 512], F32, 6)
        names = ["xr", "xk", "xv", "lw", "al", "kk", "L", "G", "Gi", "aT", "rT", "bT", "kT", "bh", "kh", "bon", "y"]
        opr = {n: Ring(nc, st, "r_o_" + n, [128, 512], F32, 2) for n in names}
        lor = Ring(nc, st, "r_lo", [128, 512], F32, 2)
        MTr_ = [Ring(nc, st, "r_MT%d" % i, [128, 512], F32, 2) for i in range(2)]
        XYr_ = [Ring(nc, st, "r_XY%d" % i, [128, 256], F32, 3) for i in range(2)]
        Fr_ = [Ring(nc, st, "r_F%d" % i, [128, 128], F32, 3) for i in range(2)]
        TKr_ = [Ring(nc, st, "r_TK%d" % i, [128, 320], F32, 2) for i in range(2)]
        Er_ = [Ring(nc, st, "r_E%d" % i, [128, 128], F32, 3) for i in range(2)]
        PTr_ = [Ring(nc, st, "r_PT%d" % i, [128, 2, 64], F32, 2) for i in range(2)]
        Qr_ = [Ring(nc, st, "r_Q%d" % i, [128, 2, 64], F32, 2) for i in range(2)]
        ZTr_ = [Ring(nc, st, "r_ZT%d" % i, [128, 128], F32, 2) for i in range(2)]
        Hr = [Ring(nc, st, "r_H%d" % i, [128, 64], F32, 3) for i in range(4)]
        gr = Ring(nc, st, "r_g", [128, 512], F32, 2)
        obr = Ring(nc, st, "r_ob", [128, 512], BF16, 2)
        Hcur = [None] * 4

        def load_mix(row, c, mu_name, tt, dst, bdst):
            ut, bu = ur.next()
            t0 = tt * 512
            if tt == 0:
                P.op("pool", lambda e: e.memset(ut[:, 0:1], 0.0), writes=[bu])
                P.dma("sp", ut[:, 1:513], U[row + c * 128:row + (c + 1) * 128, 0:512], writes=[bu])
            else:
                P.dma("sp", ut[:, :], U[row + c * 128:row + (c + 1) * 128, t0 - 1:t0 + 512], writes=[bu])
            d, bd = tr.next()
            P.op("pool", lambda e: e.tensor_tensor(out=d[:], in0=ut[:, 0:512], in1=ut[:, 1:513], op=SUB),
                 reads=[bu], writes=[bd])
            P.op("dve", lambda e: e.scalar_tensor_tensor(out=dst[:], in0=d[:], scalar=pc(c, mu_name), in1=ut[:, 1:513],
                                                         op0=MUL, op1=ADD), reads=[bd, bu, bprm], writes=[bdst])

        LO = {}
        YSW = {}
        NEXT = [None]

        def item(tt, pr_):
            if pr_ == 0:
                lo, blo = lor.next()
                load_mix(r0["B_lora"], 0, "mu_lora", tt, lo, blo)
                P.op("act", lambda e: e.activation(out=lo[0:64, :], in_=lo[0:64, :], func=AF.Tanh), reads=[blo], writes=[blo])
                LO[tt] = (lo, blo)
                yield
            lo, blo = LO[tt]
            T = {}
            for n in names:
                T[n] = opr[n].next()
            xr_, bxr = T["xr"]; xk, bxk = T["xk"]; xv, bxv = T["xv"]
            load_mix(r0["B_r"], pr_, "mu_r", tt, xr_, bxr)
            yield
            load_mix(r0["B_k"], pr_, "mu_k", tt, xk, bxk)
            yield
            load_mix(r0["B_v"], pr_, "mu_v", tt, xv, bxv)
            yield
            lw, blw = T["lw"]; al, bal = T["al"]
            ps, bps = psr.next()
            P.op("pe", lambda e: e.matmul(ps[:], lhsT=lw2[0:64, pr_ * 128:(pr_ + 1) * 128], rhs=lo[0:64, :],
                                          start=True, stop=True), reads=[blw2, blo], writes=[bps])
            P.op("act", lambda e: e.activation(out=lw[:], in_=ps[:], func=AF.Sigmoid, bias=pc(pr_, "w0"), scale=1.0),
                 reads=[bps, bprm], writes=[blw])
            P.op("dve", lambda e: e.tensor_scalar(out=lw[:], in0=lw[:], scalar1=-0.6065306597126334, scalar2=None,
                                                  op0=MUL), reads=[blw], writes=[blw])
            ps, bps = psr.next()
            P.op("pe", lambda e: e.matmul(ps[:], lhsT=lw2[64:128, pr_ * 128:(pr_ + 1) * 128], rhs=lo[64:128, :],
                                          start=True, stop=True), reads=[blw2, blo], writes=[bps])
            P.op("act", lambda e: e.activation(out=al[:], in_=ps[:], func=AF.Sigmoid, bias=pc(pr_, "a0"), scale=1.0),
                 reads=[bps, bprm], writes=[bal])
            yield
            kk, bkk = T["kk"]
            P.op("pool", lambda e: e.tensor_scalar(out=kk[:], in0=xk[:], scalar1=pc(pr_, "k_k"), scalar2=None, op0=MUL),
                 reads=[bxk, bprm], writes=[bkk])
            sq, bsq = tr.next()
            P.op("pool", lambda e: e.tensor_tensor(out=sq[:], in0=kk[:], in1=kk[:], op=MUL), reads=[bkk], writes=[bsq])
            ps, bps = psr.next()
            P.op("pe", lambda e: e.matmul(ps[:], lhsT=blk, rhs=sq[:], start=True, stop=True), reads=[bcs, bsq], writes=[bps])
            rn, brn = tr.next()
            P.op("act", lambda e: e.activation(out=rn[:], in_=ps[:], func=AF.Sqrt), reads=[bps], writes=[brn])
            P.op("dve", lambda e: e.tensor_scalar(out=rn[:], in0=rn[:], scalar1=1e-12, scalar2=None, op0=ALU.max),
                 reads=[brn], writes=[brn])
            P.op("dve", lambda e: e.reciprocal(out=rn[:], in_=rn[:]), reads=[brn], writes=[brn])
            P.op("pool", lambda e: e.tensor_tensor(out=kk[:], in0=kk[:], in1=rn[:], op=MUL), reads=[bkk, brn], writes=[bkk])
            yield
            t1, bt1 = tr.next()
            P.op("dve", lambda e: e.tensor_scalar(out=t1[:], in0=al[:], scalar1=pc(pr_, "k_a"), scalar2=pc(pr_, "omka"),
                                                  op0=MUL, op1=ADD), reads=[bal, bprm], writes=[bt1])
            P.op("pool", lambda e: e.tensor_tensor(out=xk[:], in0=xk[:], in1=t1[:], op=MUL), reads=[bxk, bt1], writes=[bxk])
            yield
            L, bL = T["L"]; G, bG = T["G"]; Gi, bGi = T["Gi"]
            P.op("dve", lambda e: e.tensor_tensor_scan(out=L[:], data0=scanm, data1=lw[:], initial=0.0, op0=MUL, op1=ADD),
                 reads=[bcs, blw], writes=[bL])
            P.op("act", lambda e: e.activation(out=G[:], in_=L[:], func=AF.Exp), reads=[bL], writes=[bG])
            P.op("act", lambda e: e.activation(out=Gi[:], in_=L[:], func=AF.Exp, scale=-1.0), reads=[bL], writes=[bGi])
            gp, bgp = tr.next()
            P.op("pool", lambda e: e.tensor_tensor(out=gp[:], in0=L[:], in1=lw[:], op=SUB), reads=[bL, blw], writes=[bgp])
            P.op("act", lambda e: e.activation(out=gp[:], in_=gp[:], func=AF.Exp), reads=[bgp], writes=[bgp])
            yield
            aT, baT = T["aT"]; rT, brT = T["rT"]; bT, bbT = T["bT"]; kT, bkT = T["kT"]
            bh, bbh = T["bh"]; kh, bkh = T["kh"]
            P.op("dve", lambda e: e.scalar_tensor_tensor(out=aT[:], in0=kk[:], scalar=-1.0, in1=gp[:], op0=MUL, op1=MUL),
                 reads=[bkk, bgp], writes=[baT])
            P.op("pool", lambda e: e.tensor_tensor(out=bT[:], in0=kk[:], in1=al[:], op=MUL), reads=[bkk, bal], writes=[bbT])
            P.op("pool", lambda e: e.tensor_tensor(out=bT[:], in0=bT[:], in1=Gi[:], op=MUL), reads=[bbT, bGi], writes=[bbT])
            P.op("pool", lambda e: e.tensor_tensor(out=rT[:], in0=xr_[:], in1=G[:], op=MUL), reads=[bxr, bG], writes=[brT])
            P.op("dve", lambda e: e.tensor_tensor(out=kT[:], in0=xk[:], in1=Gi[:], op=MUL), reads=[bxk, bGi], writes=[bkT])
            yield
            gcb = G[:, :].rearrange("p (c t) -> p c t", t=64)[:, :, 63:64].to_broadcast([128, 8, 64])
            P.op("dve", lambda e: e.tensor_tensor(out=bh[:, :].rearrange("p (c t) -> p c t", t=64),
                                                  in0=bT[:, :].rearrange("p (c t) -> p c t", t=64), in1=gcb, op=MUL),
                 reads=[bbT, bG], writes=[bbh])
            P.op("pool", lambda e: e.tensor_tensor(out=kh[:, :].rearrange("p (c t) -> p c t", t=64),
                                                   in0=kT[:, :].rearrange("p (c t) -> p c t", t=64), in1=gcb, op=MUL),
                 reads=[bkT, bG], writes=[bkh])
            yield
            bon, bbon = T["bon"]
            t2, bt2 = tr.next()
            P.op("dve", lambda e: e.scalar_tensor_tensor(out=t2[:], in0=xr_[:], scalar=pc(pr_, "r_k"), in1=xk[:], op0=MUL, op1=MUL),
                 reads=[bxr, bxk, bprm], writes=[bt2])
            ps, bps = psr.next()
            P.op("pe", lambda e: e.matmul(ps[:], lhsT=blk, rhs=t2[:], start=True, stop=True), reads=[bcs, bt2], writes=[bps])
            P.op("dve", lambda e: e.tensor_tensor(out=bon[:], in0=ps[:], in1=xv[:], op=MUL), reads=[bps, bxv], writes=[bbon])

            yield "PREP_DONE"
            y, by = T["y"]
            if DBG == 1:
                P.op('pool', lambda e: e.memset(y[:], 0.0), writes=[by])
            def head_stream(hh):
                MTr, XYr, Fr, TKr, Er = MTr_[hh], XYr_[hh], Fr_[hh], TKr_[hh], Er_[hh]
                PTr, Qr, ZTr = PTr_[hh], Qr_[hh], ZTr_[hh]
                hd = pr_ * 2 + hh
                R = slice(hh * 64, hh * 64 + 64)
                if Hcur[hd] is None:
                    Hcur[hd] = Hr[hd].next()
                    P.op("pool", lambda e: e.memset(Hcur[hd][0][:], 0.0), writes=[Hcur[hd][1]])
                for cp in range(4):
                    Cc = slice(cp * 128, cp * 128 + 128)
                    MT, bMT = MTr.next()
                    ps, bps = psr.next()
                    for i, (lt, blt, rt, brt) in enumerate([(bT, bbT, aT, baT), (bT, bbT, rT, brT),
                                                            (kT, bkT, aT, baT), (kT, bkT, rT, brT)]):
                        P.op("pe", lambda e: e.matmul(ps[:, i * 128:(i + 1) * 128], lhsT=lt[R, Cc], rhs=rt[R, Cc],
                                                      start=True, stop=True), reads=[blt, brt], writes=[bps], sig=(i == 3))
                    P.op("dve", lambda e: e.tensor_tensor(out=MT[:], in0=ps[:], in1=mask4, op=MUL),
                         reads=[bps, bcs], writes=[bMT])
                    XY, bXY = XYr.next()
                    ps2, bps2 = psr.next()
                    P.op("pe", lambda e: e.matmul(ps2[:, 0:128], lhsT=aT[R, Cc], rhs=bT[R, Cc], start=True, stop=True),
                         reads=[baT, bbT], writes=[bps2])
                    P.op("dve", lambda e: e.tensor_tensor(out=XY[:, 128:256], in0=ps2[:, 0:128], in1=masksl, op=MUL),
                         reads=[bps2, bcs], writes=[bXY])
                    yield
                    if DBG == 2:
                        continue
                    TK, bTK = TKr.next()
                    ps3, bps3 = psr.next()
                    for i, (src, bsrc) in enumerate([(aT, baT), (xv, bxv), (bh, bbh), (kh, bkh)]):
                        P.op("pe", lambda e: e.transpose(out=ps3[:, 64 + i * 64:128 + i * 64], in_=src[R, Cc],
                                                         identity=idf[R, R]), reads=[bsrc, bidf], writes=[bps3], sig=(i == 3))
                    P.op("act", lambda e: e.copy(out=TK[:, 64:320], in_=ps3[:, 64:320]), reads=[bps3], writes=[bTK])
                    yield
                    if DBG == 3:
                        continue
                    ps4, bps4 = psr.next()
                    P.op("pe", lambda e: e.matmul(ps4[:, 0:64], lhsT=MT[:, 256:384], rhs=TK[:, 128:192], start=True, stop=True),
                         reads=[bMT, bTK], writes=[bps4])
                    P.op("dve", lambda e: e.tensor_copy(out=TK[:, 0:64], in_=ps4[:, 0:64]), reads=[bps4], writes=[bTK])
                    if DBG == 31:
                        continue
                    F, bF = Fr.next()
                    P.op("pool", lambda e: e.tensor_tensor(out=F[:], in0=MT[:, 0:128], in1=idf[:], op=ADD),
                         reads=[bMT, bidf], writes=[bF])
                    yield
                    Ecur, bEcur = TK[:, 0:128], bTK
                    Xc, bXc = MT[:, 0:128], bMT
                    Yc, bYc = XY[:, 128:256], bXY
                    for lev in range(6):
                        pe_, bpe = psr.next()
                        P.op("pe", lambda e: e.matmul(pe_[:, 0:128], lhsT=F[:], rhs=Ecur, start=True, stop=True),
                             reads=[bF, bEcur], writes=[bpe])
                        En, bEn = Er.next()
                        P.op("act", lambda e: e.copy(out=En[:], in_=pe_[:, 0:128]), reads=[bpe], writes=[bEn])
                        Ecur, bEcur = En[:], bEn
                        yield
                        if lev == 5 or (DBG == 32):
                            break
                        px, bpx = psr.next()
                        P.op("pe", lambda e: e.matmul(px[:, 0:128], lhsT=Yc, rhs=Xc, start=True, stop=True),
                             reads=[bYc, bXc], writes=[bpx], sig=(lev >= 4))
                        if lev < 4:
                            P.op("pe", lambda e: e.matmul(px[:, 128:256], lhsT=Xc, rhs=Yc, start=True, stop=True),
                                 reads=[bYc, bXc], writes=[bpx])
                        XYn, bXYn = XYr.next()
                        F, bF = Fr.next()
                        P.op("dve", lambda e: e.tensor_tensor(out=F[:], in0=px[:, 0:128], in1=idf[:], op=ADD),
                             reads=[bpx, bidf], writes=[bF])
                        if lev < 4:
                            P.op("act", lambda e: e.copy(out=XYn[:], in_=px[:, 0:256]), reads=[bpx], writes=[bXYn])
                        Xc, bXc = XYn[:, 0:128], bXYn
                        Yc, bYc = XYn[:, 128:256], bXYn
                        yield
                        if DBG == 33 + lev:
                            break
                    if DBG == 4:
                        continue
                    if DBG >= 32 and DBG < 40:
                        continue
                    U0 = Ecur[:, 0:64]
                    Wm = Ecur[:, 64:128]
                    PT, bPT = PTr.next()
                    Q, bQ = Qr.next()
                    pq, bpq = pqr[hh].next()
                    for c2 in range(2):
                        Rc = slice(c2 * 64, c2 * 64 + 64)
                        P.op("pe", lambda e: e.matmul(pq[R, c2 * 64:c2 * 64 + 64], lhsT=Wm[Rc, :], rhs=TK[Rc, 192:256],
                                                      start=True, stop=True), reads=[bEcur, bTK], writes=[bpq])
                        gC = G[R, cp * 128 + c2 * 64 + 63:cp * 128 + c2 * 64 + 64]
                        P.op("dve", lambda e: e.scalar_tensor_tensor(out=PT[R, c2, :], in0=idf[R, R], scalar=gC,
                                                                     in1=pq[R, c2 * 64:c2 * 64 + 64], op0=MUL, op1=ADD),
                             reads=[bidf, bG, bpq], writes=[bPT])
                        P.op("pe", lambda e: e.matmul(pq[R, 128 + c2 * 64:192 + c2 * 64], lhsT=TK[Rc, 192:256], rhs=U0[Rc, :],
                                                      start=True, stop=False), reads=[bEcur, bTK], writes=[bpq], sig=False)
                        P.op("pe", lambda e: e.matmul(pq[R, 128 + c2 * 64:192 + c2 * 64], lhsT=TK[Rc, 256:320], rhs=TK[Rc, 128:192],
                                                      start=False, stop=True), reads=[bTK], writes=[bpq])
                        yield
                    P.op("act", lambda e: e.copy(out=Q[R, :, :].rearrange("p a b -> p (a b)"), in_=pq[R, 128:256]),
                         reads=[bpq], writes=[bQ])
                    ZT, bZT = ZTr.next()
                    pz, bpz = psr.next()
                    P.op("pe", lambda e: e.matmul(pz[R, 0:128], lhsT=Wm, rhs=MT[:, 128:256], start=True, stop=True),
                         reads=[bEcur, bMT], writes=[bpz])
                    P.op("dve", lambda e: e.tensor_tensor(out=ZT[R, :], in0=pz[R, 0:128], in1=rT[R, Cc], op=ADD),
                         reads=[bpz, brT], writes=[bZT])
                    yield
                    if DBG == 5:
                        continue
                    py, bpy = pyr[hh].next()
                    P.op("pe", lambda e: e.matmul(py[R, 0:128], lhsT=U0, rhs=MT[:, 128:256], start=True, stop=False),
                         reads=[bEcur, bMT], writes=[bpy], sig=False)
                    P.op("pe", lambda e: e.matmul(py[R, 0:128], lhsT=TK[:, 128:192], rhs=MT[:, 384:512], start=False, stop=False),
                         reads=[bTK, bMT], writes=[bpy], sig=False)
                    for c2 in range(2):
                        H, bH = Hcur[hd]
                        P.op("pe", lambda e: e.matmul(py[R, c2 * 64:c2 * 64 + 64], lhsT=H[R, :], rhs=ZT[R, c2 * 64:c2 * 64 + 64],
                                                      start=False, stop=(c2 == 1)), reads=[bH, bZT], writes=[bpy], sig=(c2 == 1))
                        ph, bph = psr.next()
                        P.op("pe", lambda e: e.matmul(ph[R, 0:64], lhsT=PT[R, c2, :], rhs=H[R, :], start=True, stop=True),
                             reads=[bPT, bH], writes=[bph])
                        Hn, bHn = Hr[hd].next()
                        P.op("dve", lambda e: e.tensor_tensor(out=Hn[R, :], in0=ph[R, 0:64], in1=Q[R, c2, :], op=ADD),
                             reads=[bph, bQ], writes=[bHn])
                        Hcur[hd] = (Hn, bHn)
                        yield
                    P.op("act", lambda e: e.copy(out=y[R, Cc], in_=py[R, 0:128]), reads=[bpy], writes=[by])
                    yield

            gens = [head_stream(hh_) for hh_ in range(2 if DBG != 1 else 0)]
            bg = NEXT[0]
            while gens:
                for g_ in list(gens):
                    try:
                        next(g_)
                    except StopIteration:
                        gens.remove(g_)
                if bg is not None and next(bg) == "PREP_DONE":
                    bg = None
            while bg is not None:
                if next(bg) == "PREP_DONE":
                    bg = None
            ps, bps = psr.next()
            P.op("pe", lambda e: e.matmul(ps[:], lhsT=blk, rhs=y[:], start=True, stop=True), reads=[bcs, by], writes=[bps])
            yc, byc = tr.next()
            P.op("dve", lambda e: e.scalar_tensor_tensor(out=yc[:], in0=ps[:], scalar=-1.0 / 64, in1=y[:], op0=MUL, op1=ADD),
                 reads=[bps, by], writes=[byc])
            sq, bsq = tr.next()
            P.op("pool", lambda e: e.tensor_tensor(out=sq[:], in0=yc[:], in1=yc[:], op=MUL), reads=[byc], writes=[bsq])
            ps, bps = psr.next()
            P.op("pe", lambda e: e.matmul(ps[:], lhsT=blk, rhs=sq[:], start=True, stop=True), reads=[bcs, bsq], writes=[bps])
            rs, brs = tr.next()
            P.op("dve", lambda e: e.tensor_scalar(out=rs[:], in0=ps[:], scalar1=1.0 / 64, scalar2=64e-5, op0=MUL, op1=ADD),
                 reads=[bps], writes=[brs])
            P.op("act", lambda e: e.activation(out=rs[:], in_=rs[:], func=AF.Sqrt), reads=[brs], writes=[brs])
            P.op("dve", lambda e: e.reciprocal(out=rs[:], in_=rs[:]), reads=[brs], writes=[brs])
            P.op("pool", lambda e: e.tensor_tensor(out=yc[:], in0=yc[:], in1=rs[:], op=MUL), reads=[byc, brs], writes=[byc])
            P.op("dve", lambda e: e.tensor_scalar(out=yc[:], in0=yc[:], scalar1=pc(pr_, "ln_g"), scalar2=pc(pr_, "ln_b"),
                                                  op0=MUL, op1=ADD), reads=[byc, bprm], writes=[byc])
            P.op("pool", lambda e: e.tensor_tensor(out=yc[:], in0=yc[:], in1=bon[:], op=ADD), reads=[byc, bbon], writes=[byc])
            gt, bgt = gr.next()
            P.dma("act", gt[:], U[r0["B_g"] + pr_ * 128:r0["B_g"] + (pr_ + 1) * 128, tt * 512:(tt + 1) * 512], writes=[bgt])
            P.op("act", lambda e: e.activation(out=gt[:], in_=gt[:], func=AF.Silu), reads=[bgt], writes=[bgt])
            ob, bob = obr.next()
            P.op("dve", lambda e: e.tensor_tensor(out=ob[:], in0=yc[:], in1=gt[:], op=MUL), reads=[byc, bgt], writes=[bob])
            bw_ = Buf("ysw")
            P.dma("sp", YS[CW + pr_ * 128:CW + (pr_ + 1) * 128, tt * 512:(tt + 1) * 512], ob[:], reads=[bob], writes=[bw_])
            YSW.setdefault(tt, []).append(bw_)
            if on_tile is not None and pr_ == 1:
                on_tile(tt, YSW[tt])

        items = [item(tt_, p_) for tt_ in range(NT) for p_ in range(2)]
        while next(items[0]) != "PREP_DONE":
            pass
        for i_, it_ in enumerate(items):
            NEXT[0] = items[i_ + 1] if i_ + 1 < len(items) else None
            for _ in it_:
                pass
        P.barrier()


def stage_conv(P, nc, U, prm_ap, YS, ntok):
    MUL, ADD = ALU.mult, ALU.add
    TW = 2048 if ntok >= 2048 else ntok
    with ExitStack() as st:
        prm, bprm = sb(nc, st, "c_prm", [128, 2, NPRM], F32)
        P.dma("sp", prm[:], prm_ap, writes=[bprm])
        cr = Ring(nc, st, "c_c", [128, TW + 2], F32, 3)
        xr = Ring(nc, st, "c_x", [128, TW + 2], F32, 3)
        br = Ring(nc, st, "c_b", [128, TW], F32, 3)
        gr = Ring(nc, st, "c_g", [128, TW], F32, 3)
        zr = Ring(nc, st, "c_z", [128, TW], F32, 3)
        obr = Ring(nc, st, "c_o", [128, TW], BF16, 3)
        for c in range(2):
            for tt in range(ntok // TW):
                t0 = tt * TW
                ct, bc = cr.next(); xt, bx = xr.next(); bt, bb = br.next(); gt, bg = gr.next()
                rows = lambda n: slice(UROW[n][0] + c * 128, UROW[n][0] + (c + 1) * 128)
                if tt == 0:
                    P.op("pool", lambda e: e.memset(ct[:, 0:2], 0.0), writes=[bc])
                    P.op("pool", lambda e: e.memset(xt[:, 0:2], 0.0), writes=[bx])
                    P.dma("sp", ct[:, 2:], U[rows("A_c"), 0:TW], writes=[bc])
                    P.dma("act", xt[:, 2:], U[rows("A_x"), 0:TW], writes=[bx])
                else:
                    P.dma("sp", ct[:, :], U[rows("A_c"), t0 - 2:t0 + TW], writes=[bc])
                    P.dma("act", xt[:, :], U[rows("A_x"), t0 - 2:t0 + TW], writes=[bx])
                P.dma("sp", bt[:], U[rows("A_b"), t0:t0 + TW], writes=[bb])
                P.dma("act", gt[:], U[rows("A_g"), t0:t0 + TW], writes=[bg])
                P.op("dve", lambda e: e.tensor_tensor(out=ct[:], in0=ct[:], in1=xt[:], op=MUL), reads=[bc, bx], writes=[bc])
                z, bz = zr.next()
                P.op("dve", lambda e: e.tensor_scalar(out=z[:], in0=ct[:, 0:TW], scalar1=prm[:, c, 11:12], scalar2=None, op0=MUL),
                     reads=[bc, bprm], writes=[bz])
                P.op("dve", lambda e: e.scalar_tensor_tensor(out=z[:], in0=ct[:, 1:TW + 1], scalar=prm[:, c, 12:13], in1=z[:], op0=MUL, op1=ADD),
                     reads=[bc, bprm, bz], writes=[bz])
                P.op("dve", lambda e: e.scalar_tensor_tensor(out=z[:], in0=ct[:, 2:TW + 2], scalar=prm[:, c, 13:14], in1=z[:], op0=MUL, op1=ADD),
                     reads=[bc, bprm, bz], writes=[bz])
                P.op("act", lambda e: e.activation(out=gt[:], in_=gt[:], func=AF.Silu), reads=[bg], writes=[bg])
                P.op("pool", lambda e: e.tensor_tensor(out=z[:], in0=z[:], in1=bt[:], op=MUL), reads=[bz, bb], writes=[bz])
                ob, bob = obr.next()
                P.op("pool", lambda e: e.tensor_tensor(out=ob[:], in0=z[:], in1=gt[:], op=MUL), reads=[bz, bg], writes=[bob])
                for o5 in range(0, TW, 512):
                    P.dma("sp", YS[c * 128:(c + 1) * 128, t0 + o5:t0 + o5 + 512], ob[:, o5:o5 + 512], reads=[bob])
        P.barrier()


def host_poolc(hg, ntok):
    win = (2, 4, 8, 16)[hg]
    sel = np.zeros((128, 4), np.float32)
    sel[:, hg] = 1.0
    invc = (1.0 / np.minimum(np.arange(ntok) + 1, win)).astype(np.float32)[None, :]
    return sel, invc


def stage_pool(P, nc, U, prm_ap, sel_ap, invc_ap, pw_ap, YS, ntok):
    MUL, ADD, SUB = ALU.mult, ALU.add, ALU.subtract
    TW = 2048 if ntok >= 2048 else ntok
    H = 16
    with ExitStack() as st:
        prm, bprm = sb(nc, st, "l_prm", [128, 2, NPRM], F32)
        P.dma("sp", prm[:], prm_ap, writes=[bprm])
        sel, bsel = sb(nc, st, "l_sel", [128, 4], F32)
        P.dma("sp", sel[:], sel_ap, writes=[bsel])
        pw, bpw = sb(nc, st, "l_pw", [128, 2, CW], BF16)
        P.dma("pool", pw[:], pw_ap.rearrange("(cc p) e -> p cc e", p=128), writes=[bpw])
        ivr = Ring(nc, st, "l_iv", [128, TW], F32, 1)
        xr = Ring(nc, st, "l_x", [128, TW + H], F32, 3)
        sr = Ring(nc, st, "l_s", [128, TW + H], F32, 4)
        cr = Ring(nc, st, "l_cmb", [128, TW], F32, 2)
        pbr = Ring(nc, st, "l_pb", [128, 2, TW], BF16, 1)
        gr = Ring(nc, st, "l_g", [128, 512], F32, 2)
        obr = Ring(nc, st, "l_o", [128, 512], BF16, 2)
        psr = Ring(nc, st, "l_ps", [128, 512], F32, 2, psum=True)
        for tt in range(ntok // TW):
            t0 = tt * TW
            iv, biv = ivr.next()
            P.dma("sp", iv[:], invc_ap[0:1, t0:t0 + TW].partition_broadcast(128), writes=[biv])
            pb, bpb = pbr.next()
            for c in range(2):
                rows = slice(UROW["D_x"][0] + c * 128, UROW["D_x"][0] + (c + 1) * 128)
                xt, bx = xr.next()
                if tt == 0:
                    P.op("pool", lambda e: e.memset(xt[:, 0:H], 0.0), writes=[bx])
                    P.dma("sp", xt[:, H:], U[rows, 0:TW], writes=[bx])
                else:
                    P.dma("sp", xt[:, :], U[rows, t0 - H:t0 + TW], writes=[bx])
                prev, bprev = xt, bx
                cmb, bcmb = cr.next()
                sh = 1
                for i in range(4):
                    s, bs = sr.next()
                    lo = 2 * sh - 1
                    P.op("pool" if i % 2 else "dve", lambda e: e.tensor_tensor(out=s[:, lo:], in0=prev[:, lo:], in1=prev[:, lo - sh:TW + H - sh], op=ADD),
                         reads=[bprev], writes=[bs])
                    if i == 0:
                        P.op("dve", lambda e: e.tensor_scalar(out=cmb[:], in0=s[:, H:], scalar1=sel[:, 0:1], scalar2=None, op0=MUL),
                             reads=[bs, bsel], writes=[bcmb])
                    else:
                        P.op("dve", lambda e: e.scalar_tensor_tensor(out=cmb[:], in0=s[:, H:], scalar=sel[:, i:i + 1], in1=cmb[:], op0=MUL, op1=ADD),
                             reads=[bs, bsel, bcmb], writes=[bcmb])
                    prev, bprev = s, bs
                    sh *= 2
                P.op("pool", lambda e: e.tensor_tensor(out=cmb[:], in0=cmb[:], in1=iv[:], op=MUL), reads=[bcmb, biv], writes=[bcmb])
                P.op("dve", lambda e: e.tensor_tensor(out=pb[:, c, :], in0=cmb[:], in1=xt[:, H:], op=SUB), reads=[bcmb, bx], writes=[bpb])
            for t5 in range(TW // 512):
                for ec in range(2):
                    ps, bps = psr.next()
                    for cc in range(2):
                        P.op("pe", lambda e: e.matmul(ps[:], lhsT=pw[:, cc, ec * 128:(ec + 1) * 128], rhs=pb[:, cc, t5 * 512:(t5 + 1) * 512],
                                                      start=(cc == 0), stop=(cc == 1)), reads=[bpw, bpb], writes=[bps], sig=(cc == 1))
                    gt, bg = gr.next()
                    grow = slice(UROW["D_g"][0] + ec * 128, UROW["D_g"][0] + (ec + 1) * 128)
                    P.dma("act", gt[:], U[grow, t0 + t5 * 512:t0 + (t5 + 1) * 512], writes=[bg])
                    P.op("act", lambda e: e.activation(out=gt[:], in_=gt[:], func=AF.Silu), reads=[bg], writes=[bg])
                    ob, bob = obr.next()
                    P.op("dve", lambda e: e.scalar_tensor_tensor(out=ob[:], in0=ps[:], scalar=prm[:, ec, 14:15], in1=gt[:], op0=MUL, op1=MUL),
                         reads=[bps, bprm, bg], writes=[bob])
                    P.dma("sp", YS[3 * CW + ec * 128:3 * CW + (ec + 1) * 128, t0 + t5 * 512:t0 + (t5 + 1) * 512], ob[:], reads=[bob])
        P.barrier()


def stage_fox(P, nc, U, bf_ap, cst, YS, ntok, identf, FC):
    MUL, ADD, SUB = ALU.mult, ALU.add, ALU.subtract
    idf, bidf = identf
    cs, bcs = cst
    NQ = ntok // 512
    NK = ntok // 128
    LW = 2048 if ntok >= 2048 else ntok
    with ExitStack() as st:
        SEG = 2048 if ntok >= 2048 else ntok
        with ExitStack() as st0:
            f, bfb = sb(nc, st0, "f_f", [4, SEG], F32)
            one4, bone4 = sb(nc, st0, "f_one", [4, SEG], F32)
            bft, bbft = sb(nc, st0, "f_bf", [4, 2], F32)
            carry, bcar = sb(nc, st0, "f_car", [4, 2], F32)
            parts, bparts = sb(nc, st0, "f_parts", [4, 6, SEG], BF16)
            r1, br1 = sb(nc, st0, "f_r1", [4, SEG], F32)
            P.dma("sp", bft[:, 0:1], bf_ap, writes=[bbft])
            P.op("dve", lambda e: e.tensor_scalar(out=bft[:, 1:2], in0=bft[:, 0:1], scalar1=-1.0, scalar2=None, op0=MUL), reads=[bbft], writes=[bbft])
            P.op("pool", lambda e: e.memset(one4[:], 1.0), writes=[bone4])
            P.op("pool", lambda e: e.memset(carry[:], 0.0), writes=[bcar])
            for s0 in range(0, ntok, SEG):
                P.dma("sp", f[:], U[UROW["C_f"][0]:UROW["C_f"][0] + 4, s0:s0 + SEG], writes=[bfb])
                P.op("act", lambda e: e.activation(out=f[:], in_=f[:], func=AF.Exp, bias=bft[:, 1:2], scale=-1.0), reads=[bfb, bbft], writes=[bfb])
                P.op("dve", lambda e: e.tensor_scalar(out=f[:], in0=f[:], scalar1=1.0, scalar2=None, op0=ADD), reads=[bfb], writes=[bfb])
                P.op("act", lambda e: e.activation(out=f[:], in_=f[:], func=AF.Ln), reads=[bfb], writes=[bfb])
                P.op("dve", lambda e: e.tensor_tensor_scan(out=r1[:], data0=one4[:], data1=f[:], initial=carry[:, 0:1], op0=MUL, op1=ADD),
                     reads=[bone4, bfb, bcar], writes=[br1])
                P.op("dve", lambda e: e.tensor_copy(out=carry[:, 0:1], in_=r1[:, SEG - 1:SEG]), reads=[br1], writes=[bcar])
                P.op("dve", lambda e: e.tensor_scalar(out=r1[:], in0=r1[:], scalar1=8.0, scalar2=None, op0=MUL), reads=[br1], writes=[br1])
                for i in range(3):
                    P.op("dve", lambda e: e.tensor_copy(out=parts[:, i, :], in_=r1[:]), reads=[br1], writes=[bparts])
                    P.op("dve", lambda e: e.tensor_scalar(out=parts[:, 3 + i, :], in0=parts[:, i, :], scalar1=-1.0, scalar2=None, op0=MUL),
                         reads=[bparts], writes=[bparts])
                    if i < 2:
                        P.op("dve", lambda e: e.tensor_tensor(out=r1[:], in0=r1[:], in1=parts[:, i, :], op=SUB), reads=[br1, bparts], writes=[br1])
                P.dma("sp", FC[:, :, s0:s0 + SEG], parts[:], reads=[bparts])
            P.barrier()
        maskb, bmaskb = sb(nc, st, "f_mask", [128, 128], BF16)
        P.op("dve", lambda e: e.tensor_copy(out=maskb[:], in_=cs[:, 1280:1408]), reads=[bcs], writes=[bmaskb])
        onesf, bonesf = sb(nc, st, "f_ones", [128, 64], F32)
        P.op("pool", lambda e: e.memset(onesf[:], 1.0), writes=[bonesf])
        qa, bqa = sb(nc, st, "f_qa", [70, ntok], BF16)
        ka, bka = sb(nc, st, "f_ka", [70, ntok], BF16)
        va, bva = sb(nc, st, "f_va", [128, NK, 65], BF16)
        ldr = Ring(nc, st, "f_ld", [64, LW], F32, 2)
        psr = Ring(nc, st, "f_ps", [128, 512], F32, 6, psum=True)
        por = Ring(nc, st, "f_po", [128, 512], F32, 2, psum=True)
        ptr_ = Ring(nc, st, "f_pt", [128, 512], BF16, 5)
        rdr = Ring(nc, st, "f_rd", [128, 512], F32, 2)
        osr = Ring(nc, st, "f_os", [64, 512], F32, 2)
        gr = Ring(nc, st, "f_g", [64, 512], F32, 2)
        obr = Ring(nc, st, "f_ob", [64, 512], BF16, 2)
        for h in range(4):
            qrow = UROW["C_q"][0] + h * 64
            krow = UROW["C_k"][0] + h * 64
            vrow = UROW["C_v"][0] + h * 64
            P.op("pool", lambda e: e.memset(qa[64:70, :], 1.0), writes=[bqa])
            P.op("pool", lambda e: e.memset(ka[64:70, :], 1.0), writes=[bka])
            P.op("pool", lambda e: e.memset(va[:, :, 64:65], 1.0), writes=[bva])
            for i in range(3):
                P.dma("sp", qa[64 + i:65 + i, :], FC[h:h + 1, 3 + i, :], writes=[bqa])
                P.dma("sp", ka[67 + i:68 + i, :], FC[h:h + 1, i, :], writes=[bka])
            for l0 in range(0, ntok, LW):
                for (row, dst, bdst, eng) in ((qrow, qa, bqa, "act"), (krow, ka, bka, "dve")):
                    ld, bld = ldr.next()
                    P.dma("sp", ld[:], U[row:row + 64, l0:l0 + LW], writes=[bld])
                    if eng == "act":
                        P.op("act", lambda e: e.copy(out=dst[0:64, l0:l0 + LW], in_=ld[:]), reads=[bld], writes=[bdst])
                    else:
                        P.op("dve", lambda e: e.tensor_copy(out=dst[0:64, l0:l0 + LW], in_=ld[:]), reads=[bld], writes=[bdst])
                ld, bld = ldr.next()
                P.dma("act", ld[:], U[vrow:vrow + 64, l0:l0 + LW], writes=[bld])
                for j8 in range(LW // 1024):
                    ps, bps = psr.next()
                    for j in range(8):
                        P.op("pe", lambda e: e.transpose(out=ps[:, j * 64:(j + 1) * 64], in_=ld[:, j8 * 1024 + j * 128:j8 * 1024 + (j + 1) * 128],
                                                         identity=idf[0:64, 0:64]), reads=[bld, bidf], writes=[bps], sig=(j == 7))
                    jb = l0 // 128 + j8 * 8
                    P.op("dve", lambda e: e.tensor_copy(out=va[:, jb:jb + 8, 0:64], in_=ps[:, :].rearrange("p (j d) -> p j d", d=64)),
                         reads=[bps], writes=[bva])
            carry = {}
            for qc in range(NQ):
                nkt = 4 * (qc + 1)
                po, bpo = por.next()
                LA = 4

                def emit_st(j, qc=qc):
                    d = j - 4 * qc
                    c0 = max(0, d) * 128
                    ps, bps = psr.next()
                    P.op("pe", lambda e: e.matmul(ps[:, c0:512], lhsT=ka[0:70, j * 128:(j + 1) * 128], rhs=qa[0:70, qc * 512 + c0:(qc + 1) * 512],
                                                  start=True, stop=True), reads=[bka, bqa], writes=[bps])
                    return ps, bps, c0, d

                pend = carry
                if not pend:
                    for j in range(min(LA, nkt)):
                        pend[j] = emit_st(j)
                for j in range(nkt):
                    if j + LA < nkt:
                        pend[j + LA] = emit_st(j + LA)
                    ps, bps, c0, d = pend.pop(j)
                    pt, bpt = ptr_.next()
                    P.op("act", lambda e: e.activation(out=pt[:, c0:512], in_=ps[:, c0:512], func=AF.Exp, scale=0.125), reads=[bps], writes=[bpt])
                    if d >= 0:
                        P.op("pool", lambda e: e.tensor_tensor(out=pt[:, c0:c0 + 128], in0=pt[:, c0:c0 + 128], in1=maskb[:], op=MUL),
                             reads=[bpt, bmaskb], writes=[bpt])
                    P.op("pe", lambda e: e.matmul(po[0:65, c0:512], lhsT=va[:, j, 0:65], rhs=pt[:, c0:512], start=(j == 0), stop=(j == nkt - 1)),
                         reads=[bva, bpt], writes=[bpo], sig=(j == nkt - 1))
                carry = {}
                if qc + 1 < NQ:
                    for j in range(LA):
                        carry[j] = emit_st(j, qc + 1)
                rd, brd = rdr.next()
                P.op("dve", lambda e: e.reciprocal(out=rd[64:65, :], in_=po[64:65, :]), reads=[bpo], writes=[brd])
                pb, bpb = psr.next()
                P.op("pe", lambda e: e.matmul(pb[0:64, :], lhsT=onesf[64:65, 0:64], rhs=rd[64:65, :], start=True, stop=True),
                     reads=[bonesf, brd], writes=[bpb])
                os_, bos = osr.next()
                P.op("act", lambda e: e.copy(out=os_[:], in_=po[0:64, :]), reads=[bpo], writes=[bos])
                gt, bg = gr.next()
                grow = UROW["C_g"][0] + h * 64
                P.dma("sp", gt[:], U[grow:grow + 64, qc * 512:(qc + 1) * 512], writes=[bg])
                P.op("act", lambda e: e.activation(out=gt[:], in_=gt[:], func=AF.Silu), reads=[bg], writes=[bg])
                P.op("dve", lambda e: e.tensor_tensor(out=os_[:], in0=os_[:], in1=pb[0:64, :], op=MUL), reads=[bos, bpb], writes=[bos])
                ob, bob = obr.next()
                P.op("pool", lambda e: e.tensor_tensor(out=ob[:], in0=os_[:], in1=gt[:], op=MUL), reads=[bos, bg], writes=[bob])
                P.dma("sp", YS[2 * CW + h * 64:2 * CW + (h + 1) * 64, qc * 512:(qc + 1) * 512], ob[:], reads=[bob])
        P.barrier()


def stage_back(P, nc, x_ap, HT, YSg, wm_ap, wb_ap, wo_ap, bm_ap, xo_ap, ntok, fg_ap=None):
    MUL, ADD = ALU.mult, ALU.add
    with ExitStack() as st:
        bm, bbm = sb(nc, st, "b_bm", [128, 4, KC], F32)
        P.dma("sp", bm[:], bm_ap, writes=[bbm])
        wo, bwo = sb(nc, st, "b_wo", [128, KC, D], BF16)
        wov = wo_ap.rearrange("(kc p) e -> p kc e", p=128)
        for kc in range(0, KC, 2):
            P.dma("pool", wo[:, kc:kc + 2, :], wov[:, kc:kc + 2, :], writes=[bwo])
        if fg_ap is not None:
            fg, bfg = sb(nc, st, "b_fg", [128, D], F32)
            P.dma("sp", fg[:], fg_ap[0:1, :].partition_broadcast(128), writes=[bfg])
        hr = Ring(nc, st, "b_h", [128, KC, 512], BF16, 1)
        yr = Ring(nc, st, "b_y", [128, 32, 512], BF16, 1)
        mr = Ring(nc, st, "b_m", [128, KC, 512], BF16, 1)
        wmr = Ring(nc, st, "b_wm", [128, KC, 128], BF16, 3)
        wbr = Ring(nc, st, "b_wb", [128, 8, 128], BF16, 3)
        gr = Ring(nc, st, "b_g", [128, 512], F32, 2)
        ar = Ring(nc, st, "b_a", [128, 512], F32, 2)
        tr = Ring(nc, st, "b_t", [128, 512], F32, 2)
        xr = Ring(nc, st, "b_x", [128, D], F32, 2)
        sr = Ring(nc, st, "b_s", [128, 4], F32, 2)
        jr = Ring(nc, st, "b_j", [128, D], BF16, 1)
        psm = Ring(nc, st, "b_psm", [128, 512], F32, 2, psum=True)
        psp = Ring(nc, st, "b_psp", [128, 512], F32, 2, psum=True)
        pso = Ring(nc, st, "b_pso", [128, 512], F32, 2, psum=True)
        wmv = wm_ap.rearrange("(kc p) c -> p kc c", p=128)
        ysv = YSg.rearrange("(j p) t -> p j t", p=128)
        for tt in range(ntok // 512):
            ht, bh = hr.next()
            P.dma("sp", ht[:], HT[:, :, tt * 512:(tt + 1) * 512], writes=[bh])
            ys, bys = yr.next()
            for j0 in range(0, 32, 8):
                P.dma("act", ys[:, j0:j0 + 8, :], ysv[:, j0:j0 + 8, tt * 512:(tt + 1) * 512], writes=[bys])
            mg, bmg = mr.next()
            for dc in range(KC):
                acc, bacc = ar.next()
                for k in range(4):
                    wmt, bwm = wmr.next()
                    c0 = k * D + dc * 128
                    P.dma("pool", wmt[:], wmv[:, :, c0:c0 + 128], writes=[bwm])
                    wbt, bwb = wbr.next()
                    P.dma("pool", wbt[:], wb_ap[k, :, dc * 128:(dc + 1) * 128].rearrange("(cc p) d -> p cc d", p=128), writes=[bwb])
                    pm, bpm = psm.next()
                    for kc in range(KC):
                        P.op("pe", lambda e: e.matmul(pm[:], lhsT=wmt[:, kc, :], rhs=ht[:, kc, :], start=(kc == 0), stop=(kc == KC - 1)),
                             reads=[bwm, bh], writes=[bpm])
                    gt, bg = gr.next()
                    P.op("act", lambda e: e.activation(out=gt[:], in_=pm[:], func=AF.Sigmoid, bias=bm[:, k, dc:dc + 1], scale=1.0),
                         reads=[bpm, bbm], writes=[bg])
                    pp, bpp = psp.next()
                    for cc in range(8):
                        P.op("pe", lambda e: e.matmul(pp[:], lhsT=wbt[:, cc, :], rhs=ys[:, k * 8 + cc, :], start=(cc == 0), stop=(cc == 7)),
                             reads=[bwb, bys], writes=[bpp])
                    if k == 0:
                        P.op("dve", lambda e: e.tensor_tensor(out=acc[:], in0=pp[:], in1=gt[:], op=MUL), reads=[bpp, bg], writes=[bacc])
                    else:
                        t_, bt_ = tr.next()
                        P.op("dve", lambda e: e.tensor_tensor(out=t_[:], in0=pp[:], in1=gt[:], op=MUL), reads=[bpp, bg], writes=[bt_])
                        if k < 3:
                            P.op("pool", lambda e: e.tensor_tensor(out=acc[:], in0=acc[:], in1=t_[:], op=ADD), reads=[bacc, bt_], writes=[bacc])
                        else:
                            P.op("pool", lambda e: e.tensor_tensor(out=mg[:, dc, :], in0=acc[:], in1=t_[:], op=ADD), reads=[bacc, bt_], writes=[bmg])
            for ts in range(4):
                xt, bx = xr.next()
                r0_ = tt * 512 + ts * 128
                P.dma("sp", xt[:], x_ap[r0_:r0_ + 128, :], writes=[bx])
                for ec in range(4):
                    po, bpo = pso.next()
                    for dc in range(KC):
                        P.op("pe", lambda e: e.matmul(po[:], lhsT=mg[:, dc, ts * 128:(ts + 1) * 128], rhs=wo[:, dc, ec * 512:(ec + 1) * 512],
                                                      start=(dc == 0), stop=(dc == KC - 1)), reads=[bmg, bwo], writes=[bpo], sig=(dc == KC - 1))
                    P.op("dve", lambda e: e.tensor_tensor(out=xt[:, ec * 512:(ec + 1) * 512], in0=po[:], in1=xt[:, ec * 512:(ec + 1) * 512], op=ADD),
                         reads=[bpo, bx], writes=[bx])
                if fg_ap is not None:
                    s, bs = sr.next()
                    j, bj = jr.next()
                    P.op("act", lambda e: e.activation(out=j[:], in_=xt[:], func=AF.Square, accum_out=s[:, 0:1]), reads=[bx], writes=[bj, bs])
                    P.op("dve", lambda e: e.tensor_scalar(out=s[:, 1:2], in0=s[:, 0:1], scalar1=1.0 / D, scalar2=EPS, op0=MUL, op1=ADD),
                         reads=[bs], writes=[bs])
                    P.op("act", lambda e: e.activation(out=s[:, 1:2], in_=s[:, 1:2], func=AF.Sqrt), reads=[bs], writes=[bs])
                    P.op("dve", lambda e: e.reciprocal(out=s[:, 2:3], in_=s[:, 1:2]), reads=[bs], writes=[bs])
                    P.op("dve", lambda e: e.scalar_tensor_tensor(out=xt[:], in0=xt[:], scalar=s[:, 2:3], in1=fg[:], op0=MUL, op1=MUL),
                         reads=[bx, bs, bfg], writes=[bx])
                P.dma("sp", xo_ap[r0_:r0_ + 128, :], xt[:], reads=[bx])
        P.barrier()


DQ = D // HG


def stage_back_a(P, nc, HT, YSall, wm_ap, wb_ap, bm_ap, MT, ntok, on_chunk=None):
    MUL, ADD = ALU.mult, ALU.add
    with ExitStack() as st:
        bm, bbm = sb(nc, st, "ba_bm", [128, 4, 4], F32)
        P.dma("sp", bm[:], bm_ap, writes=[bbm])
        wm, bwm = sb(nc, st, "ba_wm", [128, KC, 4 * DQ], BF16)
        wmv = wm_ap.rearrange("(kc p) c -> p kc c", p=128)
        for kc in range(0, KC, 2):
            P.dma("pool", wm[:, kc:kc + 2, :], wmv[:, kc:kc + 2, :], writes=[bwm])
        wb, bwb = sb(nc, st, "ba_wb", [128, 4, 8, DQ], BF16)
        for k in range(4):
            P.dma("pool", wb[:, k, :, :], wb_ap[k, :, :].rearrange("(cc p) d -> p cc d", p=128), writes=[bwb])
        hr = Ring(nc, st, "ba_h", [128, KC, 512], BF16, 2)
        yr = Ring(nc, st, "ba_y", [128, 32, 512], BF16, 1)
        gr = Ring(nc, st, "ba_g", [128, 512], F32, 2)
        ar = Ring(nc, st, "ba_a", [128, 512], F32, 2)
        tr = Ring(nc, st, "ba_t", [128, 512], F32, 2)
        mr = Ring(nc, st, "ba_m", [128, 512], BF16, 2)
        psm = Ring(nc, st, "ba_psm", [128, 512], F32, 4, psum=True)
        psp = Ring(nc, st, "ba_psp", [128, 512], F32, 4, psum=True)
        mtw = []
        bysk = [Buf("ysk%d" % k_) for k_ in range(4)]
        for tt in range(ntok // 512):
            ht, bh = hr.next()
            P.dma("sp", ht[:], HT[:, :, tt * 512:(tt + 1) * 512], writes=[bh])
            ys, _ = yr.next()
            for k in range(4):
                for hg in range(HG):
                    row = hg * 4 * CW + k * CW
                    P.dma("act" if (k + hg) % 2 else "sp", ys[:, k * 8 + hg * 2:k * 8 + hg * 2 + 2, :],
                          YSall[row:row + CW, tt * 512:(tt + 1) * 512].rearrange("(j p) t -> p j t", p=128), writes=[bysk[k]])
            for dcl in range(4):
                acc, bacc = ar.next()
                for k in range(4):
                    pm, bpm = psm.next()
                    for kc in range(KC):
                        P.op("pe", lambda e: e.matmul(pm[:], lhsT=wm[:, kc, k * DQ + dcl * 128:k * DQ + (dcl + 1) * 128], rhs=ht[:, kc, :],
                                                      start=(kc == 0), stop=(kc == KC - 1)), reads=[bwm, bh], writes=[bpm], sig=(kc == KC - 1))
                    gt, bg = gr.next()
                    P.op("act", lambda e: e.activation(out=gt[:], in_=pm[:], func=AF.Sigmoid, bias=bm[:, k, dcl:dcl + 1], scale=1.0),
                         reads=[bpm, bbm], writes=[bg])
                    pp, bpp = psp.next()
                    for cc in range(8):
                        P.op("pe", lambda e: e.matmul(pp[:], lhsT=wb[:, k, cc, dcl * 128:(dcl + 1) * 128], rhs=ys[:, k * 8 + cc, :],
                                                      start=(cc == 0), stop=(cc == 7)), reads=[bwb, bysk[k]], writes=[bpp], sig=(cc == 7))
                    if k == 0:
                        P.op("dve", lambda e: e.tensor_tensor(out=acc[:], in0=pp[:], in1=gt[:], op=MUL), reads=[bpp, bg], writes=[bacc])
                    else:
                        t_, bt_ = tr.next()
                        P.op("dve", lambda e: e.tensor_tensor(out=t_[:], in0=pp[:], in1=gt[:], op=MUL), reads=[bpp, bg], writes=[bt_])
                        if k < 3:
                            P.op("pool", lambda e: e.tensor_tensor(out=acc[:], in0=acc[:], in1=t_[:], op=ADD), reads=[bacc, bt_], writes=[bacc])
                        else:
                            mg, bmg = mr.next()
                            P.op("pool", lambda e: e.tensor_tensor(out=mg[:], in0=acc[:], in1=t_[:], op=ADD), reads=[bacc, bt_], writes=[bmg])
                            bw_ = Buf("mtw")
                            P.dma("sp", MT[dcl * 128:(dcl + 1) * 128, tt * 512:(tt + 1) * 512], mg[:], reads=[bmg], writes=[bw_])
                            mtw.append(bw_)
            if on_chunk is not None and tt % 2 == 1:
                on_chunk(tt // 2, mtw)
                mtw = []
        P.barrier()


def stage_back_b(P, nc, MTall, wo_ap, Xcol, ntok, on_tile=None):
    with ExitStack() as st:
        wo, bwo = sb(nc, st, "bb_wo", [128, KC, DQ], BF16)
        P.dma("pool", wo[:], wo_ap.rearrange("(kc p) e -> p kc e", p=128), writes=[bwo])
        mr = Ring(nc, st, "bb_m", [128, KC, 512], BF16, 2)
        xr = Ring(nc, st, "bb_x", [128, DQ], F32, 3)
        pso = Ring(nc, st, "bb_ps", [128, 512], F32, 3, psum=True)
        for tt in range(ntok // 512):
            mg, bmg = mr.next()
            P.dma("sp", mg[:], MTall[0:D, tt * 512:(tt + 1) * 512].rearrange("(dc p) t -> p dc t", p=128), writes=[bmg])
            xw = []
            for ts in range(4):
                r0_ = tt * 512 + ts * 128
                xt, bx = xr.next()
                P.dma("act", xt[:], Xcol[r0_:r0_ + 128, :], writes=[bx])
                po, bpo = pso.next()
                for dc in range(KC):
                    P.op("pe", lambda e: e.matmul(po[:], lhsT=mg[:, dc, ts * 128:(ts + 1) * 128], rhs=wo[:, dc, :],
                                                  start=(dc == 0), stop=(dc == KC - 1)), reads=[bmg, bwo], writes=[bpo], sig=(dc == KC - 1))
                P.op("dve", lambda e: e.tensor_tensor(out=xt[:], in0=po[:], in1=xt[:], op=ALU.add), reads=[bpo, bx], writes=[bx])
                bw_ = Buf("xw")
                P.dma("sp", Xcol[r0_:r0_ + 128, :], xt[:], reads=[bx], writes=[bw_])
                xw.append(bw_)
            if on_tile is not None:
                on_tile(tt, xw)
        P.barrier()


def stage_final(P, nc, xg4, Xcol, fg_ap, out_ap, ntok):
    MUL, ADD = ALU.mult, ALU.add
    with ExitStack() as st:
        fg, bfg = sb(nc, st, "fn_fg", [128, DQ], F32)
        P.dma("sp", fg[:], fg_ap[0:1, :].partition_broadcast(128), writes=[bfg])
        xr = Ring(nc, st, "fn_x", [128, D], F32, 2)
        cr = Ring(nc, st, "fn_c", [128, DQ], F32, 2)
        jr = Ring(nc, st, "fn_j", [128, D], BF16, 1)
        sr = Ring(nc, st, "fn_s", [128, 4], F32, 2)
        for tt in range(ntok // 128):
            xt, bx = xr.next()
            P.dma("sp", xt[:, :].rearrange("p (r e) -> p r e", r=4), xg4[tt * 128:(tt + 1) * 128, :, :], writes=[bx])
            ct, bc = cr.next()
            P.dma("act", ct[:], Xcol[tt * 128:(tt + 1) * 128, :], writes=[bc])
            s, bs = sr.next()
            j, bj = jr.next()
            P.op("act", lambda e: e.activation(out=j[:], in_=xt[:], func=AF.Square, accum_out=s[:, 0:1]), reads=[bx], writes=[bj, bs])
            P.op("dve", lambda e: e.tensor_scalar(out=s[:, 1:2], in0=s[:, 0:1], scalar1=1.0 / D, scalar2=EPS, op0=MUL, op1=ADD),
                 reads=[bs], writes=[bs])
            P.op("act", lambda e: e.activation(out=s[:, 1:2], in_=s[:, 1:2], func=AF.Sqrt), reads=[bs], writes=[bs])
            P.op("dve", lambda e: e.reciprocal(out=s[:, 2:3], in_=s[:, 1:2]), reads=[bs], writes=[bs])
            P.op("dve", lambda e: e.scalar_tensor_tensor(out=ct[:], in0=ct[:], scalar=s[:, 2:3], in1=fg[:], op0=MUL, op1=MUL),
                 reads=[bc, bs, bfg], writes=[bc])
            P.dma("sp", out_ap[tt * 128:(tt + 1) * 128, :], ct[:], reads=[bc])
        P.barrier()


GROUPS = [[0, 1, 2, 3], [4, 5, 6, 7]]


class ColChunks:
    def __init__(self, aps, ch):
        self.aps, self.ch = aps, ch

    def __getitem__(self, idx):
        rsl, csl = idx
        j = csl.start // self.ch
        assert (csl.stop - 1) // self.ch == j
        return self.aps[j][rsl, csl.start - j * self.ch:csl.stop - j * self.ch]


class RowChunks:
    def __init__(self, aps, ch):
        self.aps, self.ch = aps, ch

    def __getitem__(self, idx):
        rsl = idx[0]
        j = rsl.start // self.ch
        assert (rsl.stop - 1) // self.ch == j
        return self.aps[j][(slice(rsl.start - j * self.ch, rsl.stop - j * self.ch),) + tuple(idx[1:])]


def build_fused(depth=DEPTH, S=S):
    nc = bass.Bass("TRN2", target_bir_lowering=False)
    di = lambda n, s, d: nc.dram_tensor(n, s, d, kind="ExternalInput").ap()
    xcol_in = di("xcol", [S, DQ], F32)
    xfull = di("xfull", [S, HG, DQ], F32)
    wc = di("wc", [depth, D, NU], F32)
    g = di("g", [depth, 128, KC], F32)
    prm = di("prm", [depth, 128, 2, NPRM], F32)
    lora = di("lora", [depth, 128, CW], F32)
    pw = di("pw", [depth, CW, CW], F32)
    bf = di("bf", [depth, 4, 1], F32)
    wm = di("wm", [depth, D, 4 * DQ], F32)
    wb = di("wb", [depth, 4, W, DQ], F32)
    wo = di("wo", [depth, D, DQ], F32)
    bm = di("bm", [depth, 128, 4, 4], F32)
    fg = di("fg", [1, DQ], F32)
    cst = di("cst", [128, NCONST], F32)
    sel = di("sel", [128, 4], F32)
    invc = di("invc", [1, S], F32)
    out = nc.dram_tensor("out", [S, DQ], F32, kind="ExternalOutput").ap()
    NXC = S // 512
    Xcol_t = [nc.dram_tensor("Xcol%d" % j, [512, DQ], F32) for j in range(NXC)]
    XG_t = [nc.dram_tensor("XG%d" % j, [HG * 512, DQ], F32) for j in range(NXC)]
    YS_t = [nc.dram_tensor("YS%d" % j, [4 * CW, 512], BF16) for j in range(NXC)]
    YSall_t = [nc.dram_tensor("YSall%d" % j, [HG * 4 * CW, 512], BF16) for j in range(NXC)]
    NMC = S // 1024
    MT_t = [nc.dram_tensor("MT%d" % j, [DQ, 1024], BF16) for j in range(NMC)]
    MTall_t = [nc.dram_tensor("MTall%d" % j, [D, 1024], BF16) for j in range(NMC)]
    HT = nc.dram_tensor("HT", [128, KC, S], BF16).ap()
    U = nc.dram_tensor("U", [NU, S], F32).ap()
    FC = nc.dram_tensor("FC", [4, 6, S], BF16).ap()
    Xcol = RowChunks([t.ap() for t in Xcol_t], 512)
    xg4 = RowChunks([t.ap().rearrange("(r t) e -> t r e", r=HG) for t in XG_t], 512)
    YS = ColChunks([t.ap() for t in YS_t], 512)
    YSall = ColChunks([t.ap() for t in YSall_t], 512)
    MT = ColChunks([t.ap() for t in MT_t], 1024)
    MTall = ColChunks([t.ap() for t in MTall_t], 1024)
    xpairs = list(zip(Xcol_t, XG_t))
    ypairs = list(zip(YS_t, YSall_t))
    mpairs = list(zip(MT_t, MTall_t))
    with ExitStack() as st:
        import os
        skip = os.environ.get("FUSE_SKIP", "")
        P = Prog(nc, st)
        if "i" not in skip:
            idf, idb = make_identity(P, nc, st)
            cs, bcs = sb(nc, st, "cst_sb", [128, NCONST], F32)
            P.dma("sp", cs[:], cst, writes=[bcs])
        for i in range(NXC):
            P.dma("sp" if i % 2 else "act", Xcol[i * 512:(i + 1) * 512, :], xcol_in[i * 512:(i + 1) * 512, :])
        import os
        dbg = os.environ.get("FUSE_DBG", "npcqfrab")
        cb_y = lambda j, deps: P.collective_async("AllGather", YS_t[j], YSall_t[j], GROUPS, deps)
        cb_m = lambda j, deps: P.collective_async("AllGather", MT_t[j], MTall_t[j], GROUPS, deps)
        cb_x = lambda j, deps: P.collective_async("AllGather", Xcol_t[j], XG_t[j], GROUPS, deps)
        for l in range(depth if "L" not in skip else 0):
            if l > 0:
                P.collective_wait()
            if "n" in dbg: stage_norm_T(P, nc, xfull if l == 0 else xg4, g[l], HT, S, idb, gathered=True)
            if "p" in dbg: stage_proj(P, nc, wc[l], NU, HT, S, U)
            if "c" in dbg: stage_conv(P, nc, U, prm[l], YS, S)
            if "q" in dbg: stage_pool(P, nc, U, prm[l], sel, invc, pw[l], YS, S)
            if "f" in dbg: stage_fox(P, nc, U, bf[l], (cs, bcs), YS, S, idf, FC)
            stage_rwkv(P, nc, U, prm[l], lora[l], (cs, bcs), YS, S, idf, on_tile=cb_y)
            P.collective_wait()
            stage_back_a(P, nc, HT, YSall, wm[l], wb[l], bm[l], MT, S, on_chunk=cb_m)
            P.collective_wait()
            stage_back_b(P, nc, MTall, wo[l], Xcol, S, on_tile=cb_x)
            if "e" in dbg: P.new_epoch()
        if "g" not in skip:
            if depth > 0:
                P.collective_wait()
            else:
                P.collectives("AllGather", xpairs, GROUPS)
        if "f" not in skip:
            stage_final(P, nc, xg4, Xcol, fg, out, S)
        P.barrier()
        print("fused nins", P.nins)
    return nc


_CACHE = {}


def kernel(**inp):
    inp = {k: np.asarray(v) for k, v in inp.items()}
    x = np.ascontiguousarray(inp["x"], dtype=np.float32)
    S = x.shape[1]
    if "fused" not in _CACHE:
        _CACHE["fused"] = build_fused(DEPTH, S)
    cst = host_consts()
    cores = list(range(8))
    L = DEPTH
    maps = []
    for c in cores:
        b, q = c // HG, c % HG
        es = slice(q * DQ, (q + 1) * DQ)
        sel, invc = host_poolc(q, S)
        cols = core_cols(q)
        m = {
            "xcol": np.ascontiguousarray(x[b][:, es]),
            "xfull": x[b].reshape(S, HG, DQ),
            "wc": np.ascontiguousarray(inp["w_in"][:, :, cols]),
            "g": np.ascontiguousarray(inp["norm_g"].reshape(L, KC, 128).transpose(0, 2, 1)),
            "prm": np.stack([host_prm(inp, l, q) for l in range(L)]),
            "lora": np.stack([host_lora(inp, l, q) for l in range(L)]),
            "pw": np.ascontiguousarray(inp["pool_w"][:, q]),
            "bf": np.ascontiguousarray(inp["fox_bf"][:, q * 4:(q + 1) * 4].reshape(L, 4, 1)),
            "wm": np.ascontiguousarray(np.concatenate(
                [inp["w_in"][:, :, OM + k * D + q * DQ:OM + k * D + (q + 1) * DQ] for k in range(4)], axis=2)),
            "wb": np.ascontiguousarray(inp["w_branch"][:, :, :, es]),
            "wo": np.ascontiguousarray(inp["w_out"][:, :, es]),
            "bm": np.ascontiguousarray(inp["b_merge"][:, :, es].reshape(L, 4, 4, 128).transpose(0, 3, 1, 2)),
            "fg": np.ascontiguousarray(inp["final_g"][es].reshape(1, DQ)),
            "cst": cst, "sel": sel, "invc": invc,
        }
        maps.append(m)
    res = run_bass_kernel_spmd(_CACHE["fused"], maps, core_ids=cores)
    out = np.empty_like(x)
    for c in cores:
        b, q = c // HG, c % HG
        out[b][:, q * DQ:(q + 1) * DQ] = np.asarray(res.results[c]["out"])
    return out
```
